# Optimizing a Trainium2 kernel written in Bass

```python
import jax, jax.numpy as jnp
from jax import lax
import numpy as np

D_MODEL = 1024
BATCH = 32
SEQ = 2048
DEPTH = 2

GRID_W = 64
CTX_LEN = 256

RWKV_WIDTH = D_MODEL // 2
RWKV_HEAD_DIM = 64
RWKV_HEADS = RWKV_WIDTH // RWKV_HEAD_DIM
RET_WIDTH = D_MODEL - RWKV_WIDTH
RET_HEAD_DIM = 64
RET_HEADS = RET_WIDTH // RET_HEAD_DIM
MIX_WIDTH = RWKV_WIDTH + RET_WIDTH
DECAY_LORA = 64
ICLR_LORA = 64
VRES_LORA = 32
GATE_LORA = 160
RWKV_COLS = 3 * RWKV_WIDTH + DECAY_LORA + ICLR_LORA + GATE_LORA
RWKV_SPLITS = (RWKV_WIDTH, 2 * RWKV_WIDTH, 3 * RWKV_WIDTH, 3 * RWKV_WIDTH + DECAY_LORA, 3 * RWKV_WIDTH + DECAY_LORA + ICLR_LORA)
N_IN = RWKV_COLS + 5 * RET_WIDTH
FFN_HIDDEN = ((8 * D_MODEL) // 3 + 255) // 256 * 256
RET_CHUNK = 128
ROPE_BASE = 10000.0
NORM_EPS = 1e-6
RWKV_GN_EPS = 64e-5
RET_GN_EPS = 1e-5

kernel_name = 'hymba_rwkv7_retnet_prefix_dit'


def rms_norm(x, g):
    x32 = x.astype(jnp.float32)
    y = x32 * lax.rsqrt(jnp.mean(x32 * x32, axis=-1, keepdims=True) + NORM_EPS)
    return y.astype(x.dtype) * g


def modulate(h, shift, scale):
    return h * (1.0 + scale) + shift


def split_heads(t, n_heads):
    return t.reshape(t.shape[:-1] + (n_heads, t.shape[-1] // n_heads))


def group_norm_heads(y, w, b, eps):
    y32 = y.astype(jnp.float32)
    mu = jnp.mean(y32, axis=-1, keepdims=True)
    var = jnp.mean(jnp.square(y32 - mu), axis=-1, keepdims=True)
    yn = ((y32 - mu) * lax.rsqrt(var + eps)).reshape(y.shape[:-2] + (-1,))
    return yn * w + b


def centred_shift_1d(u):
    h = u.shape[-1] // 2
    prev = jnp.pad(u[:, :-1, :h], ((0, 0), (1, 0), (0, 0)))
    nxt = jnp.pad(u[:, 1:, h:], ((0, 0), (0, 1), (0, 0)))
    return jnp.concatenate([prev, nxt], axis=-1)


def quad_shift_2d(u, rows):
    b, l, ch = u.shape
    q = ch // 4
    g = u.reshape(b, rows, GRID_W, ch)
    from_left = jnp.pad(g[:, :, :-1, :q], ((0, 0), (0, 0), (1, 0), (0, 0)))
    from_right = jnp.pad(g[:, :, 1:, q:2 * q], ((0, 0), (0, 0), (0, 1), (0, 0)))
    from_up = jnp.pad(g[:, :-1, :, 2 * q:3 * q], ((0, 0), (1, 0), (0, 0), (0, 0)))
    from_down = jnp.pad(g[:, 1:, :, 3 * q:], ((0, 0), (0, 1), (0, 0), (0, 0)))
    return jnp.concatenate([from_left, from_right, from_up, from_down], axis=-1).reshape(b, l, ch)


def lerp(u, shifted, mu):
    return u + (shifted - u) * mu


def latent_rope(rows):
    row = jnp.repeat(jnp.arange(rows, dtype=jnp.float32), GRID_W)
    col = jnp.tile(jnp.arange(GRID_W, dtype=jnp.float32), rows)
    nf = RET_HEAD_DIM // 4
    freqs = ROPE_BASE ** (-jnp.arange(nf, dtype=jnp.float32) / nf)
    ang = jnp.concatenate([row[:, None] * freqs, col[:, None] * freqs], axis=-1)
    return jnp.cos(ang)[None, :, None, :], jnp.sin(ang)[None, :, None, :]


def apply_rope(t, cos, sin):
    half = t.shape[-1] // 2
    t1, t2 = t[..., :half], t[..., half:]
    return jnp.concatenate([t1 * cos - t2 * sin, t1 * sin + t2 * cos], axis=-1)


def retention_log_gammas():
    return jnp.log(1.0 - 2.0 ** (-5.0 - jnp.arange(RET_HEADS, dtype=jnp.float32)))


def rwkv7_scan(r, w, k, v, kk, a, s0, reverse):
    def step(S, inp):
        r_t, w_t, k_t, v_t, kk_t, a_t = inp
        sa = jnp.einsum('bhij,bhj->bhi', S, -kk_t)
        S = S * w_t[:, :, None, :] + sa[..., None] * (kk_t * a_t)[:, :, None, :] + v_t[..., None] * k_t[:, :, None, :]
        return S, jnp.einsum('bhij,bhj->bhi', S, r_t)
    xs = tuple(jnp.moveaxis(t, 1, 0) for t in (r, w, k, v, kk, a))
    s_final, y = lax.scan(step, s0, xs, reverse=reverse)
    return jnp.moveaxis(y, 0, 1), s_final


def rwkv7_mix(r, k, v, wd, ad, gd, s0, w0, w_up, a0, a_up, g_up, k_k, k_a, r_k, ln_w, ln_b):
    out_dtype = r.dtype
    r, k, v, wd, ad, gd = (t.astype(jnp.float32) for t in (r, k, v, wd, ad, gd))
    H = RWKV_HEADS
    rh, kh, vh = split_heads(r, H), split_heads(k, H), split_heads(v, H)
    kk = split_heads(k * k_k, H)
    kk = kk * lax.rsqrt(jnp.maximum(jnp.sum(kk * kk, axis=-1, keepdims=True), 1e-24))
    tw = jnp.tanh(wd)
    outs, finals = [], []
    for d in range(2):
        decay = jnp.exp(-jnp.exp(-jax.nn.softplus(-(w0[d] + tw @ w_up[d])) - 0.5))
        iclr = jax.nn.sigmoid(a0[d] + ad @ a_up[d])
        k_dir = k * (1.0 + (iclr - 1.0) * k_a)
        y, s_fin = rwkv7_scan(rh, split_heads(decay, H), split_heads(k_dir, H), vh, kk,
                              split_heads(iclr, H), s0[d], d == 1)
        outs.append(y)
        finals.append(s_fin)
    y = group_norm_heads(outs[0] + outs[1], ln_w, ln_b, RWKV_GN_EPS)
    bonus = (jnp.sum(rh * kh * r_k, axis=-1, keepdims=True) * vh).reshape(y.shape)
    gate = jax.nn.sigmoid(gd) @ g_up
    return ((y + bonus) * gate).astype(out_dtype), jnp.stack(finals)


def retention_chunkwise(q, k, v, log_gamma, s0):
    b, l, h, n = q.shape
    c = RET_CHUNK
    nc = l // c
    idx = jnp.arange(c, dtype=jnp.float32)
    diff = idx[:, None] - idx[None, :]
    inner_decay = jnp.where(diff >= 0, jnp.exp(log_gamma[:, None, None] * jnp.maximum(diff, 0.0)), 0.0)
    query_decay = jnp.exp(log_gamma[None, :] * (idx[:, None] + 1.0))
    key_decay = jnp.exp(log_gamma[None, :] * (c - 1.0 - idx[:, None]))
    chunk_decay = jnp.exp(log_gamma * c)

    def to_chunks(t):
        return jnp.moveaxis(t.reshape(b, nc, c, h, n), 1, 0)

    def step(state, inp):
        qc, kc, vc = inp
        scores = jnp.einsum('bihn,bjhn->bhij', qc, kc) * inner_decay
        inner = jnp.einsum('bhij,bjhn->bihn', scores, vc)
        cross = jnp.einsum('bihn,bhnm->bihm', qc, state) * query_decay[None, :, :, None]
        state = state * chunk_decay[None, :, None, None] + jnp.einsum('bjhn,jh,bjhm->bhnm', kc, key_decay, vc)
        return state, inner + cross

    s_final, out = lax.scan(step, s0, (to_chunks(q), to_chunks(k), to_chunks(v)))
    return jnp.moveaxis(out, 0, 1).reshape(b, l, h, n), s_final


def retnet_mix(q, k, v, g_f, g_b, cos, sin, s0, gn_w, gn_b):
    out_dtype = q.dtype
    H = RET_HEADS
    qh, kh, vh = (split_heads(t.astype(jnp.float32), H) for t in (q, k, v))
    if cos is not None:
        qh, kh = apply_rope(qh, cos, sin), apply_rope(kh, cos, sin)
    kh = kh * RET_HEAD_DIM ** -0.5
    lg = retention_log_gammas()
    o_f, s_f = retention_chunkwise(qh, kh, vh, lg, s0[0])
    o_b, s_b = retention_chunkwise(jnp.flip(qh, 1), jnp.flip(kh, 1), jnp.flip(vh, 1), jnp.flip(lg, 0), s0[1])
    o_b = jnp.flip(o_b, 1)
    y = (group_norm_heads(o_f, gn_w, gn_b, RET_GN_EPS) * jax.nn.silu(g_f.astype(jnp.float32))
         + group_norm_heads(o_b, gn_w, gn_b, RET_GN_EPS) * jax.nn.silu(g_b.astype(jnp.float32)))
    return y.astype(out_dtype), jnp.stack([s_f, s_b])


def swiglu(h, w_in, w_out):
    gate, up = jnp.split(h @ w_in, 2, axis=-1)
    return (jax.nn.silu(gate) * up) @ w_out


def setup_inputs(seed: int = 0) -> dict:
    key = jax.random.key(seed)
    ks = jax.random.split(key, 32)
    f32 = jnp.float32
    D, W = D_MODEL, RWKV_WIDTH

    def nrm(k, shape, scale):
        return jax.random.normal(k, shape, f32) * scale

    return {
        'x': nrm(ks[0], (BATCH, SEQ, D), 1.0),
        'c': nrm(ks[1], (BATCH, D), 1.0),
        'ctx': nrm(ks[2], (BATCH, CTX_LEN, D), 1.0),
        'c_ctx': nrm(ks[3], (D,), 1.0),
        'w_ada': nrm(ks[4], (DEPTH, D, 6 * D), 0.5 * D ** -0.5),
        'b_ada': nrm(ks[5], (DEPTH, 6 * D), 0.02),
        'norm1': 1.0 + nrm(ks[6], (DEPTH, D), 0.02),
        'norm2': 1.0 + nrm(ks[7], (DEPTH, D), 0.02),
        'norm_f': 1.0 + nrm(ks[8], (D,), 0.02),
        'w_in': nrm(ks[9], (DEPTH, D, N_IN), D ** -0.5),
        'w_vres_down': nrm(ks[10], (DEPTH - 1, D, VRES_LORA), D ** -0.5),
        'mu_rwkv': jax.random.uniform(ks[11], (DEPTH, RWKV_COLS), f32),
        'mu_vres': jax.random.uniform(ks[12], (DEPTH - 1, VRES_LORA), f32),
        'w0': jax.random.uniform(ks[13], (DEPTH, 2, W), f32, minval=-6.0, maxval=-1.0),
        'w_up': nrm(ks[14], (DEPTH, 2, DECAY_LORA, W), 0.1),
        'a0': nrm(ks[15], (DEPTH, 2, W), 0.1),
        'a_up': nrm(ks[16], (DEPTH, 2, ICLR_LORA, W), 0.1),
        'g_up': nrm(ks[17], (DEPTH, GATE_LORA, W), GATE_LORA ** -0.5),
        'k_k': 0.85 + nrm(ks[18], (DEPTH, W), 0.02),
        'k_a': 1.0 + nrm(ks[19], (DEPTH, W), 0.02),
        'r_k': nrm(ks[20], (DEPTH, RWKV_HEADS, RWKV_HEAD_DIM), 0.1),
        'v0': 1.0 + nrm(ks[21], (DEPTH - 1, W), 0.1),
        'v_up': nrm(ks[22], (DEPTH - 1, VRES_LORA, W), 0.1),
        'ln_x_w': 1.0 + nrm(ks[23], (DEPTH, W), 0.02),
        'ln_x_b': nrm(ks[24], (DEPTH, W), 0.02),
        'ret_gn_w': 1.0 + nrm(ks[25], (DEPTH, RET_WIDTH), 0.02),
        'ret_gn_b': nrm(ks[26], (DEPTH, RET_WIDTH), 0.02),
        'w_out': nrm(ks[27], (DEPTH, MIX_WIDTH, D), MIX_WIDTH ** -0.5),
        'w_ffn_in': nrm(ks[28], (DEPTH, D, 2 * FFN_HIDDEN), D ** -0.5),
        'w_ffn_out': nrm(ks[29], (DEPTH, FFN_HIDDEN, D), FFN_HIDDEN ** -0.5),
    }


def reference(x, c, ctx, c_ctx, w_ada, b_ada, norm1, norm2, norm_f, w_in, w_vres_down, mu_rwkv, mu_vres,
              w0, w_up, a0, a_up, g_up, k_k, k_a, r_k, v0, v_up, ln_x_w, ln_x_b, ret_gn_w, ret_gn_b,
              w_out, w_ffn_in, w_ffn_out):
    b = x.shape[0]
    ROWS = x.shape[1] // GRID_W
    cos, sin = latent_rope(ROWS)
    silu_c = jax.nn.silu(c)
    silu_cc = jax.nn.silu(c_ctx)
    zero_rwkv = jnp.zeros((2, b, RWKV_HEADS, RWKV_HEAD_DIM, RWKV_HEAD_DIM), jnp.float32)
    zero_ret = jnp.zeros((2, b, RET_HEADS, RET_HEAD_DIM, RET_HEAD_DIM), jnp.float32)
    v_first_c = None
    v_first_l = None
    for i in range(DEPTH):
        last = i == DEPTH - 1
        sh1_l, sc1_l, g1_l, sh2_l, sc2_l, g2_l = jnp.split((silu_c @ w_ada[i] + b_ada[i])[:, None, :], 6, axis=-1)
        sh1_c, sc1_c, g1_c, sh2_c, sc2_c, g2_c = jnp.split(silu_cc @ w_ada[i] + b_ada[i], 6, axis=-1)

        w_proj = w_in[i] if i == 0 else jnp.concatenate([w_in[i], w_vres_down[i - 1]], axis=1)
        p_c = modulate(rms_norm(ctx, norm1[i]), sh1_c, sc1_c) @ w_proj
        p_l = modulate(rms_norm(x, norm1[i]), sh1_l, sc1_l) @ w_proj

        u_c = p_c[..., :RWKV_COLS]
        u_l = p_l[..., :RWKV_COLS]
        u_c = lerp(u_c, centred_shift_1d(u_c), mu_rwkv[i])
        u_l = lerp(u_l, quad_shift_2d(u_l, ROWS), mu_rwkv[i])
        r_c, k_c, v_c, wd_c, ad_c, gd_c = jnp.split(u_c, RWKV_SPLITS, axis=-1)
        r_l, k_l, v_l, wd_l, ad_l, gd_l = jnp.split(u_l, RWKV_SPLITS, axis=-1)
        if i == 0:
            v_first_c, v_first_l = v_c, v_l
        else:
            vr_c = p_c[..., N_IN:]
            vr_l = p_l[..., N_IN:]
            vr_c = lerp(vr_c, centred_shift_1d(vr_c), mu_vres[i - 1])
            vr_l = lerp(vr_l, quad_shift_2d(vr_l, ROWS), mu_vres[i - 1])
            v_c = v_c + (v_first_c - v_c) * jax.nn.sigmoid(v0[i - 1] + vr_c @ v_up[i - 1])
            v_l = v_l + (v_first_l - v_l) * jax.nn.sigmoid(v0[i - 1] + vr_l @ v_up[i - 1])
        rwkv_p = (w0[i], w_up[i], a0[i], a_up[i], g_up[i], k_k[i], k_a[i], r_k[i], ln_x_w[i], ln_x_b[i])
        o_rwkv_c, st_rwkv = rwkv7_mix(r_c, k_c, v_c, wd_c, ad_c, gd_c, zero_rwkv, *rwkv_p)
        o_rwkv_l, _ = rwkv7_mix(r_l, k_l, v_l, wd_l, ad_l, gd_l, st_rwkv, *rwkv_p)

        q_c, kr_c, vr2_c, gf_c, gb_c = jnp.split(p_c[..., RWKV_COLS:N_IN], 5, axis=-1)
        q_l, kr_l, vr2_l, gf_l, gb_l = jnp.split(p_l[..., RWKV_COLS:N_IN], 5, axis=-1)
        o_ret_c, st_ret = retnet_mix(q_c, kr_c, vr2_c, gf_c, gb_c, None, None, zero_ret, ret_gn_w[i], ret_gn_b[i])
        o_ret_l, _ = retnet_mix(q_l, kr_l, vr2_l, gf_l, gb_l, cos, sin, st_ret, ret_gn_w[i], ret_gn_b[i])

        x = x + g1_l * (jnp.concatenate([o_rwkv_l, o_ret_l], axis=-1) @ w_out[i])
        x = x + g2_l * swiglu(modulate(rms_norm(x, norm2[i]), sh2_l, sc2_l), w_ffn_in[i], w_ffn_out[i])
        if not last:
            ctx = ctx + g1_c * (jnp.concatenate([o_rwkv_c, o_ret_c], axis=-1) @ w_out[i])
            ctx = ctx + g2_c * swiglu(modulate(rms_norm(ctx, norm2[i]), sh2_c, sc2_c), w_ffn_in[i], w_ffn_out[i])
    return rms_norm(x, norm_f)
```

```python
import numpy as np
import ml_dtypes
from contextlib import ExitStack
import concourse.bass as bass
import concourse.mybir as mybir
from concourse.bass_utils import run_bass_kernel_spmd

F32 = mybir.dt.float32
BF16 = mybir.dt.bfloat16
ALU = mybir.AluOpType
AF = mybir.ActivationFunctionType

D = 1024
T = 2304
NCTX = 256
SEQ = 2048
DEPTH = 2
NIN = 4384
HID = 2816
NJ = HID // 128
CDEC = 0.6065306597126334
BLK = [(0, 256), (256, 768), (768, 1280), (1280, 1792), (1792, 2304)]
NPT = 35
SAME = True
NDS = 24

SPL = {}
_o = 0
for _n, _w in [("norm1", 16), ("norm2", 16), ("normf", 8), ("bada", 96), ("mu", 2 * 15 * 7), ("w0", 16), ("a0", 16),
               ("kk", 8), ("ka", 8), ("rk", 8), ("v0", 4), ("lnw", 8), ("lnb", 8), ("rgw", 8), ("rgb", 8)]:
    SPL[_n] = _o
    _o += _w
NSP = _o


def rwkv_tile_cols(l):
    tiles = []
    for i in range(12):
        tiles.append([i * 128 + p for p in range(128)])
    tiles.append([1536 + p for p in range(128)])
    tiles.append([1664 + p for p in range(128)])
    g1 = [1792 + p for p in range(32)]
    if l == 1:
        g1 += [("v", ch) for ch in range(32)]
    tiles.append(g1)
    return tiles


def build_sp(inp):
    sp = np.zeros((128, NSP), np.float32)
    p = np.arange(128)
    for l in range(2):
        for c in range(8):
            sp[:, SPL["norm1"] + l * 8 + c] = inp["norm1"][l, c * 128 + p]
            sp[:, SPL["norm2"] + l * 8 + c] = inp["norm2"][l, c * 128 + p]
        for q in range(48):
            sp[:, SPL["bada"] + l * 48 + q] = inp["b_ada"][l, q * 128 + p]
        tiles = rwkv_tile_cols(l)
        for ti, cols in enumerate(tiles):
            base = SPL["mu"] + (l * 15 + ti) * 7
            for pp, col in enumerate(cols):
                if isinstance(col, tuple):
                    ch = col[1]
                    m = inp["mu_vres"][0, ch]
                    ld, cd = ch // 8, ch // 16
                else:
                    m = inp["mu_rwkv"][l, col]
                    ld, cd = col // 456, col // 912
                sp[pp, base + 0] = m
                sp[pp, base + 1 + ld] = m
                sp[pp, base + 5 + cd] = m
        for d in range(2):
            for hp in range(4):
                sp[:, SPL["w0"] + (l * 2 + d) * 4 + hp] = inp["w0"][l, d, hp * 128 + p]
                sp[:, SPL["a0"] + (l * 2 + d) * 4 + hp] = inp["a0"][l, d, hp * 128 + p]
        for hp in range(4):
            sp[:, SPL["kk"] + l * 4 + hp] = inp["k_k"][l, hp * 128 + p]
            sp[:, SPL["ka"] + l * 4 + hp] = inp["k_a"][l, hp * 128 + p]
            sp[:, SPL["rk"] + l * 4 + hp] = inp["r_k"][l].reshape(512)[hp * 128 + p]
            sp[:, SPL["lnw"] + l * 4 + hp] = inp["ln_x_w"][l, hp * 128 + p]
            sp[:, SPL["lnb"] + l * 4 + hp] = inp["ln_x_b"][l, hp * 128 + p]
            sp[:, SPL["rgw"] + l * 4 + hp] = inp["ret_gn_w"][l, hp * 128 + p]
            sp[:, SPL["rgb"] + l * 4 + hp] = inp["ret_gn_b"][l, hp * 128 + p]
    for c in range(8):
        sp[:, SPL["normf"] + c] = inp["norm_f"][c * 128 + p]
    for hp in range(4):
        sp[:, SPL["v0"] + hp] = inp["v0"][0, hp * 128 + p]
    return sp


def build_consts():
    c = {}
    s = np.arange(128)[:, None]
    t = np.arange(128)[None, :]
    c["ident"] = np.eye(128, dtype=np.float32)
    c["onesr"] = np.full((128, 128), 1.0 / 1024, np.float32)
    bo = np.zeros((128, 128), np.float32)
    bo[:64, :64] = 1
    bo[64:, 64:] = 1
    c["bo"] = bo
    pm = np.zeros((128, 128), np.float32)
    for m in range(128):
        n = m % 64
        pm[(m - n) + ((n + 32) % 64), m] = 1
    c["pm"] = pm
    m2 = np.zeros((2, 128, 256), np.float32)
    m2[0, :, :128] = s < t
    m2[0, :, 128:] = s <= t
    m2[1, :, :128] = s > t
    m2[1, :, 128:] = s >= t
    c["mask2"] = m2
    mm_ = np.zeros((2, 128, 128), np.float32)
    mm_[0] = s > t
    mm_[1] = s < t
    c["mmask"] = mm_
    rm = np.ones((128, 512), np.float32)
    rm[:, ::128] = 0
    c["rmask"] = rm
    tok = np.arange(SEQ)
    row = (tok // 64).astype(np.float32)
    col = (tok % 64).astype(np.float32)
    nf = 16
    freqs = (np.float32(10000.0) ** (-np.arange(nf, dtype=np.float32) / nf)).astype(np.float32)
    ang = np.concatenate([row[:, None] * freqs, col[:, None] * freqs], -1).astype(np.float32)
    cosT = np.ones((128, T), np.float32)
    ssinT = np.zeros((128, T), np.float32)
    for p_ in range(128):
        n = p_ % 64
        cosT[p_, NCTX:] = np.cos(ang[:, n % 32])
        ssinT[p_, NCTX:] = np.sin(ang[:, n % 32]) * (-1.0 if n < 32 else 1.0)
    c["cosT"] = cosT
    c["ssinT"] = ssinT
    lg = np.log(1.0 - 2.0 ** (-5.0 - np.arange(8, dtype=np.float64)))
    dt_ = np.zeros((2, 4, 128, 2, 128), np.float32)
    qd = np.zeros((2, 4, 128, 128), np.float32)
    kd = np.zeros((2, 4, 128, 128), np.float32)
    cd = np.zeros((128, 8), np.float32)
    sv = np.arange(128)
    for d in range(2):
        for hp in range(4):
            for e in range(2):
                h = 2 * hp + e
                g = lg[h] if d == 0 else lg[7 - h]
                if d == 0:
                    dt_[d, hp, :, e, :] = np.where(t >= s, np.exp(g * np.maximum(t - s, 0)), 0)
                    qd[d, hp, 64 * e:64 * e + 64, :] = np.exp(g * (sv + 1.0))[None, :]
                    kd[d, hp, :, 64 * e:64 * e + 64] = np.exp(g * (127.0 - sv))[:, None]
                else:
                    dt_[d, hp, :, e, :] = np.where(s >= t, np.exp(g * np.maximum(s - t, 0)), 0)
                    qd[d, hp, 64 * e:64 * e + 64, :] = np.exp(g * (128.0 - sv))[None, :]
                    kd[d, hp, :, 64 * e:64 * e + 64] = np.exp(g * sv)[:, None]
                cd[64 * e:64 * e + 64, d * 4 + hp] = np.exp(g * 128.0)
    c["dT"] = dt_.reshape(8, 128, 256)
    c["qd"] = qd.reshape(8, 128, 128)
    c["kd"] = kd.reshape(8, 128, 128)
    c["cd"] = cd
    return c


CONST_SHAPES = {"ident": [128, 128], "onesr": [128, 128], "bo": [128, 128], "pm": [128, 128], "mask2": [2, 128, 256],
                "mmask": [2, 128, 128], "rmask": [128, 512], "cosT": [128, T], "ssinT": [128, T],
                "dT": [8, 128, 256], "qd": [8, 128, 128], "kd": [8, 128, 128], "cd": [128, 8]}
WEIGHT_SHAPES = {"w_ada": [2, 1024, 6144], "w_in": [2, 1024, NIN], "w_vres_down": [1, 1024, 32],
                 "w_up": [2, 2, 64, 512], "a_up": [2, 2, 64, 512], "g_up": [2, 160, 512], "v_up": [1, 32, 512],
                 "w_out": [2, 1024, 1024], "w_ffn_in": [2, 1024, 2 * HID], "w_ffn_out": [2, HID, 1024]}


class Reg:
    __slots__ = ("w", "r")

    def __init__(self):
        self.w = None
        self.r = {}


class V:
    def __init__(self, ap, g):
        self.ap = ap
        self.g = g if isinstance(g, list) else [g]

    def __getitem__(self, idx):
        return V(self.ap[idx], self.g)

    def rr(self, pat, **kw):
        return V(self.ap.rearrange(pat, **kw), self.g)

    def bc(self, shape):
        return V(self.ap.to_broadcast(shape), self.g)

    def bitcast(self, dt):
        return V(self.ap.bitcast(dt), self.g)


class KB:
    def __init__(self, nc, es):
        self.nc = nc
        self.es = es
        self.eng = {"pe": nc.tensor, "dve": nc.vector, "act": nc.scalar, "pool": nc.gpsimd, "sp": nc.sync}
        self.sem = {e: es.enter_context(nc.semaphore("s_" + e)) for e in self.eng}
        self.cnt = {e: 0 for e in self.eng}
        self.seen = {e: {} for e in self.eng}
        self.dsem = [es.enter_context(nc.semaphore("d%d" % i)) for i in range(NDS)]
        self.dcnt = [0] * NDS
        self.dnext = 0
        self.nins = 0

    def sb(self, name, shape, dt=F32):
        t = self.es.enter_context(self.nc.sbuf_tensor(name, list(shape), dt))
        return V(t[:], Reg())

    def _wait(self, e, ev):
        key, sem, val = ev
        if self.seen[e].get(key, 0) >= val:
            return
        if key == e and (e == "pe" or not SAME):
            return
        self.eng[e].wait_ge(sem, val)
        self.seen[e][key] = val
        self.nins += 1

    def _deps(self, e, reads, writes):
        for r in reads:
            if r.w is not None:
                self._wait(e, r.w)
        for w in writes:
            if w.w is not None:
                self._wait(e, w.w)
            for ev in list(w.r.values()):
                self._wait(e, ev)

    def _commit(self, ev, reads, writes):
        for r in reads:
            old = r.r.get(ev[0])
            if old is None or old[2] < ev[2]:
                r.r[ev[0]] = ev
        for w in writes:
            w.w = ev
            w.r = {}

    def op(self, e, fn, reads, writes):
        self._deps(e, reads, writes)
        ins = fn(self.eng[e])
        self.cnt[e] += 1
        ins.then_inc(self.sem[e], 1)
        self.nins += 1
        self._commit((e, self.sem[e], self.cnt[e]), reads, writes)

    def dma(self, q, out, in_):
        reads, writes = in_.g, out.g
        i = self.dnext
        self.dnext = (i + 1) % NDS
        key = ("d", i)
        if self.dcnt[i] > 0:
            self._wait(q, (key, self.dsem[i], self.dcnt[i]))
        self._deps(q, reads, writes)
        self.dcnt[i] += 16
        self.eng[q].dma_start(out=out.ap, in_=in_.ap).then_inc(self.dsem[i], 16)
        self.nins += 1
        self._commit((key, self.dsem[i], self.dcnt[i]), reads, writes)

    def wait_all(self, e, regs):
        self._deps(e, regs, [])

    def tt(self, e, out, in0, in1, op):
        self.op(e, lambda E: E.tensor_tensor(out=out.ap, in0=in0.ap, in1=in1.ap, op=op), in0.g + in1.g, out.g)

    def ts(self, e, out, in0, s1, s2=None, op0=ALU.mult, op1=None):
        rd = list(in0.g)
        a1 = s1
        a2 = s2
        if isinstance(s1, V):
            rd += s1.g
            a1 = s1.ap
        if isinstance(s2, V):
            rd += s2.g
            a2 = s2.ap
        if op1 is None:
            self.op(e, lambda E: E.tensor_scalar(out=out.ap, in0=in0.ap, scalar1=a1, scalar2=None, op0=op0), rd, out.g)
        else:
            self.op(e, lambda E: E.tensor_scalar(out=out.ap, in0=in0.ap, scalar1=a1, scalar2=a2, op0=op0, op1=op1), rd, out.g)

    def stt(self, out, in0, sc, in1, op0, op1):
        rd = in0.g + in1.g
        a = sc
        if isinstance(sc, V):
            rd = rd + sc.g
            a = sc.ap
        self.op("dve", lambda E: E.scalar_tensor_tensor(out=out.ap, in0=in0.ap, scalar=a, in1=in1.ap, op0=op0, op1=op1), rd, out.g)

    def act(self, out, in_, func, bias=None, scale=1.0):
        rd = list(in_.g)
        kw = {}
        if isinstance(bias, V):
            rd += bias.g
            kw["bias"] = bias.ap
        elif bias is not None:
            kw["bias"] = bias
        if isinstance(scale, V):
            rd += scale.g
            kw["scale"] = scale.ap
        else:
            kw["scale"] = scale
        self.op("act", lambda E: E.activation(out=out.ap, in_=in_.ap, func=func, **kw), rd, out.g)

    def cp(self, e, out, in_):
        if e == "act":
            self.act(out, in_, AF.Copy)
        else:
            self.op(e, lambda E: E.tensor_copy(out=out.ap, in_=in_.ap), in_.g, out.g)

    def memset(self, e, out, val):
        self.op(e, lambda E: E.memset(out.ap, val), [], out.g)

    def recip(self, out, in_):
        self.op("dve", lambda E: E.reciprocal(out=out.ap, in_=in_.ap), in_.g, out.g)

    def mm(self, out, lhsT, rhs, start=True, stop=True):
        self.op("pe", lambda E: E.matmul(out.ap, lhsT=lhsT.ap, rhs=rhs.ap, start=start, stop=stop), lhsT.g + rhs.g, out.g)

    def tr(self, out, in_, ident):
        self.op("pe", lambda E: E.transpose(out.ap, in_.ap, ident.ap), in_.g + ident.g, out.g)

    def scan(self, out, d0, d1, init, op0, op1):
        self.op("dve", lambda E: E.tensor_tensor_scan(out=out.ap, data0=d0.ap, data1=d1.ap, initial=init, op0=op0, op1=op1),
                d0.g + d1.g, out.g)


def build(nb, nlayers=DEPTH, dbg=()):
    nc = bass.Bass("TRN2", target_bir_lowering=False)
    es = ExitStack()
    k = KB(nc, es)

    def din(name, shape):
        return V(nc.dram_tensor(name, list(shape), F32, kind="ExternalInput").ap(), Reg())

    xT_d = din("xT", [nb, D, SEQ])
    ctxT_d = din("ctxT", [nb, D, NCTX])
    cT_d = din("cT", [D, 5])
    sp_d = din("sp", [128, NSP])
    CD = {n: din("c_" + n, s) for n, s in CONST_SHAPES.items()}
    WD = {n: din(n, s) for n, s in WEIGHT_SHAPES.items()}
    outT_d = V(nc.dram_tensor("outT", [nb, D, SEQ], F32, kind="ExternalOutput").ap(), Reg())
    pbuf_ap = nc.dram_tensor("pbuf", [NPT, 128, T], F32, kind="Internal").ap()
    pbuf = [V(pbuf_ap[i], Reg()) for i in range(NPT)]
    vf_ap = nc.dram_tensor("vfirst", [4, 128, T], F32, kind="Internal").ap()
    vfd = [V(vf_ap[i], Reg()) for i in range(4)]
    dbg_out = {}

    def dump(name, v, shape, q="sp"):
        if name in dbg:
            o = V(nc.dram_tensor("dbg_" + name, list(shape), F32, kind="ExternalOutput").ap(), Reg())
            dbg_out[name] = o
            k.dma(q, o, v)

    xT = k.sb("xT_sb", [128, 8, T])
    hT = k.sb("hT_sb", [128, 8, T], BF16)
    psall = es.enter_context(nc.psum_tensor("ps", [128, 8, 512], F32))
    PSR = [Reg() for _ in range(8)]

    def PS(b, lo=0, hi=512):
        return V(psall[:, b, lo:hi], PSR[b])

    def PS2(b0, lo, hi):
        return V(psall[:, b0:b0 + 2, lo:hi], [PSR[b0], PSR[b0 + 1]])

    spt = k.sb("spt", [128, NSP])
    k.dma("sp", spt, sp_d)

    def SPc(name, idx):
        return spt[:, SPL[name] + idx:SPL[name] + idx + 1]

    cst = {}
    for n in ["onesr", "bo", "rmask"]:
        cst[n] = k.sb("sc_" + n, CONST_SHAPES[n])
        k.dma("sp", cst[n], CD[n])
    for n in ["ident", "pm"]:
        cst[n] = k.sb("sc_" + n, CONST_SHAPES[n], BF16)
        k.dma("pool", cst[n], CD[n])
    cst["mask2"] = k.sb("sc_mask2", [128, 2, 256])
    cst["mmask"] = k.sb("sc_mmask", [128, 2, 128])
    for d in range(2):
        k.dma("sp", cst["mask2"][:, d, :], CD["mask2"][d])
        k.dma("sp", cst["mmask"][:, d, :], CD["mmask"][d])
    cst["cd"] = k.sb("sc_cd", [128, 8])
    k.dma("sp", cst["cd"], CD["cd"])
    WAup = k.sb("WAup", [128, 2, 2, 512], BF16)
    G0up = k.sb("G0up", [128, 2, 512], BF16)
    GVup = k.sb("GVup", [64, 2, 512], BF16)
    for l in range(2):
        for d in range(2):
            k.dma("pool", WAup[0:64, l, d, :], WD["w_up"][l, d])
            k.dma("pool", WAup[64:128, l, d, :], WD["a_up"][l, d])
        k.dma("pool", G0up[:, l, :], WD["g_up"][l, 0:128, :])
        k.dma("pool", GVup[0:32, l, :], WD["g_up"][l, 128:160, :])
    k.dma("pool", GVup[32:64, 1, :], WD["v_up"][0])
    omm = k.sb("omm", [128, 30])
    mu0 = spt[:, SPL["mu"]:SPL["mu"] + 210].rr("p (t s) -> p t s", s=7)[:, :, 0]
    k.ts("dve", omm, mu0, -1.0, 1.0, ALU.mult, ALU.add)
    omka = k.sb("omka", [128, 8])
    k.ts("dve", omka, spt[:, SPL["ka"]:SPL["ka"] + 8], -1.0, 1.0, ALU.mult, ALU.add)

    ARW = 16384
    arena = es.enter_context(nc.sbuf_tensor("arena", [128, ARW], F32))

    class Phase:
        def __init__(self):
            self.off = 0
            self.regs = []

        def sb(self, shape, dt=F32):
            n = 1
            for s_ in shape[1:]:
                n *= s_
            words = n if dt == F32 else (n + 1) // 2
            words = (words + 7) // 8 * 8
            assert self.off + words <= ARW, (self.off, words)
            ap = arena[0:shape[0], self.off:self.off + words]
            self.off += words
            if dt != F32:
                ap = ap.bitcast(dt)
            ap = ap[:, 0:n]
            if len(shape) > 2:
                names = " ".join("d%d" % i for i in range(len(shape) - 1))
                ap = ap.rearrange("p (%s) -> p %s" % (names, names), **{"d%d" % i: shape[i + 1] for i in range(len(shape) - 1)})
            r = Reg()
            self.regs.append(r)
            return V(ap, r)

        def close(self):
            for e in ("pe", "dve", "act", "pool", "sp"):
                k._deps(e, [], self.regs)

    scT = k.sb("scT", [128, 8, 5])
    k.dma("sp", scT, cT_d.rr("(c p) w -> p c w", p=128))
    k.act(scT, scT, AF.Silu)
    ada = [k.sb("ada%d" % l, [128, 48, 5]) for l in range(2)]
    ph0 = Phase()
    wadab = [ph0.sb([128, 8, 512]) for i in range(2)]
    it = 0
    for l in range(nlayers):
        for qg in range(12):
            wb = wadab[it % 2]
            it += 1
            k.dma("sp", wb, WD["w_ada"][l].rr("(c p) n -> p c n", p=128)[:, :, qg * 512:(qg + 1) * 512])
            for qq in range(4):
                q = qg * 4 + qq
                for c in range(8):
                    k.mm(PS(0, q * 5, q * 5 + 5), wb[:, c, qq * 128:(qq + 1) * 128], scT[:, c, :], start=(c == 0), stop=(c == 7))
        k.tt("dve", ada[l], PS(0, 0, 240).rr("p (q w) -> p q w", w=5),
             spt[:, SPL["bada"] + l * 48:SPL["bada"] + (l + 1) * 48].rr("p (q o) -> p q o", o=1).bc([128, 48, 5]), ALU.add)
    ph0.close()
    A1 = [k.sb("A1_%d" % l, [128, 8, 5]) for l in range(2)]
    A2 = [k.sb("A2_%d" % l, [128, 8, 5]) for l in range(2)]
    for l in range(nlayers):
        for (A, nm, q0) in ((A1, "norm1", 8), (A2, "norm2", 32)):
            k.ts("dve", A[l], ada[l][:, q0:q0 + 8, :], 1.0, None, ALU.add)
            k.tt("dve", A[l], A[l], spt[:, SPL[nm] + l * 8:SPL[nm] + l * 8 + 8].rr("p (c o) -> p c o", o=1).bc([128, 8, 5]), ALU.mult)
    zero_col = k.sb("zero_col", [128, 1])
    k.memset("dve", zero_col, 0.0)
    eps_n = k.sb("eps_n", [128, 1])
    k.memset("dve", eps_n, 1e-6)
    eps_r = k.sb("eps_r", [128, 1])
    k.memset("dve", eps_r, 64e-5)
    eps_t = k.sb("eps_t", [128, 1])
    k.memset("dve", eps_t, 1e-5)

    def norm(ph, scale_fn, bias_fn, out_fn, blocks):
        sq = [ph.sb([128, 512]) for i in range(2)]
        rstd = ph.sb([128, 512])
        ntmp = [ph.sb([128, 512]) for i in range(2)]
        n = 0
        for (t0, t1) in blocks:
            W = t1 - t0
            for c in range(8):
                s = sq[n % 2]
                n += 1
                k.act(s[:, :W], xT[:, c, t0:t1], AF.Square)
                k.mm(PS(0, 0, W), cst["onesr"], s[:, :W], start=(c == 0), stop=(c == 7))
            k.act(rstd[:, :W], PS(0, 0, W), AF.Sqrt, bias=eps_n)
            k.recip(rstd[:, :W], rstd[:, :W])
            for c in range(8):
                tmp = ntmp[c % 2]
                k.tt("dve", tmp[:, :W], xT[:, c, t0:t1], rstd[:, :W], ALU.mult)
                b_ = bias_fn(c, t0 < NCTX)
                k.act(out_fn(c, t0, t1), tmp[:, :W], AF.Identity, bias=(b_ if b_ is not None else zero_col), scale=scale_fn(c, t0 < NCTX))

    def who_idx(is_ctx, b):
        return 4 if is_ctx else b

    RNG = [(256 * i, 256 * (i + 1)) for i in range(9)]
    RW = 256

    for b in range(nb):
        for c in range(8):
            k.dma("sp", xT[:, c, NCTX:T], xT_d[b, c * 128:(c + 1) * 128, :])
            k.dma("sp", xT[:, c, 0:NCTX], ctxT_d[b, c * 128:(c + 1) * 128, :])
        for l in range(nlayers):
            last = (l == DEPTH - 1)
            ph = Phase()
            norm(ph, lambda c, ic: A1[l][:, c, who_idx(ic, b):who_idx(ic, b) + 1],
                 lambda c, ic: ada[l][:, 0 + c, who_idx(ic, b):who_idx(ic, b) + 1],
                 lambda c, t0, t1: hT[:, c, t0:t1], BLK)
            ph.close()
            if ("hT_%d_%d" % (b, l)) in dbg:
                o_ = V(nc.dram_tensor("dbg_hT_%d_%d" % (b, l), [128, 8, T], BF16, kind="ExternalOutput").ap(), Reg())
                dbg_out["hT_%d_%d" % (b, l)] = o_
                k.dma("sp", o_, hT)
            ph = Phase()
            ubuf = [ph.sb([128, T]) for i in range(2)]
            lbuf = [ph.sb([128, T]) for i in range(1)]
            wg = [ph.sb([128, 8, 512], BF16) for i in range(2)]
            groups = [(0, 512), (512, 512), (1024, 512), (1536, 288)] + [(1824 + 512 * i, 512) for i in range(5)]
            ti = 0
            gi = 0
            for (c0, ncol) in groups:
                wgb = wg[gi % 2]
                gi += 1
                k.dma("pool", wgb[:, :, 0:ncol], WD["w_in"][l].rr("(c p) n -> p c n", p=128)[:, :, c0:c0 + ncol])
                if c0 == 1536 and l == 1:
                    k.dma("pool", wgb[:, :, 288:320], WD["w_vres_down"][0].rr("(c p) n -> p c n", p=128))
                if c0 == 1536:
                    tl = [(0, 128), (128, 128), (256, 64 if l == 1 else 32)]
                else:
                    tl = [(i * 128, 128) for i in range(4)]
                for (off, M) in tl:
                    is_rwkv = ti < 15
                    u = ubuf[ti % 2]
                    o = lbuf[0]
                    dst = u if is_rwkv else o
                    for bi, (t0, t1) in enumerate(BLK):
                        W = t1 - t0
                        pb = (ti * 5 + bi) % 8
                        for c in range(8):
                            k.mm(PS(pb, 0, W)[0:M], wgb[:, c, off:off + M], hT[:, c, t0:t1], start=(c == 0), stop=(c == 7))
                        k.cp("act" if (bi % 2 == 0) else "dve", dst[0:M, t0:t1], PS(pb, 0, W)[0:M])
                    if is_rwkv:
                        mub = SPL["mu"] + (l * 15 + ti) * 7
                        k.act(o[0:M, :], u[0:M, :], AF.Copy, scale=omm[0:M, l * 15 + ti:l * 15 + ti + 1])
                        cols = rwkv_tile_cols(l)[ti]
                        ldirs = sorted(set((c_[1] // 8) if isinstance(c_, tuple) else (c_ // 456) for c_ in cols))
                        cdirs = sorted(set((c_[1] // 16) if isinstance(c_, tuple) else (c_ // 912) for c_ in cols))
                        uL = u[0:M, NCTX:T].rr("p (r c) -> p r c", c=64)
                        oL = o[0:M, NCTX:T].rr("p (r c) -> p r c", c=64)
                        for dr in ldirs:
                            m = spt[0:M, mub + 1 + dr:mub + 2 + dr]
                            if dr == 0:
                                k.stt(oL[:, :, 1:64], uL[:, :, 0:63], m, oL[:, :, 1:64], ALU.mult, ALU.add)
                            elif dr == 1:
                                k.stt(oL[:, :, 0:63], uL[:, :, 1:64], m, oL[:, :, 0:63], ALU.mult, ALU.add)
                            elif dr == 2:
                                k.stt(o[0:M, NCTX + 64:T], u[0:M, NCTX:T - 64], m, o[0:M, NCTX + 64:T], ALU.mult, ALU.add)
                            else:
                                k.stt(o[0:M, NCTX:T - 64], u[0:M, NCTX + 64:T], m, o[0:M, NCTX:T - 64], ALU.mult, ALU.add)
                        for dr in cdirs:
                            m = spt[0:M, mub + 5 + dr:mub + 6 + dr]
                            if dr == 0:
                                k.stt(o[0:M, 1:NCTX], u[0:M, 0:NCTX - 1], m, o[0:M, 1:NCTX], ALU.mult, ALU.add)
                            else:
                                k.stt(o[0:M, 0:NCTX - 1], u[0:M, 1:NCTX], m, o[0:M, 0:NCTX - 1], ALU.mult, ALU.add)
                        if ti == 12:
                            k.act(o[0:64, :], o[0:64, :], AF.Tanh)
                        elif ti == 13:
                            k.act(o[0:128, :], o[0:128, :], AF.Sigmoid)
                        elif ti == 14:
                            k.act(o[0:32, :], o[0:32, :], AF.Sigmoid)
                    k.dma("sp", pbuf[ti][0:M, :], o[0:M, :])
                    if l == 0 and 8 <= ti < 12:
                        k.dma("sp", vfd[ti - 8], o[0:M, :])
                    ti += 1
            assert ti == NPT
            ph.close()
            if ("pbuf_%d_%d" % (b, l)) in dbg:
                o_ = V(nc.dram_tensor("dbg_pbuf_%d_%d" % (b, l), [NPT, 128, T], F32, kind="ExternalOutput").ap(), Reg())
                dbg_out["pbuf_%d_%d" % (b, l)] = o_
                for i_ in range(NPT):
                    k.dma("sp", o_[i_], pbuf[i_])
            if "stop_p2" in dbg:
                break
            ph = Phase()
            WAs = ph.sb([128, RW], BF16)
            G0s = ph.sb([128, RW], BF16)
            G1s = ph.sb([64, RW], BF16)
            ybuf = ph.sb([128, T])
            fb = [ph.sb([128, RW]) for i in range(13)]
            hb = [ph.sb([128, RW], BF16) for i in range(6)]
            ARb = ph.sb([128, 2, 2, 128], BF16)
            Btok = ph.sb([128, 2, 128], BF16)
            Ktok = ph.sb([128, 2, 128], BF16)
            Vtok = ph.sb([128, 2, 128], BF16)
            gC = ph.sb([128, 2])
            Am = ph.sb([128, 2, 2, 256], BF16)
            NMb = [ph.sb([128, 2, 2, 128]) for i in range(2)]
            Ppb = [ph.sb([128, 2, 128]) for i in range(2)]
            Xsb = ph.sb([128, 2, 64])
            Usb = ph.sb([128, 2, 64], BF16)
            Sst = ph.sb([128, 64])
            Sbf = ph.sb([128, 64], BF16)
            scm = ph.sb([128, 2, 128], BF16)
            dTt = ph.sb([128, 256])
            qdt = ph.sb([128, 128])
            kdt = ph.sb([128, 128])
            oT = hT
            W = RW
            nch = 2

            def v3(x):
                return x.rr("p (c t) -> p c t", t=128)

            def gn_block(src, wcol, bcol, eps_col, out_f):
                cen, sqv, rs = fb[10], fb[11], fb[12]
                k.mm(PS(0, 0, W), cst["bo"], src, True, True)
                k.stt(cen, PS(0, 0, W), -1.0 / 64, src, ALU.mult, ALU.add)
                k.act(sqv, cen, AF.Square)
                k.mm(PS(1, 0, W), cst["bo"], sqv, True, True)
                k.act(rs, PS(1, 0, W), AF.Sqrt, bias=eps_col, scale=1.0 / 64)
                k.recip(rs, rs)
                k.tt("dve", cen, cen, rs, ALU.mult)
                k.act(out_f, cen, AF.Identity, bias=bcol, scale=wcol)

            def transposes(src_bf, dst_tok, mul_tab=None):
                pt = PS(4).bitcast(BF16)
                for c in range(nch):
                    k.tr(pt[:, c * 128:(c + 1) * 128], src_bf[:, c * 128:(c + 1) * 128], cst["ident"])
                ptv = pt[:, 0:nch * 128].rr("p (c f) -> p c f", f=128)
                if mul_tab is None:
                    k.cp("act", dst_tok, ptv)
                else:
                    k.tt("dve", dst_tok, ptv, mul_tab.rr("p (o f) -> p o f", o=1).bc([128, nch, 128]), ALU.mult)

            for hp in range(4 if "skip_rwkv" not in dbg else 0):
                hc = slice(hp * 128, (hp + 1) * 128)
                if l == 1:
                    for (t0, t1) in RNG:
                        vF, vfF, sg = fb[0], fb[1], fb[2]
                        k.dma("sp", vF, pbuf[8 + hp][:, t0:t1])
                        k.dma("sp", vfF, vfd[hp][:, t0:t1])
                        k.dma("pool", G1s, pbuf[14][0:64, t0:t1])
                        k.mm(PS(0, 0, W), GVup[32:64, 1, hc], G1s[32:64, :])
                        k.act(sg, PS(0, 0, W), AF.Sigmoid, bias=SPc("v0", hp))
                        k.tt("dve", vfF, vfF, vF, ALU.subtract)
                        k.tt("dve", vfF, vfF, sg, ALU.mult)
                        k.tt("dve", vF, vF, vfF, ALU.add)
                        k.dma("sp", pbuf[8 + hp][:, t0:t1], vF)
                for d in range(2):
                    order = list(range(9)) if d == 0 else [0] + list(range(8, 0, -1))
                    k.memset("dve", Sst, 0.0)
                    k.memset("dve", Sbf, 0.0)
                    for ri in order:
                        t0, t1 = RNG[ri]
                        rF, kF, vF = fb[0], fb[1], fb[2]
                        k.dma("sp", rF, pbuf[0 + hp][:, t0:t1])
                        k.dma("sp", kF, pbuf[4 + hp][:, t0:t1])
                        k.dma("sp", vF, pbuf[8 + hp][:, t0:t1])
                        k.dma("pool", WAs, pbuf[12][:, t0:t1])
                        sw, Lr, Lin, Lex, icl, kk, t1b, t2b, Eb = fb[3], fb[4], fb[5], fb[6], fb[7], fb[8], fb[9], fb[10], fb[11]
                        k.mm(PS(0, 0, W), WAup[0:64, l, d, hc], WAs[0:64, :])
                        k.act(sw, PS(0, 0, W), AF.Sigmoid, bias=SPc("w0", (l * 2 + d) * 4 + hp))
                        k.scan(Lr, cst["rmask"][:, :W], sw, 0.0, ALU.mult, ALU.add)
                        totb = v3(Lr)[:, :, 127:128].bc([128, nch, 128])
                        if d == 0:
                            k.cp("dve", Lin, Lr)
                        else:
                            k.tt("dve", t1b, sw, Lr, ALU.subtract)
                            k.tt("dve", v3(Lin), v3(t1b), totb, ALU.add)
                        k.tt("dve", Lex, Lin, sw, ALU.subtract)
                        k.act(gC, v3(Lr)[:, :, 127], AF.Exp, scale=-CDEC)
                        k.mm(PS(1, 0, W), WAup[64:128, l, d, hc], WAs[64:128, :])
                        k.act(icl, PS(1, 0, W), AF.Sigmoid, bias=SPc("a0", (l * 2 + d) * 4 + hp))
                        k.ts("dve", kk, kF, SPc("kk", l * 4 + hp), None, ALU.mult)
                        k.act(t1b, kk, AF.Square)
                        k.mm(PS(0, 0, W), cst["bo"], t1b)
                        k.ts("dve", t2b, PS(0, 0, W), 1e-24, None, ALU.max)
                        k.act(t2b, t2b, AF.Sqrt)
                        k.recip(t2b, t2b)
                        k.tt("dve", kk, kk, t2b, ALU.mult)
                        k.act(Eb, Lex, AF.Exp, scale=-CDEC)
                        k.stt(ARb[:, :, 0, :], v3(kk), -1.0, v3(Eb), ALU.mult, ALU.mult)
                        k.act(Eb, Lin, AF.Exp, scale=-CDEC)
                        k.tt("dve", ARb[:, :, 1, :], v3(rF), v3(Eb), ALU.mult)
                        ktil, bp = t1b, t2b
                        k.ts("dve", ktil, icl, SPc("ka", l * 4 + hp), omka[:, l * 4 + hp:l * 4 + hp + 1], ALU.mult, ALU.add)
                        k.tt("dve", ktil, ktil, kF, ALU.mult)
                        k.tt("dve", bp, kk, icl, ALU.mult)
                        BH, KH, bck, kck, vbf = hb[0], hb[1], hb[2], hb[3], hb[4]
                        k.act(Eb, Lin, AF.Exp, scale=CDEC)
                        k.tt("dve", BH, bp, Eb, ALU.mult)
                        k.tt("dve", KH, ktil, Eb, ALU.mult)
                        k.tt("dve", v3(Lex), v3(Lin), totb, ALU.subtract)
                        k.act(Eb, Lex, AF.Exp, scale=CDEC)
                        k.tt("dve", bck, bp, Eb, ALU.mult)
                        k.tt("dve", kck, ktil, Eb, ALU.mult)
                        k.cp("act", vbf, vF)
                        transposes(bck, Btok)
                        transposes(kck, Ktok)
                        transposes(vbf, Vtok)
                        corder = list(range(nch)) if d == 0 else list(range(nch - 1, -1, -1))
                        for c in corder:
                            cs = slice(c * 128, (c + 1) * 128)
                            for e in range(2):
                                Re = slice(64 * e, 64 * e + 64)
                                arv = ARb[Re, c, :, :].rr("p a t -> p (a t)")
                                k.mm(PS(e, 0, 256), BH[Re, cs], arv)
                                k.mm(PS(e, 256, 512), KH[Re, cs], arv)
                                k.mm(PS(2 + e, 0, 128), ARb[Re, c, 0, :], BH[Re, cs])
                            for e in range(2):
                                k.tt("dve", Am[:, e, :, :], PS(e).rr("p (a t) -> p a t", a=2),
                                     cst["mask2"][:, d, :].rr("p (o t) -> p o t", o=1).bc([128, 2, 256]), ALU.mult)
                            Mps = PS2(2, 0, 128)
                            nm0 = NMb[0]
                            k.tt("dve", nm0[:, :, 0, :], PS2(0, 0, 128), cst["mask2"][:, d, 0:128].rr("p (o t) -> p o t", o=1).bc([128, 2, 128]), ALU.mult)
                            k.tt("dve", nm0[:, :, 1, :], Mps, cst["mmask"][:, d, :].rr("p (o t) -> p o t", o=1).bc([128, 2, 128]), ALU.mult)
                            k.tt("dve", Ppb[0], nm0[:, :, 0, :], cst["ident"].rr("p (o t) -> p o t", o=1).bc([128, 2, 128]), ALU.add)
                            cur = 0
                            for itn in range(6):
                                nmc, nmn = NMb[cur], NMb[1 - cur]
                                pc, pn = Ppb[cur], Ppb[1 - cur]
                                for e in range(2):
                                    if itn < 5:
                                        k.mm(PS(4, e * 256, e * 256 + 128), nmc[:, e, 1, :], nmc[:, e, 0, :])
                                    k.mm(PS(4, e * 256 + 128, e * 256 + 256), nmc[:, e, 0, :], nmc[:, e, 1, :])
                                if itn < 5:
                                    k.cp("act", nmn.rr("p e a t -> p (e a t)"), PS(4))
                                else:
                                    k.cp("act", nmn[:, :, 1, :], PS(4).rr("p (e a t) -> p e a t", e=2, a=2)[:, :, 1, :])
                                for e in range(2):
                                    k.mm(PS(5, e * 128, e * 128 + 128), nmn[:, e, 1, :], pc[:, e, :])
                                k.tt("dve", pn, PS(5, 0, 256).rr("p (e t) -> p e t", e=2), pc, ALU.add)
                                cur = 1 - cur
                            TT = Ppb[cur]
                            for e in range(2):
                                Re = slice(64 * e, 64 * e + 64)
                                k.mm(PS(2 + e, 128, 192), ARb[Re, c, 0, :], Sbf[Re, :], True, False)
                                k.mm(PS(2 + e, 128, 192), Am[:, e, 1, 0:128], Vtok[:, c, Re], False, True)
                            k.cp("act", Xsb, PS2(2, 128, 192))
                            for e in range(2):
                                k.mm(PS(2 + e, 192, 256), TT[:, e, :], Xsb[:, e, :])
                            k.cp("act", Usb, PS2(2, 192, 256))
                            for e in range(2):
                                Re = slice(64 * e, 64 * e + 64)
                                po = PS(6 + e, 0, 128)[Re]
                                k.mm(po, Sbf[Re, :], ARb[Re, c, 1, :], True, False)
                                k.mm(po, Usb[:, e, :], Am[:, e, 0, 128:256], False, False)
                                k.mm(po, Vtok[:, c, Re], Am[:, e, 1, 128:256], False, True)
                                yv = ybuf[Re, t0 + c * 128:t0 + (c + 1) * 128]
                                if d == 0:
                                    k.cp("act", yv, po)
                                else:
                                    k.tt("dve", yv, yv, po, ALU.add)
                            for e in range(2):
                                Ce = slice(64 * e, 64 * e + 64)
                                pd = PS(5, 256, 320)[Ce]
                                k.mm(pd, Btok[:, c, Ce], Usb[:, e, :], True, False)
                                k.mm(pd, Ktok[:, c, Ce], Vtok[:, c, Ce], False, True)
                            k.stt(Sst, Sst, gC[:, c:c + 1], PS(5, 256, 320), ALU.mult, ALU.add)
                            k.cp("act", Sbf, Sst)
                for (t0, t1) in RNG:
                    rF, kF, vF = fb[0], fb[1], fb[2]
                    k.dma("sp", rF, pbuf[0 + hp][:, t0:t1])
                    k.dma("sp", kF, pbuf[4 + hp][:, t0:t1])
                    k.dma("sp", vF, pbuf[8 + hp][:, t0:t1])
                    k.dma("pool", G0s, pbuf[13][:, t0:t1])
                    k.dma("pool", G1s, pbuf[14][0:64, t0:t1])
                    k.stt(rF, rF, SPc("rk", l * 4 + hp), kF, ALU.mult, ALU.mult)
                    k.mm(PS(2, 0, W), cst["bo"], rF)
                    k.tt("dve", vF, vF, PS(2, 0, W), ALU.mult)
                    gnv = fb[3]
                    gn_block(ybuf[:, t0:t1], SPc("lnw", l * 4 + hp), SPc("lnb", l * 4 + hp), eps_r, gnv)
                    k.tt("dve", gnv, gnv, vF, ALU.add)
                    k.mm(PS(3, 0, W), G0up[:, l, hc], G0s, True, False)
                    k.mm(PS(3, 0, W), GVup[0:32, l, hc], G1s[0:32, :], False, True)
                    k.tt("dve", oT[:, hp, t0:t1], gnv, PS(3, 0, W), ALU.mult)
            for hp in range(4 if "skip_ret" not in dbg else 0):
                for d in range(2):
                    order = list(range(9)) if d == 0 else [0] + list(range(8, 0, -1))
                    k.dma("sp", dTt, CD["dT"][d * 4 + hp])
                    k.dma("sp", qdt, CD["qd"][d * 4 + hp])
                    k.dma("sp", kdt, CD["kd"][d * 4 + hp])
                    k.memset("dve", Sst, 0.0)
                    k.memset("dve", Sbf, 0.0)
                    for ri in order:
                        t0, t1 = RNG[ri]
                        qF, kF, vF, gF, cosF, sinF = fb[0], fb[1], fb[2], fb[3], fb[4], fb[5]
                        k.dma("sp", qF, pbuf[15 + hp][:, t0:t1])
                        k.dma("sp", kF, pbuf[19 + hp][:, t0:t1])
                        k.dma("sp", vF, pbuf[23 + hp][:, t0:t1])
                        k.dma("sp", gF, pbuf[(27 if d == 0 else 31) + hp][:, t0:t1])
                        k.dma("sp", cosF, CD["cosT"][:, t0:t1])
                        k.dma("sp", sinF, CD["ssinT"][:, t0:t1])
                        qb, kb, qr, kr, qh, vbf = hb[0], hb[1], hb[2], hb[3], hb[4], hb[5]
                        t1b, t2b = fb[6], fb[7]
                        for (src, sb_, dstb, isq) in ((qF, qb, qr, True), (kF, kb, kr, False)):
                            k.cp("act", sb_, src)
                            k.mm(PS(0, 0, W), cst["pm"], sb_)
                            k.tt("dve", t1b, src, cosF, ALU.mult)
                            k.tt("dve", t2b, PS(0, 0, W), sinF, ALU.mult)
                            k.tt("dve", t1b, t1b, t2b, ALU.add)
                            if isq:
                                k.cp("act", dstb, t1b)
                                k.tt("dve", v3(qh), v3(t1b), qdt.rr("p (o t) -> p o t", o=1).bc([128, nch, 128]), ALU.mult)
                            else:
                                k.act(dstb, t1b, AF.Copy, scale=0.125)
                        k.cp("act", vbf, vF)
                        transposes(kr, Ktok, mul_tab=kdt)
                        transposes(vbf, Vtok)
                        orng = fb[8]
                        corder = list(range(nch)) if d == 0 else list(range(nch - 1, -1, -1))
                        for c in corder:
                            cs = slice(c * 128, (c + 1) * 128)
                            for e in range(2):
                                Re = slice(64 * e, 64 * e + 64)
                                k.mm(PS(e, 0, 128), kr[Re, cs], qr[Re, cs])
                            k.tt("dve", scm, PS2(0, 0, 128), dTt.rr("p (e t) -> p e t", e=2), ALU.mult)
                            for e in range(2):
                                Re = slice(64 * e, 64 * e + 64)
                                po = PS(6 + e, 0, 128)[Re]
                                k.mm(po, Sbf[Re, :], qh[Re, cs], True, False)
                                k.mm(po, Vtok[:, c, Re], scm[:, e, :], False, True)
                                k.cp("act", orng[Re, cs], po)
                            for e in range(2):
                                Ce = slice(64 * e, 64 * e + 64)
                                pd = PS(5, 256, 320)[Ce]
                                k.mm(pd, Ktok[:, c, Ce], Vtok[:, c, Ce], True, True)
                            k.stt(Sst, Sst, cst["cd"][:, d * 4 + hp:d * 4 + hp + 1], PS(5, 256, 320), ALU.mult, ALU.add)
                            k.cp("act", Sbf, Sst)
                        gnv = fb[9]
                        gn_block(orng, SPc("rgw", l * 4 + hp), SPc("rgb", l * 4 + hp), eps_t, gnv)
                        k.act(gF, gF, AF.Silu)
                        if d == 0:
                            k.tt("dve", ybuf[:, t0:t1], gnv, gF, ALU.mult)
                        else:
                            k.tt("dve", gnv, gnv, gF, ALU.mult)
                            k.tt("dve", oT[:, 4 + hp, t0:t1], gnv, ybuf[:, t0:t1], ALU.add)
            ph.close()
            if ("oT_%d_%d" % (b, l)) in dbg:
                o_ = V(nc.dram_tensor("dbg_oT_%d_%d" % (b, l), [128, 8, T], BF16, kind="ExternalOutput").ap(), Reg())
                dbg_out["oT_%d_%d" % (b, l)] = o_
                k.dma("sp", o_, oT)
            if "stop_p3" in dbg:
                break
            ph = Phase()
            wob = ph.sb([128, 8, 1024], BF16)
            k.dma("pool", wob, WD["w_out"][l].rr("(c p) n -> p c n", p=128))
            n = 0
            for m in range(8):
                for (t0, t1) in BLK:
                    W = t1 - t0
                    pb = n % 8
                    n += 1
                    for kc in range(8):
                        k.mm(PS(pb, 0, W), wob[:, kc, m * 128:(m + 1) * 128], oT[:, kc, t0:t1], start=(kc == 0), stop=(kc == 7))
                    wi = who_idx(t0 < NCTX, b)
                    k.stt(xT[:, m, t0:t1], PS(pb, 0, W), ada[l][:, 16 + m, wi:wi + 1], xT[:, m, t0:t1], ALU.mult, ALU.add)
            ph.close()
            ph = Phase()
            norm(ph, lambda c, ic: A2[l][:, c, who_idx(ic, b):who_idx(ic, b) + 1],
                 lambda c, ic: ada[l][:, 24 + c, who_idx(ic, b):who_idx(ic, b) + 1],
                 lambda c, t0, t1: hT[:, c, t0:t1], BLK)
            ph.close()
            ph = Phase()
            wgf = [ph.sb([128, 8, 512], BF16) for i in range(2)]
            actb = ph.sb([128, NJ, 512], BF16)
            wfo = [ph.sb([128, NJ, 128], BF16) for i in range(2)]
            sgb = [ph.sb([128, 512]) for i in range(2)]
            wfi_v = WD["w_ffn_in"][l].rr("(c p) n -> p c n", p=128)
            wfo_v = WD["w_ffn_out"][l].rr("(j p) n -> p j n", p=128)
            wi_ = 0
            for (t0, t1) in BLK:
                W = t1 - t0
                if last and t0 < NCTX:
                    continue
                for jg in range(6):
                    nj = 4 if jg < 5 else 2
                    wgate = wgf[0]
                    wup = wgf[1]
                    k.dma("pool", wgate[:, :, 0:nj * 128], wfi_v[:, :, jg * 512:jg * 512 + nj * 128])
                    k.dma("pool", wup[:, :, 0:nj * 128], wfi_v[:, :, HID + jg * 512:HID + jg * 512 + nj * 128])
                    for jj in range(nj):
                        j = jg * 4 + jj
                        pg = (2 * j) % 8
                        pu = (2 * j + 1) % 8
                        for c in range(8):
                            k.mm(PS(pg, 0, W), wgate[:, c, jj * 128:(jj + 1) * 128], hT[:, c, t0:t1], start=(c == 0), stop=(c == 7))
                        for c in range(8):
                            k.mm(PS(pu, 0, W), wup[:, c, jj * 128:(jj + 1) * 128], hT[:, c, t0:t1], start=(c == 0), stop=(c == 7))
                        sgt = sgb[j % 2]
                        k.act(sgt[:, :W], PS(pg, 0, W), AF.Silu)
                        k.tt("dve", actb[:, j, :W], sgt[:, :W], PS(pu, 0, W), ALU.mult)
                for m in range(8):
                    wf = wfo[wi_ % 2]
                    wi_ += 1
                    k.dma("pool", wf, wfo_v[:, :, m * 128:(m + 1) * 128])
                    pb = m % 8
                    for j in range(NJ):
                        k.mm(PS(pb, 0, W), wf[:, j, :], actb[:, j, :W], start=(j == 0), stop=(j == NJ - 1))
                    wi = who_idx(t0 < NCTX, b)
                    k.stt(xT[:, m, t0:t1], PS(pb, 0, W), ada[l][:, 40 + m, wi:wi + 1], xT[:, m, t0:t1], ALU.mult, ALU.add)
            ph.close()
            if ("xT_%d_%d" % (b, l)) in dbg:
                o_ = V(nc.dram_tensor("dbg_xT_%d_%d" % (b, l), [128, 8, T], F32, kind="ExternalOutput").ap(), Reg())
                dbg_out["xT_%d_%d" % (b, l)] = o_
                k.dma("sp", o_, xT)
        if "stop_p2" in dbg or "stop_p3" in dbg:
            continue
        ph = Phase()
        obuf = [ph.sb([128, T]) for i in range(2)]
        rst_all = ph.sb([128, T])
        sq = [ph.sb([128, 512]) for i in range(2)]
        n = 0
        for (t0, t1) in BLK[1:]:
            W = t1 - t0
            for c in range(8):
                s = sq[n % 2]
                n += 1
                k.act(s[:, :W], xT[:, c, t0:t1], AF.Square)
                k.mm(PS(0, 0, W), cst["onesr"], s[:, :W], start=(c == 0), stop=(c == 7))
            k.act(rst_all[:, t0:t1], PS(0, 0, W), AF.Sqrt, bias=eps_n)
            k.recip(rst_all[:, t0:t1], rst_all[:, t0:t1])
        for c in range(8):
            ob = obuf[c % 2]
            k.tt("dve", ob[:, NCTX:T], xT[:, c, NCTX:T], rst_all[:, NCTX:T], ALU.mult)
            k.act(ob[:, NCTX:T], ob[:, NCTX:T], AF.Copy, scale=SPc("normf", c))
            k.dma("sp", outT_d[b, c * 128:(c + 1) * 128, :], ob[:, NCTX:T])
        ph.close()
    k.wait_all("sp", [outT_d.g[0]] + [v.g[0] for v in dbg_out.values()])
    es.close()
    return nc, k, dbg_out


_CACHE = {}


def kernel(**inp):
    inp = {k_: np.asarray(v) for k_, v in inp.items()}
    ncores = 8
    B = inp["x"].shape[0]
    nb = B // ncores
    if "nc" not in _CACHE:
        _CACHE["nc"] = build(nb)[0]
    nc = _CACHE["nc"]
    xT = np.ascontiguousarray(np.transpose(inp["x"], (0, 2, 1)))
    ctxT = np.ascontiguousarray(np.transpose(inp["ctx"], (0, 2, 1)))
    sp = build_sp(inp)
    consts = build_consts()
    in_maps = []
    for i in range(ncores):
        m = {"xT": xT[i * nb:(i + 1) * nb], "ctxT": ctxT[i * nb:(i + 1) * nb]}
        cT = np.zeros((D, 5), np.float32)
        cT[:, :nb] = inp["c"][i * nb:(i + 1) * nb].T
        cT[:, 4] = inp["c_ctx"]
        m["cT"] = cT
        m["sp"] = sp
        for n_ in CONST_SHAPES:
            m["c_" + n_] = consts[n_]
        for n_ in WEIGHT_SHAPES:
            m[n_] = inp[n_]
        in_maps.append(m)
    res = run_bass_kernel_spmd(nc, in_maps, core_ids=list(range(ncores)))
    outT = np.concatenate([np.asarray(r["outT"]) for r in res.results], axis=0)
    return np.ascontiguousarray(np.transpose(outT, (0, 2, 1))).astype(np.float32)
```

```python
import numpy as np
import ml_dtypes
from contextlib import ExitStack
import concourse.bass as bass
import concourse.mybir as mybir
from concourse.bass_utils import run_bass_kernel_spmd

F32 = mybir.dt.float32
BF16 = mybir.dt.bfloat16
F32R = mybir.dt.float32r
DBL_R = False
ALU = mybir.AluOpType
AF = mybir.ActivationFunctionType

D = 1024
T = 2304
NCTX = 256
SEQ = 2048
DEPTH = 2
NIN = 4384
HID = 2816
NJ = HID // 128
CDEC = 0.6065306597126334
BLK = [(0, 256), (256, 768), (768, 1280), (1280, 1792), (1792, 2304)]
NPT = 35
SAME = True
NDS = 24

SPL = {}
_o = 0
for _n, _w in [("norm1", 16), ("norm2", 16), ("normf", 8), ("bada", 96), ("mu", 2 * 15 * 7), ("w0", 16), ("a0", 16),
               ("kk", 8), ("ka", 8), ("rk", 8), ("v0", 4), ("lnw", 8), ("lnb", 8), ("rgw", 8), ("rgb", 8)]:
    SPL[_n] = _o
    _o += _w
NSP = _o


def rwkv_tile_cols(l):
    tiles = []
    for i in range(12):
        tiles.append([i * 128 + p for p in range(128)])
    tiles.append([1536 + p for p in range(128)])
    tiles.append([1664 + p for p in range(128)])
    g1 = [1792 + p for p in range(32)]
    if l == 1:
        g1 += [("v", ch) for ch in range(32)]
    tiles.append(g1)
    return tiles


def build_sp(inp):
    sp = np.zeros((128, NSP), np.float32)
    p = np.arange(128)
    for l in range(2):
        for c in range(8):
            sp[:, SPL["norm1"] + l * 8 + c] = inp["norm1"][l, c * 128 + p]
            sp[:, SPL["norm2"] + l * 8 + c] = inp["norm2"][l, c * 128 + p]
        for q in range(48):
            sp[:, SPL["bada"] + l * 48 + q] = inp["b_ada"][l, q * 128 + p]
        tiles = rwkv_tile_cols(l)
        for ti, cols in enumerate(tiles):
            base = SPL["mu"] + (l * 15 + ti) * 7
            for pp, col in enumerate(cols):
                if isinstance(col, tuple):
                    ch = col[1]
                    m = inp["mu_vres"][0, ch]
                    ld, cd = ch // 8, ch // 16
                else:
                    m = inp["mu_rwkv"][l, col]
                    ld, cd = col // 456, col // 912
                sp[pp, base + 0] = m
                sp[pp, base + 1 + ld] = m
                sp[pp, base + 5 + cd] = m
        for d in range(2):
            for hp in range(4):
                sp[:, SPL["w0"] + (l * 2 + d) * 4 + hp] = inp["w0"][l, d, hp * 128 + p]
                sp[:, SPL["a0"] + (l * 2 + d) * 4 + hp] = inp["a0"][l, d, hp * 128 + p]
        for hp in range(4):
            sp[:, SPL["kk"] + l * 4 + hp] = inp["k_k"][l, hp * 128 + p]
            sp[:, SPL["ka"] + l * 4 + hp] = inp["k_a"][l, hp * 128 + p]
            sp[:, SPL["rk"] + l * 4 + hp] = inp["r_k"][l].reshape(512)[hp * 128 + p]
            sp[:, SPL["lnw"] + l * 4 + hp] = inp["ln_x_w"][l, hp * 128 + p]
            sp[:, SPL["lnb"] + l * 4 + hp] = inp["ln_x_b"][l, hp * 128 + p]
            sp[:, SPL["rgw"] + l * 4 + hp] = inp["ret_gn_w"][l, hp * 128 + p]
            sp[:, SPL["rgb"] + l * 4 + hp] = inp["ret_gn_b"][l, hp * 128 + p]
    for c in range(8):
        sp[:, SPL["normf"] + c] = inp["norm_f"][c * 128 + p]
    for hp in range(4):
        sp[:, SPL["v0"] + hp] = inp["v0"][0, hp * 128 + p]
    return sp


def build_consts():
    c = {}
    s = np.arange(128)[:, None]
    t = np.arange(128)[None, :]
    c["ident"] = np.eye(128, dtype=np.float32)
    c["onesr"] = np.full((128, 128), 1.0 / 1024, np.float32)
    bo = np.zeros((128, 128), np.float32)
    bo[:64, :64] = 1
    bo[64:, 64:] = 1
    c["bo"] = bo
    pm = np.zeros((128, 128), np.float32)
    for m in range(128):
        n = m % 64
        pm[(m - n) + ((n + 32) % 64), m] = 1
    c["pm"] = pm
    m2 = np.zeros((2, 128, 256), np.float32)
    m2[0, :, :128] = s < t
    m2[0, :, 128:] = s <= t
    m2[1, :, :128] = s > t
    m2[1, :, 128:] = s >= t
    c["mask2"] = m2
    mm_ = np.zeros((2, 128, 128), np.float32)
    mm_[0] = s > t
    mm_[1] = s < t
    c["mmask"] = mm_
    rm = np.ones((128, 512), np.float32)
    rm[:, ::128] = 0
    c["rmask"] = rm
    tok = np.arange(SEQ)
    row = (tok // 64).astype(np.float32)
    col = (tok % 64).astype(np.float32)
    nf = 16
    freqs = (np.float32(10000.0) ** (-np.arange(nf, dtype=np.float32) / nf)).astype(np.float32)
    ang = np.concatenate([row[:, None] * freqs, col[:, None] * freqs], -1).astype(np.float32)
    cosT = np.ones((128, T), np.float32)
    ssinT = np.zeros((128, T), np.float32)
    for p_ in range(128):
        n = p_ % 64
        cosT[p_, NCTX:] = np.cos(ang[:, n % 32])
        ssinT[p_, NCTX:] = np.sin(ang[:, n % 32]) * (-1.0 if n < 32 else 1.0)
    c["cosT"] = cosT
    c["ssinT"] = ssinT
    lg = np.log(1.0 - 2.0 ** (-5.0 - np.arange(8, dtype=np.float64)))
    dt_ = np.zeros((2, 4, 128, 2, 128), np.float32)
    qd = np.zeros((2, 4, 128, 128), np.float32)
    kd = np.zeros((2, 4, 128, 128), np.float32)
    cd = np.zeros((128, 8), np.float32)
    sv = np.arange(128)
    for d in range(2):
        for hp in range(4):
            for e in range(2):
                h = 2 * hp + e
                g = lg[h] if d == 0 else lg[7 - h]
                if d == 0:
                    dt_[d, hp, :, e, :] = np.where(t >= s, np.exp(g * np.maximum(t - s, 0)), 0)
                    qd[d, hp, 64 * e:64 * e + 64, :] = np.exp(g * (sv + 1.0))[None, :]
                    kd[d, hp, :, 64 * e:64 * e + 64] = np.exp(g * (127.0 - sv))[:, None]
                else:
                    dt_[d, hp, :, e, :] = np.where(s >= t, np.exp(g * np.maximum(s - t, 0)), 0)
                    qd[d, hp, 64 * e:64 * e + 64, :] = np.exp(g * (128.0 - sv))[None, :]
                    kd[d, hp, :, 64 * e:64 * e + 64] = np.exp(g * sv)[:, None]
                cd[64 * e:64 * e + 64, d * 4 + hp] = np.exp(g * 128.0)
    c["dT"] = dt_.reshape(8, 128, 256)
    c["qd"] = qd.reshape(8, 128, 128)
    c["kd"] = kd.reshape(8, 128, 128)
    c["cd"] = cd
    return c


CONST_SHAPES = {"ident": [128, 128], "onesr": [128, 128], "bo": [128, 128], "pm": [128, 128], "mask2": [2, 128, 256],
                "mmask": [2, 128, 128], "rmask": [128, 512], "cosT": [128, T], "ssinT": [128, T],
                "dT": [8, 128, 256], "qd": [8, 128, 128], "kd": [8, 128, 128], "cd": [128, 8]}
WEIGHT_SHAPES = {"w_ada": [2, 1024, 6144], "w_in": [2, 1024, NIN], "w_vres_down": [1, 1024, 32],
                 "w_up": [2, 2, 64, 512], "a_up": [2, 2, 64, 512], "g_up": [2, 160, 512], "v_up": [1, 32, 512],
                 "w_out": [2, 1024, 1024], "w_ffn_in": [2, 1024, 2 * HID], "w_ffn_out": [2, HID, 1024]}


class Reg:
    __slots__ = ("w", "r")

    def __init__(self):
        self.w = None
        self.r = {}


class V:
    def __init__(self, ap, g):
        self.ap = ap
        self.g = g if isinstance(g, list) else [g]

    def __getitem__(self, idx):
        return V(self.ap[idx], self.g)

    def rr(self, pat, **kw):
        return V(self.ap.rearrange(pat, **kw), self.g)

    def bc(self, shape):
        return V(self.ap.to_broadcast(shape), self.g)

    def bitcast(self, dt):
        return V(self.ap.bitcast(dt), self.g)


class KB:
    def __init__(self, nc, es):
        self.nc = nc
        self.es = es
        self.eng = {"pe": nc.tensor, "dve": nc.vector, "act": nc.scalar, "pool": nc.gpsimd, "sp": nc.sync}
        self.sem = {e: es.enter_context(nc.semaphore("s_" + e)) for e in self.eng}
        self.cnt = {e: 0 for e in self.eng}
        self.seen = {e: {} for e in self.eng}
        self.dsem = [es.enter_context(nc.semaphore("d%d" % i)) for i in range(NDS)]
        self.dcnt = [0] * NDS
        self.dnext = 0
        self.nins = 0

    def sb(self, name, shape, dt=F32):
        t = self.es.enter_context(self.nc.sbuf_tensor(name, list(shape), dt))
        return V(t[:], Reg())

    def _wait(self, e, ev):
        key, sem, val = ev
        if self.seen[e].get(key, 0) >= val:
            return
        if key == e and (e == "pe" or not SAME):
            return
        self.eng[e].wait_ge(sem, val)
        self.seen[e][key] = val
        self.nins += 1

    def _deps(self, e, reads, writes):
        for r in reads:
            if r.w is not None:
                self._wait(e, r.w)
        for w in writes:
            if w.w is not None:
                self._wait(e, w.w)
            for ev in list(w.r.values()):
                self._wait(e, ev)

    def _commit(self, ev, reads, writes):
        for r in reads:
            old = r.r.get(ev[0])
            if old is None or old[2] < ev[2]:
                r.r[ev[0]] = ev
        for w in writes:
            w.w = ev
            w.r = {}

    def op(self, e, fn, reads, writes):
        self._deps(e, reads, writes)
        ins = fn(self.eng[e])
        self.cnt[e] += 1
        ins.then_inc(self.sem[e], 1)
        self.nins += 1
        self._commit((e, self.sem[e], self.cnt[e]), reads, writes)

    def dma(self, q, out, in_):
        reads, writes = in_.g, out.g
        i = self.dnext
        self.dnext = (i + 1) % NDS
        key = ("d", i)
        if self.dcnt[i] > 0:
            self._wait(q, (key, self.dsem[i], self.dcnt[i]))
        self._deps(q, reads, writes)
        self.dcnt[i] += 16
        self.eng[q].dma_start(out=out.ap, in_=in_.ap).then_inc(self.dsem[i], 16)
        self.nins += 1
        self._commit((key, self.dsem[i], self.dcnt[i]), reads, writes)

    def wait_all(self, e, regs):
        self._deps(e, regs, [])

    def tt(self, e, out, in0, in1, op):
        self.op(e, lambda E: E.tensor_tensor(out=out.ap, in0=in0.ap, in1=in1.ap, op=op), in0.g + in1.g, out.g)

    def ts(self, e, out, in0, s1, s2=None, op0=ALU.mult, op1=None):
        rd = list(in0.g)
        a1 = s1
        a2 = s2
        if isinstance(s1, V):
            rd += s1.g
            a1 = s1.ap
        if isinstance(s2, V):
            rd += s2.g
            a2 = s2.ap
        if op1 is None:
            self.op(e, lambda E: E.tensor_scalar(out=out.ap, in0=in0.ap, scalar1=a1, scalar2=None, op0=op0), rd, out.g)
        else:
            self.op(e, lambda E: E.tensor_scalar(out=out.ap, in0=in0.ap, scalar1=a1, scalar2=a2, op0=op0, op1=op1), rd, out.g)

    def stt(self, out, in0, sc, in1, op0, op1):
        rd = in0.g + in1.g
        a = sc
        if isinstance(sc, V):
            rd = rd + sc.g
            a = sc.ap
        self.op("dve", lambda E: E.scalar_tensor_tensor(out=out.ap, in0=in0.ap, scalar=a, in1=in1.ap, op0=op0, op1=op1), rd, out.g)

    def act(self, out, in_, func, bias=None, scale=1.0):
        rd = list(in_.g)
        kw = {}
        if isinstance(bias, V):
            rd += bias.g
            kw["bias"] = bias.ap
        elif bias is not None:
            kw["bias"] = bias
        if isinstance(scale, V):
            rd += scale.g
            kw["scale"] = scale.ap
        else:
            kw["scale"] = scale
        self.op("act", lambda E: E.activation(out=out.ap, in_=in_.ap, func=func, **kw), rd, out.g)

    def cp(self, e, out, in_):
        if e == "act":
            self.act(out, in_, AF.Copy)
        else:
            self.op(e, lambda E: E.tensor_copy(out=out.ap, in_=in_.ap), in_.g, out.g)

    def memset(self, e, out, val):
        self.op(e, lambda E: E.memset(out.ap, val), [], out.g)

    def recip(self, out, in_):
        self.op("dve", lambda E: E.reciprocal(out=out.ap, in_=in_.ap), in_.g, out.g)

    def mm(self, out, lhsT, rhs, start=True, stop=True):
        self.op("pe", lambda E: E.matmul(out.ap, lhsT=lhsT.ap, rhs=rhs.ap, start=start, stop=stop), lhsT.g + rhs.g, out.g)

    def tr(self, out, in_, ident):
        self.op("pe", lambda E: E.transpose(out.ap, in_.ap, ident.ap), in_.g + ident.g, out.g)

    def scan(self, out, d0, d1, init, op0, op1):
        self.op("dve", lambda E: E.tensor_tensor_scan(out=out.ap, data0=d0.ap, data1=d1.ap, initial=init, op0=op0, op1=op1),
                d0.g + d1.g, out.g)


def build(nb, nlayers=DEPTH, dbg=()):
    nc = bass.Bass("TRN2", target_bir_lowering=False)
    es = ExitStack()
    k = KB(nc, es)

    def din(name, shape):
        return V(nc.dram_tensor(name, list(shape), F32, kind="ExternalInput").ap(), Reg())

    xT_d = din("xT", [nb, D, SEQ])
    ctxT_d = din("ctxT", [nb, D, NCTX])
    cT_d = din("cT", [D, 5])
    sp_d = din("sp", [128, NSP])
    CD = {n: din("c_" + n, s) for n, s in CONST_SHAPES.items()}
    WD = {n: din(n, s) for n, s in WEIGHT_SHAPES.items()}
    outT_d = V(nc.dram_tensor("outT", [nb, D, SEQ], F32, kind="ExternalOutput").ap(), Reg())
    pbuf_ap = nc.dram_tensor("pbuf", [NPT, 128, T], F32, kind="Internal").ap()
    pbuf = [V(pbuf_ap[i], Reg()) for i in range(NPT)]
    vf_ap = nc.dram_tensor("vfirst", [4, 128, T], F32, kind="Internal").ap()
    vfd = [V(vf_ap[i], Reg()) for i in range(4)]
    dbg_out = {}

    def dump(name, v, shape, q="sp"):
        if name in dbg:
            o = V(nc.dram_tensor("dbg_" + name, list(shape), F32, kind="ExternalOutput").ap(), Reg())
            dbg_out[name] = o
            k.dma(q, o, v)

    xT = k.sb("xT_sb", [128, 8, T])
    hT = k.sb("hT_sb", [128, 8, T], BF16)
    psall = es.enter_context(nc.psum_tensor("ps", [128, 8, 512], F32))
    PSR = [Reg() for _ in range(8)]

    def PS(b, lo=0, hi=512):
        return V(psall[:, b, lo:hi], PSR[b])

    def PS2(b0, lo, hi):
        return V(psall[:, b0:b0 + 2, lo:hi], [PSR[b0], PSR[b0 + 1]])

    spt = k.sb("spt", [128, NSP])
    k.dma("sp", spt, sp_d)

    def SPc(name, idx):
        return spt[:, SPL[name] + idx:SPL[name] + idx + 1]

    cst = {}
    for n in ["onesr", "bo", "rmask"]:
        cst[n] = k.sb("sc_" + n, CONST_SHAPES[n])
        k.dma("sp", cst[n], CD[n])
    for n in ["ident", "pm"]:
        cst[n] = k.sb("sc_" + n, CONST_SHAPES[n], BF16)
        k.dma("pool", cst[n], CD[n])
    cst["mask2"] = k.sb("sc_mask2", [128, 2, 256])
    cst["mmask"] = k.sb("sc_mmask", [128, 2, 128])
    for d in range(2):
        k.dma("sp", cst["mask2"][:, d, :], CD["mask2"][d])
        k.dma("sp", cst["mmask"][:, d, :], CD["mmask"][d])
    cst["cd"] = k.sb("sc_cd", [128, 8])
    k.dma("sp", cst["cd"], CD["cd"])
    WAup = k.sb("WAup", [128, 2, 2, 512], BF16)
    G0up = k.sb("G0up", [128, 2, 512], BF16)
    GVup = k.sb("GVup", [64, 2, 512], BF16)
    for l in range(2):
        for d in range(2):
            k.dma("pool", WAup[0:64, l, d, :], WD["w_up"][l, d])
            k.dma("pool", WAup[64:128, l, d, :], WD["a_up"][l, d])
        k.dma("pool", G0up[:, l, :], WD["g_up"][l, 0:128, :])
        k.dma("pool", GVup[0:32, l, :], WD["g_up"][l, 128:160, :])
    k.dma("pool", GVup[32:64, 1, :], WD["v_up"][0])
    omm = k.sb("omm", [128, 30])
    mu0 = spt[:, SPL["mu"]:SPL["mu"] + 210].rr("p (t s) -> p t s", s=7)[:, :, 0]
    k.ts("dve", omm, mu0, -1.0, 1.0, ALU.mult, ALU.add)
    omka = k.sb("omka", [128, 8])
    k.ts("dve", omka, spt[:, SPL["ka"]:SPL["ka"] + 8], -1.0, 1.0, ALU.mult, ALU.add)

    ARW = 15360
    arena = es.enter_context(nc.sbuf_tensor("arena", [128, ARW], F32))

    class Phase:
        def __init__(self):
            self.off = 0
            self.regs = []

        def sb(self, shape, dt=F32):
            n = 1
            for s_ in shape[1:]:
                n *= s_
            words = n if dt == F32 else (n + 1) // 2
            words = (words + 7) // 8 * 8
            assert self.off + words <= ARW, (self.off, words)
            ap = arena[0:shape[0], self.off:self.off + words]
            self.off += words
            if dt != F32:
                ap = ap.bitcast(dt)
            ap = ap[:, 0:n]
            if len(shape) > 2:
                names = " ".join("d%d" % i for i in range(len(shape) - 1))
                ap = ap.rearrange("p (%s) -> p %s" % (names, names), **{"d%d" % i: shape[i + 1] for i in range(len(shape) - 1)})
            r = Reg()
            self.regs.append(r)
            return V(ap, r)

        def close(self):
            for e in ("pe", "dve", "act", "pool", "sp"):
                k._deps(e, [], self.regs)

    scT = k.sb("scT", [128, 8, 5])
    k.dma("sp", scT, cT_d.rr("(c p) w -> p c w", p=128))
    k.act(scT, scT, AF.Silu)
    ada = [k.sb("ada%d" % l, [128, 48, 5]) for l in range(2)]
    ph0 = Phase()
    wadab = [ph0.sb([128, 8, 512]) for i in range(2)]
    it = 0
    for l in range(nlayers):
        for qg in range(12):
            wb = wadab[it % 2]
            it += 1
            k.dma("sp", wb, WD["w_ada"][l].rr("(c p) n -> p c n", p=128)[:, :, qg * 512:(qg + 1) * 512])
            for qq in range(4):
                q = qg * 4 + qq
                for c in range(8):
                    k.mm(PS(0, q * 5, q * 5 + 5), wb[:, c, qq * 128:(qq + 1) * 128], scT[:, c, :], start=(c == 0), stop=(c == 7))
        k.tt("dve", ada[l], PS(0, 0, 240).rr("p (q w) -> p q w", w=5),
             spt[:, SPL["bada"] + l * 48:SPL["bada"] + (l + 1) * 48].rr("p (q o) -> p q o", o=1).bc([128, 48, 5]), ALU.add)
    ph0.close()
    A1 = [k.sb("A1_%d" % l, [128, 8, 5]) for l in range(2)]
    A2 = [k.sb("A2_%d" % l, [128, 8, 5]) for l in range(2)]
    for l in range(nlayers):
        for (A, nm, q0) in ((A1, "norm1", 8), (A2, "norm2", 32)):
            k.ts("dve", A[l], ada[l][:, q0:q0 + 8, :], 1.0, None, ALU.add)
            k.tt("dve", A[l], A[l], spt[:, SPL[nm] + l * 8:SPL[nm] + l * 8 + 8].rr("p (c o) -> p c o", o=1).bc([128, 8, 5]), ALU.mult)
    zero_col = k.sb("zero_col", [128, 1])
    k.memset("dve", zero_col, 0.0)
    eps_n = k.sb("eps_n", [128, 1])
    k.memset("dve", eps_n, 1e-6)
    eps_r = k.sb("eps_r", [128, 1])
    k.memset("dve", eps_r, 64e-5)
    eps_t = k.sb("eps_t", [128, 1])
    k.memset("dve", eps_t, 1e-5)

    def norm(ph, scale_fn, bias_fn, out_fn, blocks):
        sq = [ph.sb([128, 512]) for i in range(2)]
        rstd = ph.sb([128, 512])
        ntmp = [ph.sb([128, 512]) for i in range(2)]
        n = 0
        for (t0, t1) in blocks:
            W = t1 - t0
            for c in range(8):
                s = sq[n % 2]
                n += 1
                k.act(s[:, :W], xT[:, c, t0:t1], AF.Square)
                k.mm(PS(0, 0, W), cst["onesr"], s[:, :W], start=(c == 0), stop=(c == 7))
            k.act(rstd[:, :W], PS(0, 0, W), AF.Sqrt, bias=eps_n)
            k.recip(rstd[:, :W], rstd[:, :W])
            for c in range(8):
                tmp = ntmp[c % 2]
                k.tt("dve", tmp[:, :W], xT[:, c, t0:t1], rstd[:, :W], ALU.mult)
                b_ = bias_fn(c, t0 < NCTX)
                k.act(out_fn(c, t0, t1), tmp[:, :W], AF.Identity, bias=(b_ if b_ is not None else zero_col), scale=scale_fn(c, t0 < NCTX))

    def who_idx(is_ctx, b):
        return 4 if is_ctx else b

    NMb_p = [[k.sb("NMb%d_%d" % (d_, i), [128, 2, 2, 128], F32) for i in range(2)] for d_ in range(2)]
    Ppb_p = [[k.sb("Ppb%d_%d" % (d_, i), [128, 2, 128], F32) for i in range(2)] for d_ in range(2)]
    Xsb_p = [k.sb("Xsb%d" % d_, [128, 2, 64], F32) for d_ in range(2)]
    RNG = [(256 * i, 256 * (i + 1)) for i in range(9)]
    RW = 256

    for b in range(nb):
        for c in range(8):
            k.dma("sp", xT[:, c, NCTX:T], xT_d[b, c * 128:(c + 1) * 128, :])
            k.dma("sp", xT[:, c, 0:NCTX], ctxT_d[b, c * 128:(c + 1) * 128, :])
        for l in range(nlayers):
            last = (l == DEPTH - 1)
            ph = Phase()
            norm(ph, lambda c, ic: A1[l][:, c, who_idx(ic, b):who_idx(ic, b) + 1],
                 lambda c, ic: ada[l][:, 0 + c, who_idx(ic, b):who_idx(ic, b) + 1],
                 lambda c, t0, t1: hT[:, c, t0:t1], BLK)
            ph.close()
            if ("hT_%d_%d" % (b, l)) in dbg:
                o_ = V(nc.dram_tensor("dbg_hT_%d_%d" % (b, l), [128, 8, T], BF16, kind="ExternalOutput").ap(), Reg())
                dbg_out["hT_%d_%d" % (b, l)] = o_
                k.dma("sp", o_, hT)
            ph = Phase()
            ubuf = [ph.sb([128, T]) for i in range(2)]
            lbuf = [ph.sb([128, T]) for i in range(1)]
            wg = [ph.sb([128, 8, 512], BF16) for i in range(2)]
            groups = [(0, 512), (512, 512), (1024, 512), (1536, 288)] + [(1824 + 512 * i, 512) for i in range(5)]
            ti = 0
            gi = 0
            for (c0, ncol) in groups:
                wgb = wg[gi % 2]
                gi += 1
                k.dma("pool", wgb[:, :, 0:ncol], WD["w_in"][l].rr("(c p) n -> p c n", p=128)[:, :, c0:c0 + ncol])
                if c0 == 1536 and l == 1:
                    k.dma("pool", wgb[:, :, 288:320], WD["w_vres_down"][0].rr("(c p) n -> p c n", p=128))
                if c0 == 1536:
                    tl = [(0, 128), (128, 128), (256, 64 if l == 1 else 32)]
                else:
                    tl = [(i * 128, 128) for i in range(4)]
                for (off, M) in tl:
                    is_rwkv = ti < 15
                    u = ubuf[ti % 2]
                    o = lbuf[0]
                    dst = u if is_rwkv else o
                    for bi, (t0, t1) in enumerate(BLK):
                        W = t1 - t0
                        pb = (ti * 5 + bi) % 8
                        for c in range(8):
                            k.mm(PS(pb, 0, W)[0:M], wgb[:, c, off:off + M], hT[:, c, t0:t1], start=(c == 0), stop=(c == 7))
                        k.cp("act" if (bi % 2 == 0) else "dve", dst[0:M, t0:t1], PS(pb, 0, W)[0:M])
                    if is_rwkv:
                        mub = SPL["mu"] + (l * 15 + ti) * 7
                        k.act(o[0:M, :], u[0:M, :], AF.Copy, scale=omm[0:M, l * 15 + ti:l * 15 + ti + 1])
                        cols = rwkv_tile_cols(l)[ti]
                        ldirs = sorted(set((c_[1] // 8) if isinstance(c_, tuple) else (c_ // 456) for c_ in cols))
                        cdirs = sorted(set((c_[1] // 16) if isinstance(c_, tuple) else (c_ // 912) for c_ in cols))
                        uL = u[0:M, NCTX:T].rr("p (r c) -> p r c", c=64)
                        oL = o[0:M, NCTX:T].rr("p (r c) -> p r c", c=64)
                        for dr in ldirs:
                            m = spt[0:M, mub + 1 + dr:mub + 2 + dr]
                            if dr == 0:
                                k.stt(oL[:, :, 1:64], uL[:, :, 0:63], m, oL[:, :, 1:64], ALU.mult, ALU.add)
                            elif dr == 1:
                                k.stt(oL[:, :, 0:63], uL[:, :, 1:64], m, oL[:, :, 0:63], ALU.mult, ALU.add)
                            elif dr == 2:
                                k.stt(o[0:M, NCTX + 64:T], u[0:M, NCTX:T - 64], m, o[0:M, NCTX + 64:T], ALU.mult, ALU.add)
                            else:
                                k.stt(o[0:M, NCTX:T - 64], u[0:M, NCTX + 64:T], m, o[0:M, NCTX:T - 64], ALU.mult, ALU.add)
                        for dr in cdirs:
                            m = spt[0:M, mub + 5 + dr:mub + 6 + dr]
                            if dr == 0:
                                k.stt(o[0:M, 1:NCTX], u[0:M, 0:NCTX - 1], m, o[0:M, 1:NCTX], ALU.mult, ALU.add)
                            else:
                                k.stt(o[0:M, 0:NCTX - 1], u[0:M, 1:NCTX], m, o[0:M, 0:NCTX - 1], ALU.mult, ALU.add)
                        if ti == 12:
                            k.act(o[0:64, :], o[0:64, :], AF.Tanh)
                        elif ti == 13:
                            k.act(o[0:128, :], o[0:128, :], AF.Sigmoid)
                        elif ti == 14:
                            k.act(o[0:32, :], o[0:32, :], AF.Sigmoid)
                    k.dma("sp", pbuf[ti][0:M, :], o[0:M, :])
                    if l == 0 and 8 <= ti < 12:
                        k.dma("sp", vfd[ti - 8], o[0:M, :])
                    ti += 1
            assert ti == NPT
            ph.close()
            if ("pbuf_%d_%d" % (b, l)) in dbg:
                o_ = V(nc.dram_tensor("dbg_pbuf_%d_%d" % (b, l), [NPT, 128, T], F32, kind="ExternalOutput").ap(), Reg())
                dbg_out["pbuf_%d_%d" % (b, l)] = o_
                for i_ in range(NPT):
                    k.dma("sp", o_[i_], pbuf[i_])
            if "stop_p2" in dbg:
                break
            ph = Phase()
            G0s = ph.sb([128, RW], BF16)
            G1s = ph.sb([64, RW], BF16)
            ybuf = ph.sb([128, T])
            oT = hT
            W = RW
            nch = 2

            class St:
                pass

            def alloc_stream(d):
                B = St()
                B.d = d
                B.fb = [ph.sb([128, RW]) for i in range(12)]
                B.hb = [ph.sb([128, RW], BF16) for i in range(6)]
                B.WAs = ph.sb([128, RW], BF16)
                B.ARb = ph.sb([128, 2, 2, 128], BF16)
                B.Btok = ph.sb([128, 2, 128], BF16)
                B.Ktok = ph.sb([128, 2, 128], BF16)
                B.Vtok = ph.sb([128, 2, 128], BF16)
                B.gC = ph.sb([128, 2])
                B.Am = ph.sb([128, 2, 2, 256], BF16)
                B.NMb = NMb_p[d]
                B.Ppb = Ppb_p[d]
                B.Xsb = Xsb_p[d]
                B.Usb = ph.sb([128, 2, 64], BF16)
                B.Sst = ph.sb([128, 64])
                B.Sbf = ph.sb([128, 64], BF16)
                B.scm = ph.sb([128, 2, 128], BF16)
                B.dTt = ph.sb([128, 256])
                B.qdt = ph.sb([128, 128])
                B.kdt = ph.sb([128, 128])
                B.off = 4 * d
                return B
            SB = [alloc_stream(0), alloc_stream(1)]

            def v3(x):
                return x.rr("p (c t) -> p c t", t=128)

            def run_streams(gens):
                gens = list(gens)
                while gens:
                    for g in list(gens):
                        try:
                            next(g)
                        except StopIteration:
                            gens.remove(g)

            def gn_block(src, wcol, bcol, eps_col, out_f, pa, pb_, gnb):
                cen, sqv, rs = gnb
                k.mm(PS(pa, 0, W), cst["bo"], src, True, True)
                k.stt(cen, PS(pa, 0, W), -1.0 / 64, src, ALU.mult, ALU.add)
                k.act(sqv, cen, AF.Square)
                k.mm(PS(pb_, 0, W), cst["bo"], sqv, True, True)
                k.act(rs, PS(pb_, 0, W), AF.Sqrt, bias=eps_col, scale=1.0 / 64)
                k.recip(rs, rs)
                k.tt("dve", cen, cen, rs, ALU.mult)
                k.act(out_f, cen, AF.Identity, bias=bcol, scale=wcol)

            def transposes(src_bf, dst_tok, bank, mul_tab=None):
                pt = PS(bank).bitcast(BF16)
                for c in range(nch):
                    k.tr(pt[:, c * 128:(c + 1) * 128], src_bf[:, c * 128:(c + 1) * 128], cst["ident"])
                ptv = pt[:, 0:nch * 128].rr("p (c f) -> p c f", f=128)
                if mul_tab is None:
                    k.cp("act", dst_tok, ptv)
                else:
                    k.tt("dve", dst_tok, ptv, mul_tab.rr("p (o f) -> p o f", o=1).bc([128, nch, 128]), ALU.mult)

            def rwkv_pass(hp, B):
                d = B.d
                hc = slice(hp * 128, (hp + 1) * 128)
                fb, hb = B.fb, B.hb
                ARb, Btok, Ktok, Vtok, gC, Am, NMb, Ppb = B.ARb, B.Btok, B.Ktok, B.Vtok, B.gC, B.Am, B.NMb, B.Ppb
                Xsb, Usb, Sst, Sbf, WAs = B.Xsb, B.Usb, B.Sst, B.Sbf, B.WAs

                def P(role, lo=0, hi=512):
                    return PS((role + B.off) % 8, lo, hi)

                def P2(role, lo, hi):
                    return PS2((role + B.off) % 8, lo, hi)
                order = list(range(9)) if d == 0 else [0] + list(range(8, 0, -1))
                k.memset("dve", Sst, 0.0)
                k.memset("dve", Sbf, 0.0)
                for ri in order:
                    t0, t1 = RNG[ri]
                    rF, kF, vF = fb[0], fb[1], fb[2]
                    k.dma("sp", rF, pbuf[0 + hp][:, t0:t1])
                    k.dma("sp", kF, pbuf[4 + hp][:, t0:t1])
                    k.dma("sp", vF, pbuf[8 + hp][:, t0:t1])
                    k.dma("pool", WAs, pbuf[12][:, t0:t1])
                    sw, Lr, Lin, Lex, icl, kk, t1b, t2b, Eb = fb[3], fb[4], fb[5], fb[6], fb[7], fb[8], fb[9], fb[10], fb[11]
                    k.mm(P(0, 0, W), WAup[0:64, l, d, hc], WAs[0:64, :])
                    k.mm(P(1, 0, W), WAup[64:128, l, d, hc], WAs[64:128, :])
                    yield
                    k.act(sw, P(0, 0, W), AF.Sigmoid, bias=SPc("w0", (l * 2 + d) * 4 + hp))
                    k.act(icl, P(1, 0, W), AF.Sigmoid, bias=SPc("a0", (l * 2 + d) * 4 + hp))
                    k.ts("dve", kk, kF, SPc("kk", l * 4 + hp), None, ALU.mult)
                    k.act(t1b, kk, AF.Square)
                    k.mm(P(0, 0, W), cst["bo"], t1b)
                    yield
                    k.scan(Lr, cst["rmask"][:, :W], sw, 0.0, ALU.mult, ALU.add)
                    totb = v3(Lr)[:, :, 127:128].bc([128, nch, 128])
                    if d == 0:
                        k.cp("dve", Lin, Lr)
                    else:
                        k.tt("dve", t2b, sw, Lr, ALU.subtract)
                        k.tt("dve", v3(Lin), v3(t2b), totb, ALU.add)
                    k.tt("dve", Lex, Lin, sw, ALU.subtract)
                    k.act(gC, v3(Lr)[:, :, 127], AF.Exp, scale=-CDEC)
                    k.ts("dve", t2b, P(0, 0, W), 1e-24, None, ALU.max)
                    k.act(t2b, t2b, AF.Sqrt)
                    yield
                    k.recip(t2b, t2b)
                    k.tt("dve", kk, kk, t2b, ALU.mult)
                    k.act(Eb, Lex, AF.Exp, scale=-CDEC)
                    yield
                    k.stt(ARb[:, :, 0, :], v3(kk), -1.0, v3(Eb), ALU.mult, ALU.mult)
                    k.act(Eb, Lin, AF.Exp, scale=-CDEC)
                    yield
                    k.tt("dve", ARb[:, :, 1, :], v3(rF), v3(Eb), ALU.mult)
                    ktil, bp = t1b, t2b
                    k.ts("dve", ktil, icl, SPc("ka", l * 4 + hp), omka[:, l * 4 + hp:l * 4 + hp + 1], ALU.mult, ALU.add)
                    k.tt("dve", ktil, ktil, kF, ALU.mult)
                    k.tt("dve", bp, kk, icl, ALU.mult)
                    BH, KH, bck, kck, vbf = hb[0], hb[1], hb[2], hb[3], hb[4]
                    k.act(Eb, Lin, AF.Exp, scale=CDEC)
                    yield
                    k.tt("dve", BH, bp, Eb, ALU.mult)
                    k.tt("dve", KH, ktil, Eb, ALU.mult)
                    k.tt("dve", v3(Lex), v3(Lin), totb, ALU.subtract)
                    k.act(Eb, Lex, AF.Exp, scale=CDEC)
                    yield
                    k.tt("dve", bck, bp, Eb, ALU.mult)
                    k.tt("dve", kck, ktil, Eb, ALU.mult)
                    k.cp("act", vbf, vF)
                    yield
                    transposes(bck, Btok, (4 + B.off) % 8)
                    yield
                    transposes(kck, Ktok, (5 + B.off) % 8)
                    yield
                    transposes(vbf, Vtok, (4 + B.off) % 8)
                    yield
                    corder = list(range(nch)) if d == 0 else list(range(nch - 1, -1, -1))
                    for c in corder:
                        cs = slice(c * 128, (c + 1) * 128)
                        for e in range(2):
                            Re = slice(64 * e, 64 * e + 64)
                            arv = ARb[Re, c, :, :].rr("p a t -> p (a t)")
                            k.mm(P(e, 0, 256), BH[Re, cs], arv)
                            k.mm(P(e, 256, 512), KH[Re, cs], arv)
                            k.mm(P(2 + e, 0, 128), ARb[Re, c, 0, :], BH[Re, cs])
                        yield
                        nm0 = NMb[0]
                        k.tt("dve", nm0[:, :, 0, :], P2(0, 0, 128), cst["mask2"][:, d, 0:128].rr("p (o t) -> p o t", o=1).bc([128, 2, 128]), ALU.mult)
                        k.tt("dve", nm0[:, :, 1, :], P2(2, 0, 128), cst["mmask"][:, d, :].rr("p (o t) -> p o t", o=1).bc([128, 2, 128]), ALU.mult)
                        k.tt("dve", Ppb[0], nm0[:, :, 0, :], cst["ident"].rr("p (o t) -> p o t", o=1).bc([128, 2, 128]), ALU.add)
                        for e in range(2):
                            k.tt("dve", Am[:, e, :, :], P(e).rr("p (a t) -> p a t", a=2),
                                 cst["mask2"][:, d, :].rr("p (o t) -> p o t", o=1).bc([128, 2, 256]), ALU.mult)
                        yield
                        cur = 0
                        pcur = 0
                        for itn in range(7):
                            nmc, nmn = NMb[cur], NMb[1 - cur]
                            pc, pn = Ppb[pcur], Ppb[1 - pcur]
                            def rc(x):
                                return x.bitcast(F32R) if DBL_R else x
                            for e in range(2):
                                if itn < 5:
                                    k.mm(P(4, e * 256, e * 256 + 128), rc(nmc[:, e, 1, :]), rc(nmc[:, e, 0, :]))
                                if itn < 6:
                                    k.mm(P(4, e * 256 + 128, e * 256 + 256), rc(nmc[:, e, 0, :]), rc(nmc[:, e, 1, :]))
                                if itn >= 1:
                                    k.mm(P(5, e * 128, e * 128 + 128), rc(nmc[:, e, 1, :]), rc(pc[:, e, :]))
                            yield
                            if itn < 5:
                                k.cp("act", nmn.rr("p e a t -> p (e a t)"), P(4))
                            elif itn == 5:
                                k.cp("act", nmn[:, :, 1, :], P(4).rr("p (e a t) -> p e a t", e=2, a=2)[:, :, 1, :])
                            if itn >= 1:
                                k.tt("dve", pn, P(5, 0, 256).rr("p (e t) -> p e t", e=2), pc, ALU.add)
                                pcur = 1 - pcur
                            yield
                            cur = 1 - cur
                        cur = pcur
                        TT = Ppb[cur]
                        for e in range(2):
                            Re = slice(64 * e, 64 * e + 64)
                            k.mm(P(2 + e, 128, 192), ARb[Re, c, 0, :], Sbf[Re, :], True, False)
                            k.mm(P(2 + e, 128, 192), Am[:, e, 1, 0:128], Vtok[:, c, Re], False, True)
                        yield
                        k.cp("act", Xsb, P2(2, 128, 192))
                        yield
                        for e in range(2):
                            k.mm(P(2 + e, 192, 256), TT[:, e, :], Xsb[:, e, :])
                        yield
                        k.cp("act", Usb, P2(2, 192, 256))
                        yield
                        for e in range(2):
                            Re = slice(64 * e, 64 * e + 64)
                            po = P(6 + e, 0, 128)[Re]
                            k.mm(po, Sbf[Re, :], ARb[Re, c, 1, :], True, False)
                            k.mm(po, Usb[:, e, :], Am[:, e, 0, 128:256], False, False)
                            k.mm(po, Vtok[:, c, Re], Am[:, e, 1, 128:256], False, True)
                        for e in range(2):
                            Ce = slice(64 * e, 64 * e + 64)
                            pd = P(5, 256, 320)[Ce]
                            k.mm(pd, Btok[:, c, Ce], Usb[:, e, :], True, False)
                            k.mm(pd, Ktok[:, c, Ce], Vtok[:, c, Ce], False, True)
                        yield
                        for e in range(2):
                            Re = slice(64 * e, 64 * e + 64)
                            po = P(6 + e, 0, 128)[Re]
                            yv = ybuf[Re, t0 + c * 128:t0 + (c + 1) * 128]
                            k.tt("dve", yv, yv, po, ALU.add)
                        k.stt(Sst, Sst, gC[:, c:c + 1], P(5, 256, 320), ALU.mult, ALU.add)
                        k.cp("act", Sbf, Sst)
                        yield

            def ret_pass(hp, B):
                d = B.d
                fb, hb = B.fb, B.hb
                Ktok, Vtok, Sst, Sbf, scm, dTt, qdt, kdt = B.Ktok, B.Vtok, B.Sst, B.Sbf, B.scm, B.dTt, B.qdt, B.kdt

                def P(role, lo=0, hi=512):
                    return PS((role + B.off) % 8, lo, hi)

                def P2(role, lo, hi):
                    return PS2((role + B.off) % 8, lo, hi)
                order = list(range(9)) if d == 0 else [0] + list(range(8, 0, -1))
                k.dma("sp", dTt, CD["dT"][d * 4 + hp])
                k.dma("sp", qdt, CD["qd"][d * 4 + hp])
                k.dma("sp", kdt, CD["kd"][d * 4 + hp])
                k.memset("dve", Sst, 0.0)
                k.memset("dve", Sbf, 0.0)
                for ri in order:
                    t0, t1 = RNG[ri]
                    qF, kF, vF, gF, cosF, sinF = fb[0], fb[1], fb[2], fb[3], fb[4], fb[5]
                    k.dma("sp", qF, pbuf[15 + hp][:, t0:t1])
                    k.dma("sp", kF, pbuf[19 + hp][:, t0:t1])
                    k.dma("sp", vF, pbuf[23 + hp][:, t0:t1])
                    k.dma("sp", gF, pbuf[(27 if d == 0 else 31) + hp][:, t0:t1])
                    k.dma("sp", cosF, CD["cosT"][:, t0:t1])
                    k.dma("sp", sinF, CD["ssinT"][:, t0:t1])
                    qb, kb, qr, kr, qh, vbf = hb[0], hb[1], hb[2], hb[3], hb[4], hb[5]
                    t1b, t2b = fb[6], fb[7]
                    for (src, sb_, dstb, isq, bank) in ((qF, qb, qr, True, 0), (kF, kb, kr, False, 1)):
                        k.cp("act", sb_, src)
                        yield
                        k.mm(P(bank, 0, W), cst["pm"], sb_)
                        k.tt("dve", t1b, src, cosF, ALU.mult)
                        yield
                        k.tt("dve", t2b, P(bank, 0, W), sinF, ALU.mult)
                        k.tt("dve", t1b, t1b, t2b, ALU.add)
                        if isq:
                            k.cp("act", dstb, t1b)
                            k.tt("dve", v3(qh), v3(t1b), qdt.rr("p (o t) -> p o t", o=1).bc([128, nch, 128]), ALU.mult)
                        else:
                            k.act(dstb, t1b, AF.Copy, scale=0.125)
                        yield
                    k.cp("act", vbf, vF)
                    yield
                    transposes(kr, Ktok, (4 + B.off) % 8, mul_tab=kdt)
                    yield
                    transposes(vbf, Vtok, (5 + B.off) % 8)
                    yield
                    orng = fb[8]
                    corder = list(range(nch)) if d == 0 else list(range(nch - 1, -1, -1))
                    for c in corder:
                        cs = slice(c * 128, (c + 1) * 128)
                        for e in range(2):
                            Re = slice(64 * e, 64 * e + 64)
                            k.mm(P(2 + e, 0, 128), kr[Re, cs], qr[Re, cs])
                        yield
                        k.tt("dve", scm, P2(2, 0, 128), dTt.rr("p (e t) -> p e t", e=2), ALU.mult)
                        yield
                        for e in range(2):
                            Re = slice(64 * e, 64 * e + 64)
                            po = P(6 + e, 0, 128)[Re]
                            k.mm(po, Sbf[Re, :], qh[Re, cs], True, False)
                            k.mm(po, Vtok[:, c, Re], scm[:, e, :], False, True)
                        for e in range(2):
                            Ce = slice(64 * e, 64 * e + 64)
                            pd = P(5, 256, 320)[Ce]
                            k.mm(pd, Ktok[:, c, Ce], Vtok[:, c, Ce], True, True)
                        yield
                        for e in range(2):
                            Re = slice(64 * e, 64 * e + 64)
                            k.cp("act", orng[Re, cs], P(6 + e, 0, 128)[Re])
                        k.stt(Sst, Sst, cst["cd"][:, d * 4 + hp:d * 4 + hp + 1], P(5, 256, 320), ALU.mult, ALU.add)
                        k.cp("act", Sbf, Sst)
                        yield
                    gnv = fb[9]
                    gn_block(orng, SPc("rgw", l * 4 + hp), SPc("rgb", l * 4 + hp), eps_t, gnv, (0 + B.off) % 8, (1 + B.off) % 8, (fb[10], fb[11], fb[6]))
                    k.act(gF, gF, AF.Silu)
                    k.tt("dve", gnv, gnv, gF, ALU.mult)
                    k.tt("dve", ybuf[:, t0:t1], ybuf[:, t0:t1], gnv, ALU.add)
                    yield

            for hp in range(4 if "skip_rwkv" not in dbg else 0):
                hc = slice(hp * 128, (hp + 1) * 128)
                fb = SB[0].fb
                if l == 1:
                    for (t0, t1) in RNG:
                        vF, vfF, sg = fb[0], fb[1], fb[2]
                        k.dma("sp", vF, pbuf[8 + hp][:, t0:t1])
                        k.dma("sp", vfF, vfd[hp][:, t0:t1])
                        k.dma("pool", G1s, pbuf[14][0:64, t0:t1])
                        k.mm(PS(0, 0, W), GVup[32:64, 1, hc], G1s[32:64, :])
                        k.act(sg, PS(0, 0, W), AF.Sigmoid, bias=SPc("v0", hp))
                        k.tt("dve", vfF, vfF, vF, ALU.subtract)
                        k.tt("dve", vfF, vfF, sg, ALU.mult)
                        k.tt("dve", vF, vF, vfF, ALU.add)
                        k.dma("sp", pbuf[8 + hp][:, t0:t1], vF)
                k.memset("dve", ybuf, 0.0)
                run_streams([rwkv_pass(hp, SB[0]), rwkv_pass(hp, SB[1])])
                for (t0, t1) in RNG:
                    rF, kF, vF = fb[0], fb[1], fb[2]
                    k.dma("sp", rF, pbuf[0 + hp][:, t0:t1])
                    k.dma("sp", kF, pbuf[4 + hp][:, t0:t1])
                    k.dma("sp", vF, pbuf[8 + hp][:, t0:t1])
                    k.dma("pool", G0s, pbuf[13][:, t0:t1])
                    k.dma("pool", G1s, pbuf[14][0:64, t0:t1])
                    k.stt(rF, rF, SPc("rk", l * 4 + hp), kF, ALU.mult, ALU.mult)
                    k.mm(PS(2, 0, W), cst["bo"], rF)
                    k.tt("dve", vF, vF, PS(2, 0, W), ALU.mult)
                    gnv = fb[3]
                    gn_block(ybuf[:, t0:t1], SPc("lnw", l * 4 + hp), SPc("lnb", l * 4 + hp), eps_r, gnv, 0, 1, (fb[9], fb[10], fb[11]))
                    k.tt("dve", gnv, gnv, vF, ALU.add)
                    k.mm(PS(3, 0, W), G0up[:, l, hc], G0s, True, False)
                    k.mm(PS(3, 0, W), GVup[0:32, l, hc], G1s[0:32, :], False, True)
                    k.tt("dve", oT[:, hp, t0:t1], gnv, PS(3, 0, W), ALU.mult)
            for hp in range(4 if "skip_ret" not in dbg else 0):
                k.memset("dve", ybuf, 0.0)
                run_streams([ret_pass(hp, SB[0]), ret_pass(hp, SB[1])])
                k.cp("act", oT[:, 4 + hp, :], ybuf)
            ph.close()
            if ("oT_%d_%d" % (b, l)) in dbg:
                o_ = V(nc.dram_tensor("dbg_oT_%d_%d" % (b, l), [128, 8, T], BF16, kind="ExternalOutput").ap(), Reg())
                dbg_out["oT_%d_%d" % (b, l)] = o_
                k.dma("sp", o_, oT)
            if "stop_p3" in dbg:
                break
            ph = Phase()
            wob = ph.sb([128, 8, 1024], BF16)
            k.dma("pool", wob, WD["w_out"][l].rr("(c p) n -> p c n", p=128))
            n = 0
            for m in range(8):
                for (t0, t1) in BLK:
                    W = t1 - t0
                    pb = n % 8
                    n += 1
                    for kc in range(8):
                        k.mm(PS(pb, 0, W), wob[:, kc, m * 128:(m + 1) * 128], oT[:, kc, t0:t1], start=(kc == 0), stop=(kc == 7))
                    wi = who_idx(t0 < NCTX, b)
                    k.stt(xT[:, m, t0:t1], PS(pb, 0, W), ada[l][:, 16 + m, wi:wi + 1], xT[:, m, t0:t1], ALU.mult, ALU.add)
            ph.close()
            ph = Phase()
            norm(ph, lambda c, ic: A2[l][:, c, who_idx(ic, b):who_idx(ic, b) + 1],
                 lambda c, ic: ada[l][:, 24 + c, who_idx(ic, b):who_idx(ic, b) + 1],
                 lambda c, t0, t1: hT[:, c, t0:t1], BLK)
            ph.close()
            ph = Phase()
            wgf = [ph.sb([128, 8, 512], BF16) for i in range(2)]
            actb = ph.sb([128, NJ, 512], BF16)
            wfo = [ph.sb([128, NJ, 128], BF16) for i in range(2)]
            sgb = [ph.sb([128, 512]) for i in range(2)]
            wfi_v = WD["w_ffn_in"][l].rr("(c p) n -> p c n", p=128)
            wfo_v = WD["w_ffn_out"][l].rr("(j p) n -> p j n", p=128)
            wi_ = 0
            for (t0, t1) in BLK:
                W = t1 - t0
                if (last and t0 < NCTX) or "skip_ffn" in dbg:
                    continue
                for jg in range(6):
                    nj = 4 if jg < 5 else 2
                    wgate = wgf[0]
                    wup = wgf[1]
                    k.dma("pool", wgate[:, :, 0:nj * 128], wfi_v[:, :, jg * 512:jg * 512 + nj * 128])
                    k.dma("pool", wup[:, :, 0:nj * 128], wfi_v[:, :, HID + jg * 512:HID + jg * 512 + nj * 128])
                    for jj in range(nj):
                        j = jg * 4 + jj
                        pg = (2 * j) % 8
                        pu = (2 * j + 1) % 8
                        for c in range(8):
                            k.mm(PS(pg, 0, W), wgate[:, c, jj * 128:(jj + 1) * 128], hT[:, c, t0:t1], start=(c == 0), stop=(c == 7))
                        for c in range(8):
                            k.mm(PS(pu, 0, W), wup[:, c, jj * 128:(jj + 1) * 128], hT[:, c, t0:t1], start=(c == 0), stop=(c == 7))
                        sgt = sgb[j % 2]
                        k.act(sgt[:, :W], PS(pg, 0, W), AF.Silu)
                        k.tt("dve", actb[:, j, :W], sgt[:, :W], PS(pu, 0, W), ALU.mult)
                for m in range(8):
                    wf = wfo[wi_ % 2]
                    wi_ += 1
                    k.dma("pool", wf, wfo_v[:, :, m * 128:(m + 1) * 128])
                    pb = m % 8
                    for j in range(NJ):
                        k.mm(PS(pb, 0, W), wf[:, j, :], actb[:, j, :W], start=(j == 0), stop=(j == NJ - 1))
                    wi = who_idx(t0 < NCTX, b)
                    k.stt(xT[:, m, t0:t1], PS(pb, 0, W), ada[l][:, 40 + m, wi:wi + 1], xT[:, m, t0:t1], ALU.mult, ALU.add)
            ph.close()
            if ("xT_%d_%d" % (b, l)) in dbg:
                o_ = V(nc.dram_tensor("dbg_xT_%d_%d" % (b, l), [128, 8, T], F32, kind="ExternalOutput").ap(), Reg())
                dbg_out["xT_%d_%d" % (b, l)] = o_
                k.dma("sp", o_, xT)
        if "stop_p2" in dbg or "stop_p3" in dbg:
            continue
        ph = Phase()
        obuf = [ph.sb([128, T]) for i in range(2)]
        rst_all = ph.sb([128, T])
        sq = [ph.sb([128, 512]) for i in range(2)]
        n = 0
        for (t0, t1) in BLK[1:]:
            W = t1 - t0
            for c in range(8):
                s = sq[n % 2]
                n += 1
                k.act(s[:, :W], xT[:, c, t0:t1], AF.Square)
                k.mm(PS(0, 0, W), cst["onesr"], s[:, :W], start=(c == 0), stop=(c == 7))
            k.act(rst_all[:, t0:t1], PS(0, 0, W), AF.Sqrt, bias=eps_n)
            k.recip(rst_all[:, t0:t1], rst_all[:, t0:t1])
        for c in range(8):
            ob = obuf[c % 2]
            k.tt("dve", ob[:, NCTX:T], xT[:, c, NCTX:T], rst_all[:, NCTX:T], ALU.mult)
            k.act(ob[:, NCTX:T], ob[:, NCTX:T], AF.Copy, scale=SPc("normf", c))
            k.dma("sp", outT_d[b, c * 128:(c + 1) * 128, :], ob[:, NCTX:T])
        ph.close()
    k.wait_all("sp", [outT_d.g[0]] + [v.g[0] for v in dbg_out.values()])
    es.close()
    return nc, k, dbg_out


_CACHE = {}


def kernel(**inp):
    inp = {k_: np.asarray(v) for k_, v in inp.items()}
    ncores = 8
    B = inp["x"].shape[0]
    nb = B // ncores
    if "nc" not in _CACHE:
        _CACHE["nc"] = build(nb)[0]
    nc = _CACHE["nc"]
    xT = np.ascontiguousarray(np.transpose(inp["x"], (0, 2, 1)))
    ctxT = np.ascontiguousarray(np.transpose(inp["ctx"], (0, 2, 1)))
    sp = build_sp(inp)
    consts = build_consts()
    in_maps = []
    for i in range(ncores):
        m = {"xT": xT[i * nb:(i + 1) * nb], "ctxT": ctxT[i * nb:(i + 1) * nb]}
        cT = np.zeros((D, 5), np.float32)
        cT[:, :nb] = inp["c"][i * nb:(i + 1) * nb].T
        cT[:, 4] = inp["c_ctx"]
        m["cT"] = cT
        m["sp"] = sp
        for n_ in CONST_SHAPES:
            m["c_" + n_] = consts[n_]
        for n_ in WEIGHT_SHAPES:
            m[n_] = inp[n_]
        in_maps.append(m)
    res = run_bass_kernel_spmd(nc, in_maps, core_ids=list(range(ncores)))
    outT = np.concatenate([np.asarray(r["outT"]) for r in res.results], axis=0)
    return np.ascontiguousarray(np.transpose(outT, (0, 2, 1))).astype(np.float32)
```

```python
import numpy as np
import ml_dtypes
from contextlib import ExitStack
import concourse.bass as bass
import concourse.mybir as mybir
from concourse.bass_utils import run_bass_kernel_spmd

F32 = mybir.dt.float32
BF16 = mybir.dt.bfloat16
F32R = mybir.dt.float32r
DBL_R = False
ALU = mybir.AluOpType
AF = mybir.ActivationFunctionType

D = 1024
T = 2304
NCTX = 256
SEQ = 2048
DEPTH = 2
NIN = 4384
HID = 2816
NJ = HID // 128
CDEC = 0.6065306597126334
BLK = [(0, 256), (256, 768), (768, 1280), (1280, 1792), (1792, 2304)]
NPT = 35
SAME = True
NDS = 24

SPL = {}
_o = 0
for _n, _w in [("norm1", 16), ("norm2", 16), ("normf", 8), ("bada", 96), ("mu", 2 * 15 * 7), ("w0", 16), ("a0", 16),
               ("kk", 8), ("ka", 8), ("rk", 8), ("v0", 4), ("lnw", 8), ("lnb", 8), ("rgw", 8), ("rgb", 8)]:
    SPL[_n] = _o
    _o += _w
NSP = _o


def rwkv_tile_cols(l):
    tiles = []
    for i in range(12):
        tiles.append([i * 128 + p for p in range(128)])
    tiles.append([1536 + p for p in range(128)])
    tiles.append([1664 + p for p in range(128)])
    g1 = [1792 + p for p in range(32)]
    if l == 1:
        g1 += [("v", ch) for ch in range(32)]
    tiles.append(g1)
    return tiles


def build_sp(inp):
    sp = np.zeros((128, NSP), np.float32)
    p = np.arange(128)
    for l in range(2):
        for c in range(8):
            sp[:, SPL["norm1"] + l * 8 + c] = inp["norm1"][l, c * 128 + p]
            sp[:, SPL["norm2"] + l * 8 + c] = inp["norm2"][l, c * 128 + p]
        for q in range(48):
            sp[:, SPL["bada"] + l * 48 + q] = inp["b_ada"][l, q * 128 + p]
        tiles = rwkv_tile_cols(l)
        for ti, cols in enumerate(tiles):
            base = SPL["mu"] + (l * 15 + ti) * 7
            for pp, col in enumerate(cols):
                if isinstance(col, tuple):
                    ch = col[1]
                    m = inp["mu_vres"][0, ch]
                    ld, cd = ch // 8, ch // 16
                else:
                    m = inp["mu_rwkv"][l, col]
                    ld, cd = col // 456, col // 912
                sp[pp, base + 0] = m
                sp[pp, base + 1 + ld] = m
                sp[pp, base + 5 + cd] = m
        for d in range(2):
            for hp in range(4):
                sp[:, SPL["w0"] + (l * 2 + d) * 4 + hp] = inp["w0"][l, d, hp * 128 + p]
                sp[:, SPL["a0"] + (l * 2 + d) * 4 + hp] = inp["a0"][l, d, hp * 128 + p]
        for hp in range(4):
            sp[:, SPL["kk"] + l * 4 + hp] = inp["k_k"][l, hp * 128 + p]
            sp[:, SPL["ka"] + l * 4 + hp] = inp["k_a"][l, hp * 128 + p]
            sp[:, SPL["rk"] + l * 4 + hp] = inp["r_k"][l].reshape(512)[hp * 128 + p]
            sp[:, SPL["lnw"] + l * 4 + hp] = inp["ln_x_w"][l, hp * 128 + p]
            sp[:, SPL["lnb"] + l * 4 + hp] = inp["ln_x_b"][l, hp * 128 + p]
            sp[:, SPL["rgw"] + l * 4 + hp] = inp["ret_gn_w"][l, hp * 128 + p]
            sp[:, SPL["rgb"] + l * 4 + hp] = inp["ret_gn_b"][l, hp * 128 + p]
    for c in range(8):
        sp[:, SPL["normf"] + c] = inp["norm_f"][c * 128 + p]
    for hp in range(4):
        sp[:, SPL["v0"] + hp] = inp["v0"][0, hp * 128 + p]
    return sp


def build_consts():
    c = {}
    s = np.arange(128)[:, None]
    t = np.arange(128)[None, :]
    c["ident"] = np.eye(128, dtype=np.float32)
    c["onesr"] = np.full((128, 128), 1.0 / 1024, np.float32)
    bo = np.zeros((128, 128), np.float32)
    bo[:64, :64] = 1
    bo[64:, 64:] = 1
    c["bo"] = bo
    pm = np.zeros((128, 128), np.float32)
    for m in range(128):
        n = m % 64
        pm[(m - n) + ((n + 32) % 64), m] = 1
    c["pm"] = pm
    m2 = np.zeros((2, 128, 256), np.float32)
    m2[0, :, :128] = s < t
    m2[0, :, 128:] = s <= t
    m2[1, :, :128] = s > t
    m2[1, :, 128:] = s >= t
    c["mask2"] = m2
    mm_ = np.zeros((2, 128, 128), np.float32)
    mm_[0] = s > t
    mm_[1] = s < t
    c["mmask"] = mm_
    rm = np.ones((128, 512), np.float32)
    rm[:, ::128] = 0
    c["rmask"] = rm
    tok = np.arange(SEQ)
    row = (tok // 64).astype(np.float32)
    col = (tok % 64).astype(np.float32)
    nf = 16
    freqs = (np.float32(10000.0) ** (-np.arange(nf, dtype=np.float32) / nf)).astype(np.float32)
    ang = np.concatenate([row[:, None] * freqs, col[:, None] * freqs], -1).astype(np.float32)
    cosT = np.ones((128, T), np.float32)
    ssinT = np.zeros((128, T), np.float32)
    for p_ in range(128):
        n = p_ % 64
        cosT[p_, NCTX:] = np.cos(ang[:, n % 32])
        ssinT[p_, NCTX:] = np.sin(ang[:, n % 32]) * (-1.0 if n < 32 else 1.0)
    c["cosT"] = cosT
    c["ssinT"] = ssinT
    lg = np.log(1.0 - 2.0 ** (-5.0 - np.arange(8, dtype=np.float64)))
    dt_ = np.zeros((2, 4, 128, 2, 128), np.float32)
    qd = np.zeros((2, 4, 128, 128), np.float32)
    kd = np.zeros((2, 4, 128, 128), np.float32)
    cd = np.zeros((128, 8), np.float32)
    sv = np.arange(128)
    for d in range(2):
        for hp in range(4):
            for e in range(2):
                h = 2 * hp + e
                g = lg[h] if d == 0 else lg[7 - h]
                if d == 0:
                    dt_[d, hp, :, e, :] = np.where(t >= s, np.exp(g * np.maximum(t - s, 0)), 0)
                    qd[d, hp, 64 * e:64 * e + 64, :] = np.exp(g * (sv + 1.0))[None, :]
                    kd[d, hp, :, 64 * e:64 * e + 64] = np.exp(g * (127.0 - sv))[:, None]
                else:
                    dt_[d, hp, :, e, :] = np.where(s >= t, np.exp(g * np.maximum(s - t, 0)), 0)
                    qd[d, hp, 64 * e:64 * e + 64, :] = np.exp(g * (128.0 - sv))[None, :]
                    kd[d, hp, :, 64 * e:64 * e + 64] = np.exp(g * sv)[:, None]
                cd[64 * e:64 * e + 64, d * 4 + hp] = np.exp(g * 128.0)
    c["dT"] = dt_.reshape(8, 128, 256)
    c["qd"] = qd.reshape(8, 128, 128)
    c["kd"] = kd.reshape(8, 128, 128)
    c["cd"] = cd
    return c


CONST_SHAPES = {"ident": [128, 128], "onesr": [128, 128], "bo": [128, 128], "pm": [128, 128], "mask2": [2, 128, 256],
                "mmask": [2, 128, 128], "rmask": [128, 512], "cosT": [128, T], "ssinT": [128, T],
                "dT": [8, 128, 256], "qd": [8, 128, 128], "kd": [8, 128, 128], "cd": [128, 8]}
WEIGHT_SHAPES = {"w_ada": [2, 1024, 6144], "w_in": [2, 1024, NIN], "w_vres_down": [1, 1024, 32],
                 "w_up": [2, 2, 64, 512], "a_up": [2, 2, 64, 512], "g_up": [2, 160, 512], "v_up": [1, 32, 512],
                 "w_out": [2, 1024, 1024], "w_ffn_in": [2, 1024, 2 * HID], "w_ffn_out": [2, HID, 1024]}


class Reg:
    __slots__ = ("w", "r", "excl")

    def __init__(self, excl=False):
        self.w = None
        self.r = {}
        self.excl = excl


class V:
    def __init__(self, ap, g):
        self.ap = ap
        self.g = g if isinstance(g, list) else [g]

    def __getitem__(self, idx):
        return V(self.ap[idx], self.g)

    def rr(self, pat, **kw):
        return V(self.ap.rearrange(pat, **kw), self.g)

    def bc(self, shape):
        return V(self.ap.to_broadcast(shape), self.g)

    def bitcast(self, dt):
        return V(self.ap.bitcast(dt), self.g)


class KB:
    def __init__(self, nc, es):
        self.nc = nc
        self.es = es
        self.eng = {"pe": nc.tensor, "dve": nc.vector, "act": nc.scalar, "pool": nc.gpsimd, "sp": nc.sync}
        self.sem = {e: es.enter_context(nc.semaphore("s_" + e)) for e in self.eng}
        self.cnt = {e: 0 for e in self.eng}
        self.seen = {e: {} for e in self.eng}
        self.dsem = [es.enter_context(nc.semaphore("d%d" % i)) for i in range(NDS)]
        self.dcnt = [0] * NDS
        self.dnext = 0
        self.nins = 0

    def sb(self, name, shape, dt=F32):
        t = self.es.enter_context(self.nc.sbuf_tensor(name, list(shape), dt))
        return V(t[:], Reg())

    def _wait(self, e, ev):
        key, sem, val = ev
        if self.seen[e].get(key, 0) >= val:
            return
        if key == e and (e == "pe" or not SAME):
            return
        self.eng[e].wait_ge(sem, val)
        self.seen[e][key] = val
        self.nins += 1

    def _deps(self, e, reads, writes):
        for r in reads:
            if r.w is not None:
                self._wait(e, r.w)
        for w in writes:
            if w.w is not None:
                self._wait(e, w.w)
            for ev in list(w.r.values()):
                self._wait(e, ev)

    def _commit(self, ev, reads, writes):
        for r in reads:
            old = r.r.get(ev[0])
            if old is None or old[2] < ev[2]:
                r.r[ev[0]] = ev
        for w in writes:
            w.w = ev
            w.r = {}

    def op(self, e, fn, reads, writes):
        ex = [r for r in reads if r.excl]
        if ex:
            writes = list(writes) + ex
        self._deps(e, reads, writes)
        ins = fn(self.eng[e])
        self.cnt[e] += 1
        ins.then_inc(self.sem[e], 1)
        self.nins += 1
        self._commit((e, self.sem[e], self.cnt[e]), reads, writes)

    def dma(self, q, out, in_):
        reads, writes = in_.g, out.g
        i = self.dnext
        self.dnext = (i + 1) % NDS
        key = ("d", i)
        if self.dcnt[i] > 0:
            self._wait(q, (key, self.dsem[i], self.dcnt[i]))
        self._deps(q, reads, writes)
        self.dcnt[i] += 16
        self.eng[q].dma_start(out=out.ap, in_=in_.ap).then_inc(self.dsem[i], 16)
        self.nins += 1
        self._commit((key, self.dsem[i], self.dcnt[i]), reads, writes)

    def wait_all(self, e, regs):
        self._deps(e, regs, [])

    def tt(self, e, out, in0, in1, op):
        self.op(e, lambda E: E.tensor_tensor(out=out.ap, in0=in0.ap, in1=in1.ap, op=op), in0.g + in1.g, out.g)

    def ts(self, e, out, in0, s1, s2=None, op0=ALU.mult, op1=None):
        rd = list(in0.g)
        a1 = s1
        a2 = s2
        if isinstance(s1, V):
            rd += s1.g
            a1 = s1.ap
        if isinstance(s2, V):
            rd += s2.g
            a2 = s2.ap
        if op1 is None:
            self.op(e, lambda E: E.tensor_scalar(out=out.ap, in0=in0.ap, scalar1=a1, scalar2=None, op0=op0), rd, out.g)
        else:
            self.op(e, lambda E: E.tensor_scalar(out=out.ap, in0=in0.ap, scalar1=a1, scalar2=a2, op0=op0, op1=op1), rd, out.g)

    def stt(self, out, in0, sc, in1, op0, op1):
        rd = in0.g + in1.g
        a = sc
        if isinstance(sc, V):
            rd = rd + sc.g
            a = sc.ap
        self.op("dve", lambda E: E.scalar_tensor_tensor(out=out.ap, in0=in0.ap, scalar=a, in1=in1.ap, op0=op0, op1=op1), rd, out.g)

    def act(self, out, in_, func, bias=None, scale=1.0):
        rd = list(in_.g)
        kw = {}
        if isinstance(bias, V):
            rd += bias.g
            kw["bias"] = bias.ap
        elif bias is not None:
            kw["bias"] = bias
        if isinstance(scale, V):
            rd += scale.g
            kw["scale"] = scale.ap
        else:
            kw["scale"] = scale
        self.op("act", lambda E: E.activation(out=out.ap, in_=in_.ap, func=func, **kw), rd, out.g)

    def cp(self, e, out, in_):
        if e == "act":
            self.act(out, in_, AF.Copy)
        else:
            self.op(e, lambda E: E.tensor_copy(out=out.ap, in_=in_.ap), in_.g, out.g)

    def memset(self, e, out, val):
        self.op(e, lambda E: E.memset(out.ap, val), [], out.g)

    def recip(self, out, in_):
        self.op("dve", lambda E: E.reciprocal(out=out.ap, in_=in_.ap), in_.g, out.g)

    def mm(self, out, lhsT, rhs, start=True, stop=True):
        self.op("pe", lambda E: E.matmul(out.ap, lhsT=lhsT.ap, rhs=rhs.ap, start=start, stop=stop), lhsT.g + rhs.g, out.g)

    def tr(self, out, in_, ident):
        self.op("pe", lambda E: E.transpose(out.ap, in_.ap, ident.ap), in_.g + ident.g, out.g)

    def scan(self, out, d0, d1, init, op0, op1):
        self.op("dve", lambda E: E.tensor_tensor_scan(out=out.ap, data0=d0.ap, data1=d1.ap, initial=init, op0=op0, op1=op1),
                d0.g + d1.g, out.g)


def build(nb, nlayers=DEPTH, dbg=()):
    nc = bass.Bass("TRN2", target_bir_lowering=False)
    es = ExitStack()
    k = KB(nc, es)

    def din(name, shape):
        return V(nc.dram_tensor(name, list(shape), F32, kind="ExternalInput").ap(), Reg())

    xT_d = din("xT", [nb, D, SEQ])
    ctxT_d = din("ctxT", [nb, D, NCTX])
    cT_d = din("cT", [D, 5])
    sp_d = din("sp", [128, NSP])
    CD = {n: din("c_" + n, s) for n, s in CONST_SHAPES.items()}
    WD = {n: din(n, s) for n, s in WEIGHT_SHAPES.items()}
    outT_d = V(nc.dram_tensor("outT", [nb, D, SEQ], F32, kind="ExternalOutput").ap(), Reg())
    pbuf_ap = nc.dram_tensor("pbuf", [NPT, 128, T], F32, kind="Internal").ap()
    pbuf = [V(pbuf_ap[i], Reg()) for i in range(NPT)]
    vf_ap = nc.dram_tensor("vfirst", [4, 128, T], F32, kind="Internal").ap()
    vfd = [V(vf_ap[i], Reg()) for i in range(4)]
    dbg_out = {}

    def dump(name, v, shape, q="sp"):
        if name in dbg:
            o = V(nc.dram_tensor("dbg_" + name, list(shape), F32, kind="ExternalOutput").ap(), Reg())
            dbg_out[name] = o
            k.dma(q, o, v)

    xT = k.sb("xT_sb", [128, 8, T])
    hT = k.sb("hT_sb", [128, 8, T], BF16)
    psall = es.enter_context(nc.psum_tensor("ps", [128, 8, 512], F32))
    PSR = [Reg(excl=True) for _ in range(8)]

    def PS(b, lo=0, hi=512):
        return V(psall[:, b, lo:hi], PSR[b])

    def PS2(b0, lo, hi):
        return V(psall[:, b0:b0 + 2, lo:hi], [PSR[b0], PSR[b0 + 1]])

    spt = k.sb("spt", [128, NSP])
    k.dma("sp", spt, sp_d)

    def SPc(name, idx):
        return spt[:, SPL[name] + idx:SPL[name] + idx + 1]

    cst = {}
    for n in ["onesr", "bo", "rmask"]:
        cst[n] = k.sb("sc_" + n, CONST_SHAPES[n])
        k.dma("sp", cst[n], CD[n])
    for n in ["ident", "pm"]:
        cst[n] = k.sb("sc_" + n, CONST_SHAPES[n], BF16)
        k.dma("pool", cst[n], CD[n])
    cst["mask2"] = k.sb("sc_mask2", [128, 2, 256])
    cst["mmask"] = k.sb("sc_mmask", [128, 2, 128])
    for d in range(2):
        k.dma("sp", cst["mask2"][:, d, :], CD["mask2"][d])
        k.dma("sp", cst["mmask"][:, d, :], CD["mmask"][d])
    cst["cd"] = k.sb("sc_cd", [128, 8])
    k.dma("sp", cst["cd"], CD["cd"])
    WAup = k.sb("WAup", [128, 2, 2, 512], BF16)
    G0up = k.sb("G0up", [128, 2, 512], BF16)
    GVup = k.sb("GVup", [64, 2, 512], BF16)
    for l in range(2):
        for d in range(2):
            k.dma("pool", WAup[0:64, l, d, :], WD["w_up"][l, d])
            k.dma("pool", WAup[64:128, l, d, :], WD["a_up"][l, d])
        k.dma("pool", G0up[:, l, :], WD["g_up"][l, 0:128, :])
        k.dma("pool", GVup[0:32, l, :], WD["g_up"][l, 128:160, :])
    k.dma("pool", GVup[32:64, 1, :], WD["v_up"][0])
    omm = k.sb("omm", [128, 30])
    mu0 = spt[:, SPL["mu"]:SPL["mu"] + 210].rr("p (t s) -> p t s", s=7)[:, :, 0]
    k.ts("dve", omm, mu0, -1.0, 1.0, ALU.mult, ALU.add)
    omka = k.sb("omka", [128, 8])
    k.ts("dve", omka, spt[:, SPL["ka"]:SPL["ka"] + 8], -1.0, 1.0, ALU.mult, ALU.add)

    ARW = 15360
    arena = es.enter_context(nc.sbuf_tensor("arena", [128, ARW], F32))

    class Phase:
        def __init__(self):
            self.off = 0
            self.regs = []

        def sb(self, shape, dt=F32):
            n = 1
            for s_ in shape[1:]:
                n *= s_
            words = n if dt == F32 else (n + 1) // 2
            words = (words + 7) // 8 * 8
            assert self.off + words <= ARW, (self.off, words)
            ap = arena[0:shape[0], self.off:self.off + words]
            self.off += words
            if dt != F32:
                ap = ap.bitcast(dt)
            ap = ap[:, 0:n]
            if len(shape) > 2:
                names = " ".join("d%d" % i for i in range(len(shape) - 1))
                ap = ap.rearrange("p (%s) -> p %s" % (names, names), **{"d%d" % i: shape[i + 1] for i in range(len(shape) - 1)})
            r = Reg()
            self.regs.append(r)
            return V(ap, r)

        def close(self):
            for e in ("pe", "dve", "act", "pool", "sp"):
                k._deps(e, [], self.regs)

    scT = k.sb("scT", [128, 8, 5])
    k.dma("sp", scT, cT_d.rr("(c p) w -> p c w", p=128))
    k.act(scT, scT, AF.Silu)
    ada = [k.sb("ada%d" % l, [128, 48, 5]) for l in range(2)]
    ph0 = Phase()
    wadab = [ph0.sb([128, 8, 512]) for i in range(2)]
    it = 0
    for l in range(nlayers):
        for qg in range(12):
            wb = wadab[it % 2]
            it += 1
            k.dma("sp", wb, WD["w_ada"][l].rr("(c p) n -> p c n", p=128)[:, :, qg * 512:(qg + 1) * 512])
            for qq in range(4):
                q = qg * 4 + qq
                for c in range(8):
                    k.mm(PS(0, q * 5, q * 5 + 5), wb[:, c, qq * 128:(qq + 1) * 128], scT[:, c, :], start=(c == 0), stop=(c == 7))
        k.tt("dve", ada[l], PS(0, 0, 240).rr("p (q w) -> p q w", w=5),
             spt[:, SPL["bada"] + l * 48:SPL["bada"] + (l + 1) * 48].rr("p (q o) -> p q o", o=1).bc([128, 48, 5]), ALU.add)
    ph0.close()
    A1 = [k.sb("A1_%d" % l, [128, 8, 5]) for l in range(2)]
    A2 = [k.sb("A2_%d" % l, [128, 8, 5]) for l in range(2)]
    for l in range(nlayers):
        for (A, nm, q0) in ((A1, "norm1", 8), (A2, "norm2", 32)):
            k.ts("dve", A[l], ada[l][:, q0:q0 + 8, :], 1.0, None, ALU.add)
            k.tt("dve", A[l], A[l], spt[:, SPL[nm] + l * 8:SPL[nm] + l * 8 + 8].rr("p (c o) -> p c o", o=1).bc([128, 8, 5]), ALU.mult)
    zero_col = k.sb("zero_col", [128, 1])
    k.memset("dve", zero_col, 0.0)
    eps_n = k.sb("eps_n", [128, 1])
    k.memset("dve", eps_n, 1e-6)
    eps_r = k.sb("eps_r", [128, 1])
    k.memset("dve", eps_r, 64e-5)
    eps_t = k.sb("eps_t", [128, 1])
    k.memset("dve", eps_t, 1e-5)

    def norm(ph, scale_fn, bias_fn, out_fn, blocks):
        sq = [ph.sb([128, 512]) for i in range(2)]
        rstd = ph.sb([128, 512])
        ntmp = [ph.sb([128, 512]) for i in range(2)]
        n = 0
        for (t0, t1) in blocks:
            W = t1 - t0
            for c in range(8):
                s = sq[n % 2]
                n += 1
                k.act(s[:, :W], xT[:, c, t0:t1], AF.Square)
                k.mm(PS(0, 0, W), cst["onesr"], s[:, :W], start=(c == 0), stop=(c == 7))
            k.act(rstd[:, :W], PS(0, 0, W), AF.Sqrt, bias=eps_n)
            k.recip(rstd[:, :W], rstd[:, :W])
            for c in range(8):
                tmp = ntmp[c % 2]
                k.tt("dve", tmp[:, :W], xT[:, c, t0:t1], rstd[:, :W], ALU.mult)
                b_ = bias_fn(c, t0 < NCTX)
                k.act(out_fn(c, t0, t1), tmp[:, :W], AF.Identity, bias=(b_ if b_ is not None else zero_col), scale=scale_fn(c, t0 < NCTX))

    def who_idx(is_ctx, b):
        return 4 if is_ctx else b

    NMb_p = [[k.sb("NMb%d_%d" % (d_, i), [128, 2, 2, 128], F32) for i in range(2)] for d_ in range(2)]
    Ppb_p = [[k.sb("Ppb%d_%d" % (d_, i), [128, 2, 128], F32) for i in range(2)] for d_ in range(2)]
    Xsb_p = [k.sb("Xsb%d" % d_, [128, 2, 64], F32) for d_ in range(2)]
    RNG = [(256 * i, 256 * (i + 1)) for i in range(9)]
    RW = 256

    for b in range(nb):
        for c in range(8):
            k.dma("sp", xT[:, c, NCTX:T], xT_d[b, c * 128:(c + 1) * 128, :])
            k.dma("sp", xT[:, c, 0:NCTX], ctxT_d[b, c * 128:(c + 1) * 128, :])
        for l in range(nlayers):
            last = (l == DEPTH - 1)
            ph = Phase()
            norm(ph, lambda c, ic: A1[l][:, c, who_idx(ic, b):who_idx(ic, b) + 1],
                 lambda c, ic: ada[l][:, 0 + c, who_idx(ic, b):who_idx(ic, b) + 1],
                 lambda c, t0, t1: hT[:, c, t0:t1], BLK)
            ph.close()
            if ("hT_%d_%d" % (b, l)) in dbg:
                o_ = V(nc.dram_tensor("dbg_hT_%d_%d" % (b, l), [128, 8, T], BF16, kind="ExternalOutput").ap(), Reg())
                dbg_out["hT_%d_%d" % (b, l)] = o_
                k.dma("sp", o_, hT)
            ph = Phase()
            ubuf = [ph.sb([128, T]) for i in range(2)]
            lbuf = [ph.sb([128, T]) for i in range(1)]
            wg = [ph.sb([128, 8, 512], BF16) for i in range(2)]
            groups = [(0, 512), (512, 512), (1024, 512), (1536, 288)] + [(1824 + 512 * i, 512) for i in range(5)]
            ti = 0
            gi = 0
            for (c0, ncol) in groups:
                wgb = wg[gi % 2]
                gi += 1
                k.dma("pool", wgb[:, :, 0:ncol], WD["w_in"][l].rr("(c p) n -> p c n", p=128)[:, :, c0:c0 + ncol])
                if c0 == 1536 and l == 1:
                    k.dma("pool", wgb[:, :, 288:320], WD["w_vres_down"][0].rr("(c p) n -> p c n", p=128))
                if c0 == 1536:
                    tl = [(0, 128), (128, 128), (256, 64 if l == 1 else 32)]
                else:
                    tl = [(i * 128, 128) for i in range(4)]
                for (off, M) in tl:
                    is_rwkv = ti < 15
                    u = ubuf[ti % 2]
                    o = lbuf[0]
                    dst = u if is_rwkv else o
                    for bi, (t0, t1) in enumerate(BLK):
                        W = t1 - t0
                        pb = (ti * 5 + bi) % 8
                        for c in range(8):
                            k.mm(PS(pb, 0, W)[0:M], wgb[:, c, off:off + M], hT[:, c, t0:t1], start=(c == 0), stop=(c == 7))
                        k.cp("act" if (bi % 2 == 0) else "dve", dst[0:M, t0:t1], PS(pb, 0, W)[0:M])
                    if is_rwkv:
                        mub = SPL["mu"] + (l * 15 + ti) * 7
                        k.act(o[0:M, :], u[0:M, :], AF.Copy, scale=omm[0:M, l * 15 + ti:l * 15 + ti + 1])
                        cols = rwkv_tile_cols(l)[ti]
                        ldirs = sorted(set((c_[1] // 8) if isinstance(c_, tuple) else (c_ // 456) for c_ in cols))
                        cdirs = sorted(set((c_[1] // 16) if isinstance(c_, tuple) else (c_ // 912) for c_ in cols))
                        uL = u[0:M, NCTX:T].rr("p (r c) -> p r c", c=64)
                        oL = o[0:M, NCTX:T].rr("p (r c) -> p r c", c=64)
                        for dr in ldirs:
                            m = spt[0:M, mub + 1 + dr:mub + 2 + dr]
                            if dr == 0:
                                k.stt(oL[:, :, 1:64], uL[:, :, 0:63], m, oL[:, :, 1:64], ALU.mult, ALU.add)
                            elif dr == 1:
                                k.stt(oL[:, :, 0:63], uL[:, :, 1:64], m, oL[:, :, 0:63], ALU.mult, ALU.add)
                            elif dr == 2:
                                k.stt(o[0:M, NCTX + 64:T], u[0:M, NCTX:T - 64], m, o[0:M, NCTX + 64:T], ALU.mult, ALU.add)
                            else:
                                k.stt(o[0:M, NCTX:T - 64], u[0:M, NCTX + 64:T], m, o[0:M, NCTX:T - 64], ALU.mult, ALU.add)
                        for dr in cdirs:
                            m = spt[0:M, mub + 5 + dr:mub + 6 + dr]
                            if dr == 0:
                                k.stt(o[0:M, 1:NCTX], u[0:M, 0:NCTX - 1], m, o[0:M, 1:NCTX], ALU.mult, ALU.add)
                            else:
                                k.stt(o[0:M, 0:NCTX - 1], u[0:M, 1:NCTX], m, o[0:M, 0:NCTX - 1], ALU.mult, ALU.add)
                        if ti == 12:
                            k.act(o[0:64, :], o[0:64, :], AF.Tanh)
                        elif ti == 13:
                            k.act(o[0:128, :], o[0:128, :], AF.Sigmoid)
                        elif ti == 14:
                            k.act(o[0:32, :], o[0:32, :], AF.Sigmoid)
                    k.dma("sp", pbuf[ti][0:M, :], o[0:M, :])
                    if l == 0 and 8 <= ti < 12:
                        k.dma("sp", vfd[ti - 8], o[0:M, :])
                    ti += 1
            assert ti == NPT
            ph.close()
            if ("pbuf_%d_%d" % (b, l)) in dbg:
                o_ = V(nc.dram_tensor("dbg_pbuf_%d_%d" % (b, l), [NPT, 128, T], F32, kind="ExternalOutput").ap(), Reg())
                dbg_out["pbuf_%d_%d" % (b, l)] = o_
                for i_ in range(NPT):
                    k.dma("sp", o_[i_], pbuf[i_])
            if "stop_p2" in dbg:
                break
            ph = Phase()
            G0s = ph.sb([128, RW], BF16)
            G1s = ph.sb([64, RW], BF16)
            ybuf = ph.sb([128, T])
            oT = hT
            W = RW
            nch = 2

            class St:
                pass

            def alloc_stream(d):
                B = St()
                B.d = d
                B.fb = [ph.sb([128, RW]) for i in range(12)]
                B.hb = [ph.sb([128, RW], BF16) for i in range(6)]
                B.WAs = ph.sb([128, RW], BF16)
                B.ARb = ph.sb([128, 2, 2, 128], BF16)
                B.Btok = ph.sb([128, 2, 128], BF16)
                B.Ktok = ph.sb([128, 2, 128], BF16)
                B.Vtok = ph.sb([128, 2, 128], BF16)
                B.gC = ph.sb([128, 2])
                B.Am = ph.sb([128, 2, 2, 256], BF16)
                B.NMb = NMb_p[d]
                B.Ppb = Ppb_p[d]
                B.Xsb = Xsb_p[d]
                B.Usb = ph.sb([128, 2, 64], BF16)
                B.Sst = ph.sb([128, 64])
                B.Sbf = ph.sb([128, 64], BF16)
                B.scm = ph.sb([128, 2, 128], BF16)
                B.dTt = ph.sb([128, 256])
                B.qdt = ph.sb([128, 128])
                B.kdt = ph.sb([128, 128])
                B.off = 4 * d
                return B
            SB = [alloc_stream(0), alloc_stream(1)]

            def v3(x):
                return x.rr("p (c t) -> p c t", t=128)

            def run_streams(gens):
                gens = list(gens)
                while gens:
                    for g in list(gens):
                        try:
                            next(g)
                        except StopIteration:
                            gens.remove(g)

            def gn_block(src, wcol, bcol, eps_col, out_f, pa, pb_, gnb):
                cen, sqv, rs = gnb
                k.mm(PS(pa, 0, W), cst["bo"], src, True, True)
                k.stt(cen, PS(pa, 0, W), -1.0 / 64, src, ALU.mult, ALU.add)
                k.act(sqv, cen, AF.Square)
                k.mm(PS(pb_, 0, W), cst["bo"], sqv, True, True)
                k.act(rs, PS(pb_, 0, W), AF.Sqrt, bias=eps_col, scale=1.0 / 64)
                k.recip(rs, rs)
                k.tt("dve", cen, cen, rs, ALU.mult)
                k.act(out_f, cen, AF.Identity, bias=bcol, scale=wcol)

            def transposes(src_bf, dst_tok, bank, mul_tab=None):
                pt = PS(bank).bitcast(BF16)
                for c in range(nch):
                    k.tr(pt[:, c * 128:(c + 1) * 128], src_bf[:, c * 128:(c + 1) * 128], cst["ident"])
                ptv = pt[:, 0:nch * 128].rr("p (c f) -> p c f", f=128)
                if mul_tab is None:
                    k.cp("act", dst_tok, ptv)
                else:
                    k.tt("dve", dst_tok, ptv, mul_tab.rr("p (o f) -> p o f", o=1).bc([128, nch, 128]), ALU.mult)

            def rwkv_pass(hp, B):
                d = B.d
                hc = slice(hp * 128, (hp + 1) * 128)
                fb, hb = B.fb, B.hb
                ARb, Btok, Ktok, Vtok, gC, Am, NMb, Ppb = B.ARb, B.Btok, B.Ktok, B.Vtok, B.gC, B.Am, B.NMb, B.Ppb
                Xsb, Usb, Sst, Sbf, WAs = B.Xsb, B.Usb, B.Sst, B.Sbf, B.WAs

                def P(role, lo=0, hi=512):
                    return PS((role + B.off) % 8, lo, hi)

                def P2(role, lo, hi):
                    return PS2((role + B.off) % 8, lo, hi)
                order = list(range(9)) if d == 0 else [0] + list(range(8, 0, -1))
                k.memset("dve", Sst, 0.0)
                k.memset("dve", Sbf, 0.0)
                for ri in order:
                    t0, t1 = RNG[ri]
                    rF, kF, vF = fb[0], fb[1], fb[2]
                    k.dma("sp", rF, pbuf[0 + hp][:, t0:t1])
                    k.dma("sp", kF, pbuf[4 + hp][:, t0:t1])
                    k.dma("sp", vF, pbuf[8 + hp][:, t0:t1])
                    k.dma("pool", WAs, pbuf[12][:, t0:t1])
                    sw, Lr, Lin, Lex, icl, kk, t1b, t2b, Eb = fb[3], fb[4], fb[5], fb[6], fb[7], fb[8], fb[9], fb[10], fb[11]
                    k.mm(P(0, 0, W), WAup[0:64, l, d, hc], WAs[0:64, :])
                    k.mm(P(1, 0, W), WAup[64:128, l, d, hc], WAs[64:128, :])
                    yield
                    k.act(sw, P(0, 0, W), AF.Sigmoid, bias=SPc("w0", (l * 2 + d) * 4 + hp))
                    k.act(icl, P(1, 0, W), AF.Sigmoid, bias=SPc("a0", (l * 2 + d) * 4 + hp))
                    k.ts("dve", kk, kF, SPc("kk", l * 4 + hp), None, ALU.mult)
                    k.act(t1b, kk, AF.Square)
                    k.mm(P(0, 0, W), cst["bo"], t1b)
                    yield
                    k.scan(Lr, cst["rmask"][:, :W], sw, 0.0, ALU.mult, ALU.add)
                    totb = v3(Lr)[:, :, 127:128].bc([128, nch, 128])
                    if d == 0:
                        k.cp("dve", Lin, Lr)
                    else:
                        k.tt("dve", t2b, sw, Lr, ALU.subtract)
                        k.tt("dve", v3(Lin), v3(t2b), totb, ALU.add)
                    k.tt("dve", Lex, Lin, sw, ALU.subtract)
                    k.act(gC, v3(Lr)[:, :, 127], AF.Exp, scale=-CDEC)
                    k.ts("dve", t2b, P(0, 0, W), 1e-24, None, ALU.max)
                    k.act(t2b, t2b, AF.Sqrt)
                    yield
                    k.recip(t2b, t2b)
                    k.tt("dve", kk, kk, t2b, ALU.mult)
                    k.act(Eb, Lex, AF.Exp, scale=-CDEC)
                    yield
                    k.stt(ARb[:, :, 0, :], v3(kk), -1.0, v3(Eb), ALU.mult, ALU.mult)
                    k.act(Eb, Lin, AF.Exp, scale=-CDEC)
                    yield
                    k.tt("dve", ARb[:, :, 1, :], v3(rF), v3(Eb), ALU.mult)
                    ktil, bp = t1b, t2b
                    k.ts("dve", ktil, icl, SPc("ka", l * 4 + hp), omka[:, l * 4 + hp:l * 4 + hp + 1], ALU.mult, ALU.add)
                    k.tt("dve", ktil, ktil, kF, ALU.mult)
                    k.tt("dve", bp, kk, icl, ALU.mult)
                    BH, KH, bck, kck, vbf = hb[0], hb[1], hb[2], hb[3], hb[4]
                    k.act(Eb, Lin, AF.Exp, scale=CDEC)
                    yield
                    k.tt("dve", BH, bp, Eb, ALU.mult)
                    k.tt("dve", KH, ktil, Eb, ALU.mult)
                    k.tt("dve", v3(Lex), v3(Lin), totb, ALU.subtract)
                    k.act(Eb, Lex, AF.Exp, scale=CDEC)
                    yield
                    k.tt("dve", bck, bp, Eb, ALU.mult)
                    k.tt("dve", kck, ktil, Eb, ALU.mult)
                    k.cp("act", vbf, vF)
                    yield
                    transposes(bck, Btok, (4 + B.off) % 8)
                    yield
                    transposes(kck, Ktok, (5 + B.off) % 8)
                    yield
                    transposes(vbf, Vtok, (4 + B.off) % 8)
                    yield
                    corder = list(range(nch)) if d == 0 else list(range(nch - 1, -1, -1))
                    for c in corder:
                        cs = slice(c * 128, (c + 1) * 128)
                        for e in range(2):
                            Re = slice(64 * e, 64 * e + 64)
                            arv = ARb[Re, c, :, :].rr("p a t -> p (a t)")
                            k.mm(P(e, 0, 256), BH[Re, cs], arv)
                            k.mm(P(e, 256, 512), KH[Re, cs], arv)
                            k.mm(P(2 + e, 0, 128), ARb[Re, c, 0, :], BH[Re, cs])
                        yield
                        nm0 = NMb[0]
                        k.tt("dve", nm0[:, :, 0, :], P2(0, 0, 128), cst["mask2"][:, d, 0:128].rr("p (o t) -> p o t", o=1).bc([128, 2, 128]), ALU.mult)
                        k.tt("dve", nm0[:, :, 1, :], P2(2, 0, 128), cst["mmask"][:, d, :].rr("p (o t) -> p o t", o=1).bc([128, 2, 128]), ALU.mult)
                        k.tt("dve", Ppb[0], nm0[:, :, 0, :], cst["ident"].rr("p (o t) -> p o t", o=1).bc([128, 2, 128]), ALU.add)
                        for e in range(2):
                            k.tt("dve", Am[:, e, :, :], P(e).rr("p (a t) -> p a t", a=2),
                                 cst["mask2"][:, d, :].rr("p (o t) -> p o t", o=1).bc([128, 2, 256]), ALU.mult)
                        yield
                        cur = 0
                        pcur = 0
                        for itn in range(7):
                            nmc, nmn = NMb[cur], NMb[1 - cur]
                            pc, pn = Ppb[pcur], Ppb[1 - pcur]
                            def rc(x):
                                return x.bitcast(F32R) if DBL_R else x
                            for e in range(2):
                                if itn < 5:
                                    k.mm(P(4, e * 256, e * 256 + 128), rc(nmc[:, e, 1, :]), rc(nmc[:, e, 0, :]))
                                if itn < 6:
                                    k.mm(P(4, e * 256 + 128, e * 256 + 256), rc(nmc[:, e, 0, :]), rc(nmc[:, e, 1, :]))
                                if itn >= 1:
                                    k.mm(P(5, e * 128, e * 128 + 128), rc(nmc[:, e, 1, :]), rc(pc[:, e, :]))
                            yield
                            if itn < 5:
                                k.cp("act", nmn.rr("p e a t -> p (e a t)"), P(4))
                            elif itn == 5:
                                k.cp("act", nmn[:, :, 1, :], P(4).rr("p (e a t) -> p e a t", e=2, a=2)[:, :, 1, :])
                            if itn >= 1:
                                k.tt("dve", pn, P(5, 0, 256).rr("p (e t) -> p e t", e=2), pc, ALU.add)
                                pcur = 1 - pcur
                            yield
                            cur = 1 - cur
                        cur = pcur
                        TT = Ppb[cur]
                        for e in range(2):
                            Re = slice(64 * e, 64 * e + 64)
                            k.mm(P(2 + e, 128, 192), ARb[Re, c, 0, :], Sbf[Re, :], True, False)
                            k.mm(P(2 + e, 128, 192), Am[:, e, 1, 0:128], Vtok[:, c, Re], False, True)
                        yield
                        k.cp("act", Xsb, P2(2, 128, 192))
                        yield
                        for e in range(2):
                            k.mm(P(2 + e, 192, 256), TT[:, e, :], Xsb[:, e, :])
                        yield
                        k.cp("act", Usb, P2(2, 192, 256))
                        yield
                        for e in range(2):
                            Re = slice(64 * e, 64 * e + 64)
                            po = P(6 + e, 0, 128)[Re]
                            k.mm(po, Sbf[Re, :], ARb[Re, c, 1, :], True, False)
                            k.mm(po, Usb[:, e, :], Am[:, e, 0, 128:256], False, False)
                            k.mm(po, Vtok[:, c, Re], Am[:, e, 1, 128:256], False, True)
                        for e in range(2):
                            Ce = slice(64 * e, 64 * e + 64)
                            pd = P(5, 256, 320)[Ce]
                            k.mm(pd, Btok[:, c, Ce], Usb[:, e, :], True, False)
                            k.mm(pd, Ktok[:, c, Ce], Vtok[:, c, Ce], False, True)
                        yield
                        for e in range(2):
                            Re = slice(64 * e, 64 * e + 64)
                            po = P(6 + e, 0, 128)[Re]
                            yv = ybuf[Re, t0 + c * 128:t0 + (c + 1) * 128]
                            k.tt("dve", yv, yv, po, ALU.add)
                        k.stt(Sst, Sst, gC[:, c:c + 1], P(5, 256, 320), ALU.mult, ALU.add)
                        k.cp("act", Sbf, Sst)
                        yield

            def ret_pass(hp, B):
                d = B.d
                fb, hb = B.fb, B.hb
                Ktok, Vtok, Sst, Sbf, scm, dTt, qdt, kdt = B.Ktok, B.Vtok, B.Sst, B.Sbf, B.scm, B.dTt, B.qdt, B.kdt

                def P(role, lo=0, hi=512):
                    return PS((role + B.off) % 8, lo, hi)

                def P2(role, lo, hi):
                    return PS2((role + B.off) % 8, lo, hi)
                order = list(range(9)) if d == 0 else [0] + list(range(8, 0, -1))
                k.dma("sp", dTt, CD["dT"][d * 4 + hp])
                k.dma("sp", qdt, CD["qd"][d * 4 + hp])
                k.dma("sp", kdt, CD["kd"][d * 4 + hp])
                k.memset("dve", Sst, 0.0)
                k.memset("dve", Sbf, 0.0)
                for ri in order:
                    t0, t1 = RNG[ri]
                    qF, kF, vF, gF, cosF, sinF = fb[0], fb[1], fb[2], fb[3], fb[4], fb[5]
                    k.dma("sp", qF, pbuf[15 + hp][:, t0:t1])
                    k.dma("sp", kF, pbuf[19 + hp][:, t0:t1])
                    k.dma("sp", vF, pbuf[23 + hp][:, t0:t1])
                    k.dma("sp", gF, pbuf[(27 if d == 0 else 31) + hp][:, t0:t1])
                    k.dma("sp", cosF, CD["cosT"][:, t0:t1])
                    k.dma("sp", sinF, CD["ssinT"][:, t0:t1])
                    qb, kb, qr, kr, qh, vbf = hb[0], hb[1], hb[2], hb[3], hb[4], hb[5]
                    t1b, t2b = fb[6], fb[7]
                    for (src, sb_, dstb, isq, bank) in ((qF, qb, qr, True, 0), (kF, kb, kr, False, 1)):
                        k.cp("act", sb_, src)
                        yield
                        k.mm(P(bank, 0, W), cst["pm"], sb_)
                        k.tt("dve", t1b, src, cosF, ALU.mult)
                        yield
                        k.tt("dve", t2b, P(bank, 0, W), sinF, ALU.mult)
                        k.tt("dve", t1b, t1b, t2b, ALU.add)
                        if isq:
                            k.cp("act", dstb, t1b)
                            k.tt("dve", v3(qh), v3(t1b), qdt.rr("p (o t) -> p o t", o=1).bc([128, nch, 128]), ALU.mult)
                        else:
                            k.act(dstb, t1b, AF.Copy, scale=0.125)
                        yield
                    k.cp("act", vbf, vF)
                    yield
                    transposes(kr, Ktok, (4 + B.off) % 8, mul_tab=kdt)
                    yield
                    transposes(vbf, Vtok, (5 + B.off) % 8)
                    yield
                    orng = fb[8]
                    corder = list(range(nch)) if d == 0 else list(range(nch - 1, -1, -1))
                    for c in corder:
                        cs = slice(c * 128, (c + 1) * 128)
                        for e in range(2):
                            Re = slice(64 * e, 64 * e + 64)
                            k.mm(P(2 + e, 0, 128), kr[Re, cs], qr[Re, cs])
                        yield
                        k.tt("dve", scm, P2(2, 0, 128), dTt.rr("p (e t) -> p e t", e=2), ALU.mult)
                        yield
                        for e in range(2):
                            Re = slice(64 * e, 64 * e + 64)
                            po = P(6 + e, 0, 128)[Re]
                            k.mm(po, Sbf[Re, :], qh[Re, cs], True, False)
                            k.mm(po, Vtok[:, c, Re], scm[:, e, :], False, True)
                        for e in range(2):
                            Ce = slice(64 * e, 64 * e + 64)
                            pd = P(5, 256, 320)[Ce]
                            k.mm(pd, Ktok[:, c, Ce], Vtok[:, c, Ce], True, True)
                        yield
                        for e in range(2):
                            Re = slice(64 * e, 64 * e + 64)
                            k.cp("act", orng[Re, cs], P(6 + e, 0, 128)[Re])
                        k.stt(Sst, Sst, cst["cd"][:, d * 4 + hp:d * 4 + hp + 1], P(5, 256, 320), ALU.mult, ALU.add)
                        k.cp("act", Sbf, Sst)
                        yield
                    gnv = fb[9]
                    gn_block(orng, SPc("rgw", l * 4 + hp), SPc("rgb", l * 4 + hp), eps_t, gnv, (0 + B.off) % 8, (1 + B.off) % 8, (fb[10], fb[11], fb[6]))
                    k.act(gF, gF, AF.Silu)
                    k.tt("dve", gnv, gnv, gF, ALU.mult)
                    k.tt("dve", ybuf[:, t0:t1], ybuf[:, t0:t1], gnv, ALU.add)
                    yield

            for hp in range(4 if "skip_rwkv" not in dbg else 0):
                hc = slice(hp * 128, (hp + 1) * 128)
                fb = SB[0].fb
                if l == 1:
                    for (t0, t1) in RNG:
                        vF, vfF, sg = fb[0], fb[1], fb[2]
                        k.dma("sp", vF, pbuf[8 + hp][:, t0:t1])
                        k.dma("sp", vfF, vfd[hp][:, t0:t1])
                        k.dma("pool", G1s, pbuf[14][0:64, t0:t1])
                        k.mm(PS(0, 0, W), GVup[32:64, 1, hc], G1s[32:64, :])
                        k.act(sg, PS(0, 0, W), AF.Sigmoid, bias=SPc("v0", hp))
                        k.tt("dve", vfF, vfF, vF, ALU.subtract)
                        k.tt("dve", vfF, vfF, sg, ALU.mult)
                        k.tt("dve", vF, vF, vfF, ALU.add)
                        k.dma("sp", pbuf[8 + hp][:, t0:t1], vF)
                k.memset("dve", ybuf, 0.0)
                run_streams([rwkv_pass(hp, SB[0]), rwkv_pass(hp, SB[1])])
                for (t0, t1) in RNG:
                    rF, kF, vF = fb[0], fb[1], fb[2]
                    k.dma("sp", rF, pbuf[0 + hp][:, t0:t1])
                    k.dma("sp", kF, pbuf[4 + hp][:, t0:t1])
                    k.dma("sp", vF, pbuf[8 + hp][:, t0:t1])
                    k.dma("pool", G0s, pbuf[13][:, t0:t1])
                    k.dma("pool", G1s, pbuf[14][0:64, t0:t1])
                    k.stt(rF, rF, SPc("rk", l * 4 + hp), kF, ALU.mult, ALU.mult)
                    k.mm(PS(2, 0, W), cst["bo"], rF)
                    k.tt("dve", vF, vF, PS(2, 0, W), ALU.mult)
                    gnv = fb[3]
                    gn_block(ybuf[:, t0:t1], SPc("lnw", l * 4 + hp), SPc("lnb", l * 4 + hp), eps_r, gnv, 0, 1, (fb[9], fb[10], fb[11]))
                    k.tt("dve", gnv, gnv, vF, ALU.add)
                    k.mm(PS(3, 0, W), G0up[:, l, hc], G0s, True, False)
                    k.mm(PS(3, 0, W), GVup[0:32, l, hc], G1s[0:32, :], False, True)
                    k.tt("dve", oT[:, hp, t0:t1], gnv, PS(3, 0, W), ALU.mult)
            for hp in range(4 if "skip_ret" not in dbg else 0):
                k.memset("dve", ybuf, 0.0)
                run_streams([ret_pass(hp, SB[0]), ret_pass(hp, SB[1])])
                k.cp("act", oT[:, 4 + hp, :], ybuf)
            ph.close()
            if ("oT_%d_%d" % (b, l)) in dbg:
                o_ = V(nc.dram_tensor("dbg_oT_%d_%d" % (b, l), [128, 8, T], BF16, kind="ExternalOutput").ap(), Reg())
                dbg_out["oT_%d_%d" % (b, l)] = o_
                k.dma("sp", o_, oT)
            if "stop_p3" in dbg:
                break
            ph = Phase()
            wob = ph.sb([128, 8, 1024], BF16)
            k.dma("pool", wob, WD["w_out"][l].rr("(c p) n -> p c n", p=128))
            n = 0
            for m in range(8):
                for (t0, t1) in BLK:
                    W = t1 - t0
                    pb = n % 8
                    n += 1
                    for kc in range(8):
                        k.mm(PS(pb, 0, W), wob[:, kc, m * 128:(m + 1) * 128], oT[:, kc, t0:t1], start=(kc == 0), stop=(kc == 7))
                    wi = who_idx(t0 < NCTX, b)
                    k.stt(xT[:, m, t0:t1], PS(pb, 0, W), ada[l][:, 16 + m, wi:wi + 1], xT[:, m, t0:t1], ALU.mult, ALU.add)
            ph.close()
            ph = Phase()
            norm(ph, lambda c, ic: A2[l][:, c, who_idx(ic, b):who_idx(ic, b) + 1],
                 lambda c, ic: ada[l][:, 24 + c, who_idx(ic, b):who_idx(ic, b) + 1],
                 lambda c, t0, t1: hT[:, c, t0:t1], BLK)
            ph.close()
            ph = Phase()
            wgf = [ph.sb([128, 8, 512], BF16) for i in range(2)]
            actb = ph.sb([128, NJ, 512], BF16)
            wfo = [ph.sb([128, NJ, 128], BF16) for i in range(2)]
            sgb = [ph.sb([128, 512]) for i in range(2)]
            wfi_v = WD["w_ffn_in"][l].rr("(c p) n -> p c n", p=128)
            wfo_v = WD["w_ffn_out"][l].rr("(j p) n -> p j n", p=128)
            wi_ = 0
            for (t0, t1) in BLK:
                W = t1 - t0
                if (last and t0 < NCTX) or "skip_ffn" in dbg:
                    continue
                for jg in range(6):
                    nj = 4 if jg < 5 else 2
                    wgate = wgf[0]
                    wup = wgf[1]
                    k.dma("pool", wgate[:, :, 0:nj * 128], wfi_v[:, :, jg * 512:jg * 512 + nj * 128])
                    k.dma("pool", wup[:, :, 0:nj * 128], wfi_v[:, :, HID + jg * 512:HID + jg * 512 + nj * 128])
                    for jj in range(nj):
                        j = jg * 4 + jj
                        pg = (2 * j) % 8
                        pu = (2 * j + 1) % 8
                        for c in range(8):
                            k.mm(PS(pg, 0, W), wgate[:, c, jj * 128:(jj + 1) * 128], hT[:, c, t0:t1], start=(c == 0), stop=(c == 7))
                        for c in range(8):
                            k.mm(PS(pu, 0, W), wup[:, c, jj * 128:(jj + 1) * 128], hT[:, c, t0:t1], start=(c == 0), stop=(c == 7))
                        sgt = sgb[j % 2]
                        k.act(sgt[:, :W], PS(pg, 0, W), AF.Silu)
                        k.tt("dve", actb[:, j, :W], sgt[:, :W], PS(pu, 0, W), ALU.mult)
                for m in range(8):
                    wf = wfo[wi_ % 2]
                    wi_ += 1
                    k.dma("pool", wf, wfo_v[:, :, m * 128:(m + 1) * 128])
                    pb = m % 8
                    for j in range(NJ):
                        k.mm(PS(pb, 0, W), wf[:, j, :], actb[:, j, :W], start=(j == 0), stop=(j == NJ - 1))
                    wi = who_idx(t0 < NCTX, b)
                    k.stt(xT[:, m, t0:t1], PS(pb, 0, W), ada[l][:, 40 + m, wi:wi + 1], xT[:, m, t0:t1], ALU.mult, ALU.add)
            ph.close()
            if ("xT_%d_%d" % (b, l)) in dbg:
                o_ = V(nc.dram_tensor("dbg_xT_%d_%d" % (b, l), [128, 8, T], F32, kind="ExternalOutput").ap(), Reg())
                dbg_out["xT_%d_%d" % (b, l)] = o_
                k.dma("sp", o_, xT)
        if "stop_p2" in dbg or "stop_p3" in dbg:
            continue
        ph = Phase()
        obuf = [ph.sb([128, T]) for i in range(2)]
        rst_all = ph.sb([128, T])
        sq = [ph.sb([128, 512]) for i in range(2)]
        n = 0
        for (t0, t1) in BLK[1:]:
            W = t1 - t0
            for c in range(8):
                s = sq[n % 2]
                n += 1
                k.act(s[:, :W], xT[:, c, t0:t1], AF.Square)
                k.mm(PS(0, 0, W), cst["onesr"], s[:, :W], start=(c == 0), stop=(c == 7))
            k.act(rst_all[:, t0:t1], PS(0, 0, W), AF.Sqrt, bias=eps_n)
            k.recip(rst_all[:, t0:t1], rst_all[:, t0:t1])
        for c in range(8):
            ob = obuf[c % 2]
            k.tt("dve", ob[:, NCTX:T], xT[:, c, NCTX:T], rst_all[:, NCTX:T], ALU.mult)
            k.act(ob[:, NCTX:T], ob[:, NCTX:T], AF.Copy, scale=SPc("normf", c))
            k.dma("sp", outT_d[b, c * 128:(c + 1) * 128, :], ob[:, NCTX:T])
        ph.close()
    k.wait_all("sp", [outT_d.g[0]] + [v.g[0] for v in dbg_out.values()])
    es.close()
    return nc, k, dbg_out


_CACHE = {}


def kernel(**inp):
    inp = {k_: np.asarray(v) for k_, v in inp.items()}
    ncores = 8
    B = inp["x"].shape[0]
    nb = B // ncores
    if "nc" not in _CACHE:
        _CACHE["nc"] = build(nb)[0]
    nc = _CACHE["nc"]
    xT = np.ascontiguousarray(np.transpose(inp["x"], (0, 2, 1)))
    ctxT = np.ascontiguousarray(np.transpose(inp["ctx"], (0, 2, 1)))
    sp = build_sp(inp)
    consts = build_consts()
    in_maps = []
    for i in range(ncores):
        m = {"xT": xT[i * nb:(i + 1) * nb], "ctxT": ctxT[i * nb:(i + 1) * nb]}
        cT = np.zeros((D, 5), np.float32)
        cT[:, :nb] = inp["c"][i * nb:(i + 1) * nb].T
        cT[:, 4] = inp["c_ctx"]
        m["cT"] = cT
        m["sp"] = sp
        for n_ in CONST_SHAPES:
            m["c_" + n_] = consts[n_]
        for n_ in WEIGHT_SHAPES:
            m[n_] = inp[n_]
        in_maps.append(m)
    res = run_bass_kernel_spmd(nc, in_maps, core_ids=list(range(ncores)))
    outT = np.concatenate([np.asarray(r["outT"]) for r in res.results], axis=0)
    return np.ascontiguousarray(np.transpose(outT, (0, 2, 1))).astype(np.float32)
```

```python
import numpy as np
import ml_dtypes
from contextlib import ExitStack
import concourse.bass as bass
import concourse.mybir as mybir
from concourse.bass_utils import run_bass_kernel_spmd

F32 = mybir.dt.float32
BF16 = mybir.dt.bfloat16
F32R = mybir.dt.float32r
DBL_R = False
ALU = mybir.AluOpType
AF = mybir.ActivationFunctionType

D = 1024
T = 2304
NCTX = 256
SEQ = 2048
DEPTH = 2
NIN = 4384
HID = 2816
NJ = HID // 128
CDEC = 0.6065306597126334
BLK = [(0, 256), (256, 768), (768, 1280), (1280, 1792), (1792, 2304)]
NPT = 35
SAME = True
NDS = 24

SPL = {}
_o = 0
for _n, _w in [("norm1", 16), ("norm2", 16), ("normf", 8), ("bada", 96), ("mu", 2 * 15 * 7), ("w0", 16), ("a0", 16),
               ("kk", 8), ("ka", 8), ("rk", 8), ("v0", 4), ("lnw", 8), ("lnb", 8), ("rgw", 8), ("rgb", 8)]:
    SPL[_n] = _o
    _o += _w
NSP = _o


def rwkv_tile_cols(l):
    tiles = []
    for i in range(12):
        tiles.append([i * 128 + p for p in range(128)])
    tiles.append([1536 + p for p in range(128)])
    tiles.append([1664 + p for p in range(128)])
    g1 = [1792 + p for p in range(32)]
    if l == 1:
        g1 += [("v", ch) for ch in range(32)]
    tiles.append(g1)
    return tiles


def build_sp(inp):
    sp = np.zeros((128, NSP), np.float32)
    p = np.arange(128)
    for l in range(2):
        for c in range(8):
            sp[:, SPL["norm1"] + l * 8 + c] = inp["norm1"][l, c * 128 + p]
            sp[:, SPL["norm2"] + l * 8 + c] = inp["norm2"][l, c * 128 + p]
        for q in range(48):
            sp[:, SPL["bada"] + l * 48 + q] = inp["b_ada"][l, q * 128 + p]
        tiles = rwkv_tile_cols(l)
        for ti, cols in enumerate(tiles):
            base = SPL["mu"] + (l * 15 + ti) * 7
            for pp, col in enumerate(cols):
                if isinstance(col, tuple):
                    ch = col[1]
                    m = inp["mu_vres"][0, ch]
                    ld, cd = ch // 8, ch // 16
                else:
                    m = inp["mu_rwkv"][l, col]
                    ld, cd = col // 456, col // 912
                sp[pp, base + 0] = m
                sp[pp, base + 1 + ld] = m
                sp[pp, base + 5 + cd] = m
        for d in range(2):
            for hp in range(4):
                sp[:, SPL["w0"] + (l * 2 + d) * 4 + hp] = inp["w0"][l, d, hp * 128 + p]
                sp[:, SPL["a0"] + (l * 2 + d) * 4 + hp] = inp["a0"][l, d, hp * 128 + p]
        for hp in range(4):
            sp[:, SPL["kk"] + l * 4 + hp] = inp["k_k"][l, hp * 128 + p]
            sp[:, SPL["ka"] + l * 4 + hp] = inp["k_a"][l, hp * 128 + p]
            sp[:, SPL["rk"] + l * 4 + hp] = inp["r_k"][l].reshape(512)[hp * 128 + p]
            sp[:, SPL["lnw"] + l * 4 + hp] = inp["ln_x_w"][l, hp * 128 + p]
            sp[:, SPL["lnb"] + l * 4 + hp] = inp["ln_x_b"][l, hp * 128 + p]
            sp[:, SPL["rgw"] + l * 4 + hp] = inp["ret_gn_w"][l, hp * 128 + p]
            sp[:, SPL["rgb"] + l * 4 + hp] = inp["ret_gn_b"][l, hp * 128 + p]
    for c in range(8):
        sp[:, SPL["normf"] + c] = inp["norm_f"][c * 128 + p]
    for hp in range(4):
        sp[:, SPL["v0"] + hp] = inp["v0"][0, hp * 128 + p]
    return sp


def build_consts():
    c = {}
    s = np.arange(128)[:, None]
    t = np.arange(128)[None, :]
    c["ident"] = np.eye(128, dtype=np.float32)
    c["onesr"] = np.full((128, 128), 1.0 / 1024, np.float32)
    bo = np.zeros((128, 128), np.float32)
    bo[:64, :64] = 1
    bo[64:, 64:] = 1
    c["bo"] = bo
    pm = np.zeros((128, 128), np.float32)
    for m in range(128):
        n = m % 64
        pm[(m - n) + ((n + 32) % 64), m] = 1
    c["pm"] = pm
    m2 = np.zeros((2, 128, 256), np.float32)
    m2[0, :, :128] = s < t
    m2[0, :, 128:] = s <= t
    m2[1, :, :128] = s > t
    m2[1, :, 128:] = s >= t
    c["mask2"] = m2
    mm_ = np.zeros((2, 128, 128), np.float32)
    mm_[0] = s > t
    mm_[1] = s < t
    c["mmask"] = mm_
    rm = np.ones((128, 512), np.float32)
    rm[:, ::128] = 0
    c["rmask"] = rm
    tok = np.arange(SEQ)
    row = (tok // 64).astype(np.float32)
    col = (tok % 64).astype(np.float32)
    nf = 16
    freqs = (np.float32(10000.0) ** (-np.arange(nf, dtype=np.float32) / nf)).astype(np.float32)
    ang = np.concatenate([row[:, None] * freqs, col[:, None] * freqs], -1).astype(np.float32)
    cosT = np.ones((128, T), np.float32)
    ssinT = np.zeros((128, T), np.float32)
    for p_ in range(128):
        n = p_ % 64
        cosT[p_, NCTX:] = np.cos(ang[:, n % 32])
        ssinT[p_, NCTX:] = np.sin(ang[:, n % 32]) * (-1.0 if n < 32 else 1.0)
    c["cosT"] = cosT
    c["ssinT"] = ssinT
    lg = np.log(1.0 - 2.0 ** (-5.0 - np.arange(8, dtype=np.float64)))
    dt_ = np.zeros((2, 4, 128, 2, 128), np.float32)
    qd = np.zeros((2, 4, 128, 128), np.float32)
    kd = np.zeros((2, 4, 128, 128), np.float32)
    cd = np.zeros((128, 8), np.float32)
    sv = np.arange(128)
    for d in range(2):
        for hp in range(4):
            for e in range(2):
                h = 2 * hp + e
                g = lg[h] if d == 0 else lg[7 - h]
                if d == 0:
                    dt_[d, hp, :, e, :] = np.where(t >= s, np.exp(g * np.maximum(t - s, 0)), 0)
                    qd[d, hp, 64 * e:64 * e + 64, :] = np.exp(g * (sv + 1.0))[None, :]
                    kd[d, hp, :, 64 * e:64 * e + 64] = np.exp(g * (127.0 - sv))[:, None]
                else:
                    dt_[d, hp, :, e, :] = np.where(s >= t, np.exp(g * np.maximum(s - t, 0)), 0)
                    qd[d, hp, 64 * e:64 * e + 64, :] = np.exp(g * (128.0 - sv))[None, :]
                    kd[d, hp, :, 64 * e:64 * e + 64] = np.exp(g * sv)[:, None]
                cd[64 * e:64 * e + 64, d * 4 + hp] = np.exp(g * 128.0)
    c["dT"] = dt_.reshape(8, 128, 256)
    c["qd"] = qd.reshape(8, 128, 128)
    c["kd"] = kd.reshape(8, 128, 128)
    c["cd"] = cd
    return c


CONST_SHAPES = {"ident": [128, 128], "onesr": [128, 128], "bo": [128, 128], "pm": [128, 128], "mask2": [2, 128, 256],
                "mmask": [2, 128, 128], "rmask": [128, 512], "cosT": [128, T], "ssinT": [128, T],
                "dT": [8, 128, 256], "qd": [8, 128, 128], "kd": [8, 128, 128], "cd": [128, 8]}
WEIGHT_SHAPES = {"w_ada": [2, 1024, 6144], "w_in": [2, 1024, NIN], "w_vres_down": [1, 1024, 32],
                 "w_up": [2, 2, 64, 512], "a_up": [2, 2, 64, 512], "g_up": [2, 160, 512], "v_up": [1, 32, 512],
                 "w_out": [2, 1024, 1024], "w_ffn_in": [2, 1024, 2 * HID], "w_ffn_out": [2, HID, 1024]}


class Reg:
    __slots__ = ("w", "r", "excl")

    def __init__(self, excl=False):
        self.w = None
        self.r = {}
        self.excl = excl


class V:
    def __init__(self, ap, g):
        self.ap = ap
        self.g = g if isinstance(g, list) else [g]

    def __getitem__(self, idx):
        return V(self.ap[idx], self.g)

    def rr(self, pat, **kw):
        return V(self.ap.rearrange(pat, **kw), self.g)

    def bc(self, shape):
        return V(self.ap.to_broadcast(shape), self.g)

    def bitcast(self, dt):
        return V(self.ap.bitcast(dt), self.g)


class KB:
    def __init__(self, nc, es):
        self.nc = nc
        self.es = es
        self.eng = {"pe": nc.tensor, "dve": nc.vector, "act": nc.scalar, "pool": nc.gpsimd, "sp": nc.sync}
        self.sem = {e: es.enter_context(nc.semaphore("s_" + e)) for e in self.eng}
        self.cnt = {e: 0 for e in self.eng}
        self.seen = {e: {} for e in self.eng}
        self.dsem = [es.enter_context(nc.semaphore("d%d" % i)) for i in range(NDS)]
        self.dcnt = [0] * NDS
        self.dnext = 0
        self.nins = 0

    def sb(self, name, shape, dt=F32):
        t = self.es.enter_context(self.nc.sbuf_tensor(name, list(shape), dt))
        return V(t[:], Reg())

    def _wait(self, e, ev):
        key, sem, val = ev
        if self.seen[e].get(key, 0) >= val:
            return
        if key == e and (e == "pe" or not SAME):
            return
        self.eng[e].wait_ge(sem, val)
        self.seen[e][key] = val
        self.nins += 1

    def _deps(self, e, reads, writes):
        for r in reads:
            if r.w is not None:
                self._wait(e, r.w)
        for w in writes:
            if w.w is not None:
                self._wait(e, w.w)
            for ev in list(w.r.values()):
                self._wait(e, ev)

    def _commit(self, ev, reads, writes):
        for r in reads:
            old = r.r.get(ev[0])
            if old is None or old[2] < ev[2]:
                r.r[ev[0]] = ev
        for w in writes:
            w.w = ev
            w.r = {}

    def op(self, e, fn, reads, writes):
        ex = [r for r in reads if r.excl]
        if ex:
            writes = list(writes) + ex
        self._deps(e, reads, writes)
        ins = fn(self.eng[e])
        self.cnt[e] += 1
        ins.then_inc(self.sem[e], 1)
        self.nins += 1
        self._commit((e, self.sem[e], self.cnt[e]), reads, writes)

    def dma(self, q, out, in_):
        reads, writes = in_.g, out.g
        i = self.dnext
        self.dnext = (i + 1) % NDS
        key = ("d", i)
        if self.dcnt[i] > 0:
            self._wait(q, (key, self.dsem[i], self.dcnt[i]))
        self._deps(q, reads, writes)
        self.dcnt[i] += 16
        self.eng[q].dma_start(out=out.ap, in_=in_.ap).then_inc(self.dsem[i], 16)
        self.nins += 1
        self._commit((key, self.dsem[i], self.dcnt[i]), reads, writes)

    def wait_all(self, e, regs):
        self._deps(e, regs, [])

    def tt(self, e, out, in0, in1, op):
        self.op(e, lambda E: E.tensor_tensor(out=out.ap, in0=in0.ap, in1=in1.ap, op=op), in0.g + in1.g, out.g)

    def ts(self, e, out, in0, s1, s2=None, op0=ALU.mult, op1=None):
        rd = list(in0.g)
        a1 = s1
        a2 = s2
        if isinstance(s1, V):
            rd += s1.g
            a1 = s1.ap
        if isinstance(s2, V):
            rd += s2.g
            a2 = s2.ap
        if op1 is None:
            self.op(e, lambda E: E.tensor_scalar(out=out.ap, in0=in0.ap, scalar1=a1, scalar2=None, op0=op0), rd, out.g)
        else:
            self.op(e, lambda E: E.tensor_scalar(out=out.ap, in0=in0.ap, scalar1=a1, scalar2=a2, op0=op0, op1=op1), rd, out.g)

    def stt(self, out, in0, sc, in1, op0, op1):
        rd = in0.g + in1.g
        a = sc
        if isinstance(sc, V):
            rd = rd + sc.g
            a = sc.ap
        self.op("dve", lambda E: E.scalar_tensor_tensor(out=out.ap, in0=in0.ap, scalar=a, in1=in1.ap, op0=op0, op1=op1), rd, out.g)

    def act(self, out, in_, func, bias=None, scale=1.0):
        rd = list(in_.g)
        kw = {}
        if isinstance(bias, V):
            rd += bias.g
            kw["bias"] = bias.ap
        elif bias is not None:
            kw["bias"] = bias
        if isinstance(scale, V):
            rd += scale.g
            kw["scale"] = scale.ap
        else:
            kw["scale"] = scale
        self.op("act", lambda E: E.activation(out=out.ap, in_=in_.ap, func=func, **kw), rd, out.g)

    def cp(self, e, out, in_):
        if e == "act":
            self.act(out, in_, AF.Copy)
        else:
            self.op(e, lambda E: E.tensor_copy(out=out.ap, in_=in_.ap), in_.g, out.g)

    def memset(self, e, out, val):
        self.op(e, lambda E: E.memset(out.ap, val), [], out.g)

    def recip(self, out, in_):
        self.op("dve", lambda E: E.reciprocal(out=out.ap, in_=in_.ap), in_.g, out.g)

    def mm(self, out, lhsT, rhs, start=True, stop=True):
        self.op("pe", lambda E: E.matmul(out.ap, lhsT=lhsT.ap, rhs=rhs.ap, start=start, stop=stop), lhsT.g + rhs.g, out.g)

    def tr(self, out, in_, ident):
        self.op("pe", lambda E: E.transpose(out.ap, in_.ap, ident.ap), in_.g + ident.g, out.g)

    def scan(self, out, d0, d1, init, op0, op1):
        self.op("dve", lambda E: E.tensor_tensor_scan(out=out.ap, data0=d0.ap, data1=d1.ap, initial=init, op0=op0, op1=op1),
                d0.g + d1.g, out.g)


def build(nb, nlayers=DEPTH, dbg=()):
    nc = bass.Bass("TRN2", target_bir_lowering=False)
    es = ExitStack()
    k = KB(nc, es)

    def din(name, shape):
        return V(nc.dram_tensor(name, list(shape), F32, kind="ExternalInput").ap(), Reg())

    xT_d = din("xT", [nb, D, SEQ])
    ctxT_d = din("ctxT", [nb, D, NCTX])
    cT_d = din("cT", [D, 5])
    sp_d = din("sp", [128, NSP])
    CD = {n: din("c_" + n, s) for n, s in CONST_SHAPES.items()}
    WD = {n: din(n, s) for n, s in WEIGHT_SHAPES.items()}
    outT_d = V(nc.dram_tensor("outT", [nb, D, SEQ], F32, kind="ExternalOutput").ap(), Reg())
    pbuf_ap = nc.dram_tensor("pbuf", [NPT, 128, T], F32, kind="Internal").ap()
    pbuf = [V(pbuf_ap[i], Reg()) for i in range(NPT)]
    vf_ap = nc.dram_tensor("vfirst", [4, 128, T], F32, kind="Internal").ap()
    vfd = [V(vf_ap[i], Reg()) for i in range(4)]
    dbg_out = {}

    def dump(name, v, shape, q="sp"):
        if name in dbg:
            o = V(nc.dram_tensor("dbg_" + name, list(shape), F32, kind="ExternalOutput").ap(), Reg())
            dbg_out[name] = o
            k.dma(q, o, v)

    xT_t = es.enter_context(nc.sbuf_tensor("xT_sb", [128, 8 * T], F32))
    xT = V(xT_t[:].rearrange("p (c t) -> p c t", c=8), Reg())
    xsp_d = V(nc.dram_tensor("xspill", [128, 8, T], F32, kind="Internal").ap(), Reg())
    hT = k.sb("hT_sb", [128, 8, T], BF16)
    psall = es.enter_context(nc.psum_tensor("ps", [128, 8, 512], F32))
    PSR = [Reg(excl=True) for _ in range(8)]

    def PS(b, lo=0, hi=512):
        return V(psall[:, b, lo:hi], PSR[b])

    def PS2(b0, lo, hi):
        return V(psall[:, b0:b0 + 2, lo:hi], [PSR[b0], PSR[b0 + 1]])

    spt = k.sb("spt", [128, NSP])
    k.dma("sp", spt, sp_d)

    def SPc(name, idx):
        return spt[:, SPL[name] + idx:SPL[name] + idx + 1]

    cst = {}
    for n in ["onesr", "bo", "rmask"]:
        cst[n] = k.sb("sc_" + n, CONST_SHAPES[n])
        k.dma("sp", cst[n], CD[n])
    for n in ["ident", "pm"]:
        cst[n] = k.sb("sc_" + n, CONST_SHAPES[n], BF16)
        k.dma("pool", cst[n], CD[n])
    cst["mask2"] = k.sb("sc_mask2", [128, 2, 256])
    cst["mmask"] = k.sb("sc_mmask", [128, 2, 128])
    for d in range(2):
        k.dma("sp", cst["mask2"][:, d, :], CD["mask2"][d])
        k.dma("sp", cst["mmask"][:, d, :], CD["mmask"][d])
    cst["cd"] = k.sb("sc_cd", [128, 8])
    k.dma("sp", cst["cd"], CD["cd"])
    WAup = k.sb("WAup", [128, 2, 2, 512], BF16)
    G0up = k.sb("G0up", [128, 2, 512], BF16)
    GVup = k.sb("GVup", [64, 2, 512], BF16)
    for l in range(2):
        for d in range(2):
            k.dma("pool", WAup[0:64, l, d, :], WD["w_up"][l, d])
            k.dma("pool", WAup[64:128, l, d, :], WD["a_up"][l, d])
        k.dma("pool", G0up[:, l, :], WD["g_up"][l, 0:128, :])
        k.dma("pool", GVup[0:32, l, :], WD["g_up"][l, 128:160, :])
    k.dma("pool", GVup[32:64, 1, :], WD["v_up"][0])
    omm = k.sb("omm", [128, 30])
    mu0 = spt[:, SPL["mu"]:SPL["mu"] + 210].rr("p (t s) -> p t s", s=7)[:, :, 0]
    k.ts("dve", omm, mu0, -1.0, 1.0, ALU.mult, ALU.add)
    omka = k.sb("omka", [128, 8])
    k.ts("dve", omka, spt[:, SPL["ka"]:SPL["ka"] + 8], -1.0, 1.0, ALU.mult, ALU.add)

    ARW = 15360
    arena = es.enter_context(nc.sbuf_tensor("arena", [128, ARW], F32))

    class Phase:
        def __init__(self, extra=None):
            self.off = 0
            self.regs = []
            self.backs = [(arena, ARW)] + ([extra] if extra is not None else [])
            self.bi = 0

        def sb(self, shape, dt=F32):
            n = 1
            for s_ in shape[1:]:
                n *= s_
            words = n if dt == F32 else (n + 1) // 2
            words = (words + 7) // 8 * 8
            if self.off + words > self.backs[self.bi][1]:
                self.bi += 1
                self.off = 0
                assert self.bi < len(self.backs), "arena overflow"
                assert words <= self.backs[self.bi][1]
            ap = self.backs[self.bi][0][0:shape[0], self.off:self.off + words]
            self.off += words
            if dt != F32:
                ap = ap.bitcast(dt)
            ap = ap[:, 0:n]
            if len(shape) > 2:
                names = " ".join("d%d" % i for i in range(len(shape) - 1))
                ap = ap.rearrange("p (%s) -> p %s" % (names, names), **{"d%d" % i: shape[i + 1] for i in range(len(shape) - 1)})
            r = Reg()
            self.regs.append(r)
            return V(ap, r)

        def close(self):
            for e in ("pe", "dve", "act", "pool", "sp"):
                k._deps(e, [], self.regs)

    scT = k.sb("scT", [128, 8, 5])
    k.dma("sp", scT, cT_d.rr("(c p) w -> p c w", p=128))
    k.act(scT, scT, AF.Silu)
    ada = [k.sb("ada%d" % l, [128, 48, 5]) for l in range(2)]
    ph0 = Phase()
    wadab = [ph0.sb([128, 8, 512]) for i in range(2)]
    it = 0
    for l in range(nlayers):
        for qg in range(12):
            wb = wadab[it % 2]
            it += 1
            k.dma("sp", wb, WD["w_ada"][l].rr("(c p) n -> p c n", p=128)[:, :, qg * 512:(qg + 1) * 512])
            for qq in range(4):
                q = qg * 4 + qq
                for c in range(8):
                    k.mm(PS(0, q * 5, q * 5 + 5), wb[:, c, qq * 128:(qq + 1) * 128], scT[:, c, :], start=(c == 0), stop=(c == 7))
        k.tt("dve", ada[l], PS(0, 0, 240).rr("p (q w) -> p q w", w=5),
             spt[:, SPL["bada"] + l * 48:SPL["bada"] + (l + 1) * 48].rr("p (q o) -> p q o", o=1).bc([128, 48, 5]), ALU.add)
    ph0.close()
    A1 = [k.sb("A1_%d" % l, [128, 8, 5]) for l in range(2)]
    A2 = [k.sb("A2_%d" % l, [128, 8, 5]) for l in range(2)]
    for l in range(nlayers):
        for (A, nm, q0) in ((A1, "norm1", 8), (A2, "norm2", 32)):
            k.ts("dve", A[l], ada[l][:, q0:q0 + 8, :], 1.0, None, ALU.add)
            k.tt("dve", A[l], A[l], spt[:, SPL[nm] + l * 8:SPL[nm] + l * 8 + 8].rr("p (c o) -> p c o", o=1).bc([128, 8, 5]), ALU.mult)
    zero_col = k.sb("zero_col", [128, 1])
    k.memset("dve", zero_col, 0.0)
    eps_n = k.sb("eps_n", [128, 1])
    k.memset("dve", eps_n, 1e-6)
    eps_r = k.sb("eps_r", [128, 1])
    k.memset("dve", eps_r, 64e-5)
    eps_t = k.sb("eps_t", [128, 1])
    k.memset("dve", eps_t, 1e-5)

    def norm(ph, scale_fn, bias_fn, out_fn, blocks):
        sq = [ph.sb([128, 512]) for i in range(2)]
        rstd = ph.sb([128, 512])
        ntmp = [ph.sb([128, 512]) for i in range(2)]
        n = 0
        for (t0, t1) in blocks:
            W = t1 - t0
            for c in range(8):
                s = sq[n % 2]
                n += 1
                k.act(s[:, :W], xT[:, c, t0:t1], AF.Square)
                k.mm(PS(0, 0, W), cst["onesr"], s[:, :W], start=(c == 0), stop=(c == 7))
            k.act(rstd[:, :W], PS(0, 0, W), AF.Sqrt, bias=eps_n)
            k.recip(rstd[:, :W], rstd[:, :W])
            for c in range(8):
                tmp = ntmp[c % 2]
                k.tt("dve", tmp[:, :W], xT[:, c, t0:t1], rstd[:, :W], ALU.mult)
                b_ = bias_fn(c, t0 < NCTX)
                k.act(out_fn(c, t0, t1), tmp[:, :W], AF.Identity, bias=(b_ if b_ is not None else zero_col), scale=scale_fn(c, t0 < NCTX))

    def who_idx(is_ctx, b):
        return 4 if is_ctx else b

    RNG = [(256 * i, 256 * (i + 1)) for i in range(9)]
    RW = 256

    for b in range(nb):
        for c in range(8):
            k.dma("sp", xT[:, c, NCTX:T], xT_d[b, c * 128:(c + 1) * 128, :])
            k.dma("sp", xT[:, c, 0:NCTX], ctxT_d[b, c * 128:(c + 1) * 128, :])
        for l in range(nlayers):
            last = (l == DEPTH - 1)
            ph = Phase()
            norm(ph, lambda c, ic: A1[l][:, c, who_idx(ic, b):who_idx(ic, b) + 1],
                 lambda c, ic: ada[l][:, 0 + c, who_idx(ic, b):who_idx(ic, b) + 1],
                 lambda c, t0, t1: hT[:, c, t0:t1], BLK)
            ph.close()
            if ("hT_%d_%d" % (b, l)) in dbg:
                o_ = V(nc.dram_tensor("dbg_hT_%d_%d" % (b, l), [128, 8, T], BF16, kind="ExternalOutput").ap(), Reg())
                dbg_out["hT_%d_%d" % (b, l)] = o_
                k.dma("sp", o_, hT)
            ph = Phase()
            ubuf = [ph.sb([128, T]) for i in range(2)]
            lbuf = [ph.sb([128, T]) for i in range(1)]
            wg = [ph.sb([128, 8, 512], BF16) for i in range(2)]
            groups = [(0, 512), (512, 512), (1024, 512), (1536, 288)] + [(1824 + 512 * i, 512) for i in range(5)]
            ti = 0
            gi = 0
            for (c0, ncol) in groups:
                wgb = wg[gi % 2]
                gi += 1
                k.dma("pool", wgb[:, :, 0:ncol], WD["w_in"][l].rr("(c p) n -> p c n", p=128)[:, :, c0:c0 + ncol])
                if c0 == 1536 and l == 1:
                    k.dma("pool", wgb[:, :, 288:320], WD["w_vres_down"][0].rr("(c p) n -> p c n", p=128))
                if c0 == 1536:
                    tl = [(0, 128), (128, 128), (256, 64 if l == 1 else 32)]
                else:
                    tl = [(i * 128, 128) for i in range(4)]
                for (off, M) in tl:
                    is_rwkv = ti < 15
                    u = ubuf[ti % 2]
                    o = lbuf[0]
                    dst = u if is_rwkv else o
                    for bi, (t0, t1) in enumerate(BLK):
                        W = t1 - t0
                        pb = (ti * 5 + bi) % 8
                        for c in range(8):
                            k.mm(PS(pb, 0, W)[0:M], wgb[:, c, off:off + M], hT[:, c, t0:t1], start=(c == 0), stop=(c == 7))
                        k.cp("act" if (bi % 2 == 0) else "dve", dst[0:M, t0:t1], PS(pb, 0, W)[0:M])
                    if is_rwkv:
                        mub = SPL["mu"] + (l * 15 + ti) * 7
                        k.act(o[0:M, :], u[0:M, :], AF.Copy, scale=omm[0:M, l * 15 + ti:l * 15 + ti + 1])
                        cols = rwkv_tile_cols(l)[ti]
                        ldirs = sorted(set((c_[1] // 8) if isinstance(c_, tuple) else (c_ // 456) for c_ in cols))
                        cdirs = sorted(set((c_[1] // 16) if isinstance(c_, tuple) else (c_ // 912) for c_ in cols))
                        uL = u[0:M, NCTX:T].rr("p (r c) -> p r c", c=64)
                        oL = o[0:M, NCTX:T].rr("p (r c) -> p r c", c=64)
                        for dr in ldirs:
                            m = spt[0:M, mub + 1 + dr:mub + 2 + dr]
                            if dr == 0:
                                k.stt(oL[:, :, 1:64], uL[:, :, 0:63], m, oL[:, :, 1:64], ALU.mult, ALU.add)
                            elif dr == 1:
                                k.stt(oL[:, :, 0:63], uL[:, :, 1:64], m, oL[:, :, 0:63], ALU.mult, ALU.add)
                            elif dr == 2:
                                k.stt(o[0:M, NCTX + 64:T], u[0:M, NCTX:T - 64], m, o[0:M, NCTX + 64:T], ALU.mult, ALU.add)
                            else:
                                k.stt(o[0:M, NCTX:T - 64], u[0:M, NCTX + 64:T], m, o[0:M, NCTX:T - 64], ALU.mult, ALU.add)
                        for dr in cdirs:
                            m = spt[0:M, mub + 5 + dr:mub + 6 + dr]
                            if dr == 0:
                                k.stt(o[0:M, 1:NCTX], u[0:M, 0:NCTX - 1], m, o[0:M, 1:NCTX], ALU.mult, ALU.add)
                            else:
                                k.stt(o[0:M, 0:NCTX - 1], u[0:M, 1:NCTX], m, o[0:M, 0:NCTX - 1], ALU.mult, ALU.add)
                        if ti == 12:
                            k.act(o[0:64, :], o[0:64, :], AF.Tanh)
                        elif ti == 13:
                            k.act(o[0:128, :], o[0:128, :], AF.Sigmoid)
                        elif ti == 14:
                            k.act(o[0:32, :], o[0:32, :], AF.Sigmoid)
                    k.dma("sp", pbuf[ti][0:M, :], o[0:M, :])
                    if l == 0 and 8 <= ti < 12:
                        k.dma("sp", vfd[ti - 8], o[0:M, :])
                    ti += 1
            assert ti == NPT
            ph.close()
            if ("pbuf_%d_%d" % (b, l)) in dbg:
                o_ = V(nc.dram_tensor("dbg_pbuf_%d_%d" % (b, l), [NPT, 128, T], F32, kind="ExternalOutput").ap(), Reg())
                dbg_out["pbuf_%d_%d" % (b, l)] = o_
                for i_ in range(NPT):
                    k.dma("sp", o_[i_], pbuf[i_])
            if "stop_p2" in dbg:
                break
            k.dma("sp", xsp_d, xT)
            for e_ in ("pe", "dve", "act", "pool", "sp"):
                k._deps(e_, [], xT.g)
            ph = Phase(extra=(xT_t, 8 * T))
            G0s = ph.sb([128, RW], BF16)
            G1s = ph.sb([64, RW], BF16)
            ybuf = ph.sb([128, T])
            oT = hT
            W = RW
            nch = 2

            class St:
                pass

            def alloc_stream(d, kind):
                B = St()
                B.d = d
                B.fb = [ph.sb([128, RW]) for i in range(12)]
                B.hb = [ph.sb([128, RW], BF16) for i in range(6)]
                B.Ktok = ph.sb([128, 2, 128], BF16)
                B.Vtok = ph.sb([128, 2, 128], BF16)
                B.Sst = ph.sb([128, 64])
                B.Sbf = ph.sb([128, 64], BF16)
                if kind == "rwkv":
                    B.WAs = ph.sb([128, RW], BF16)
                    B.ARb = ph.sb([128, 2, 2, 128], BF16)
                    B.Btok = ph.sb([128, 2, 128], BF16)
                    B.gC = ph.sb([128, 2])
                    B.Am = ph.sb([128, 2, 2, 256], BF16)
                    B.NMb = [ph.sb([128, 2, 2, 128]) for i in range(2)]
                    B.Ppb = [ph.sb([128, 2, 128]) for i in range(2)]
                    B.Xsb = ph.sb([128, 2, 64])
                    B.Usb = ph.sb([128, 2, 64], BF16)
                else:
                    B.scm = ph.sb([128, 2, 128], BF16)
                    B.dTt = ph.sb([128, 256])
                    B.qdt = ph.sb([128, 128])
                    B.kdt = ph.sb([128, 128])
                B.off = 4 * d
                return B
            SB = [alloc_stream(0, "rwkv"), alloc_stream(1, "rwkv")]
            SBr = [alloc_stream(0, "ret")]
            ybuf2 = ph.sb([128, T])

            def v3(x):
                return x.rr("p (c t) -> p c t", t=128)

            def run_streams(gens):
                gens = list(gens)
                while gens:
                    for g in list(gens):
                        try:
                            next(g)
                        except StopIteration:
                            gens.remove(g)

            def gn_block(src, wcol, bcol, eps_col, out_f, pa, pb_, gnb, region=None):
                cen, sqv, rs = gnb
                ra = region if region is not None else PS(pa, 0, W)
                rb = region if region is not None else PS(pb_, 0, W)
                k.mm(ra, cst["bo"], src, True, True)
                k.stt(cen, ra, -1.0 / 64, src, ALU.mult, ALU.add)
                k.act(sqv, cen, AF.Square)
                k.mm(rb, cst["bo"], sqv, True, True)
                k.act(rs, rb, AF.Sqrt, bias=eps_col, scale=1.0 / 64)
                k.recip(rs, rs)
                k.tt("dve", cen, cen, rs, ALU.mult)
                k.act(out_f, cen, AF.Identity, bias=bcol, scale=wcol)

            def transposes(src_bf, dst_tok, bank, mul_tab=None, region=None):
                pt = (region if region is not None else PS(bank)).bitcast(BF16)
                for c in range(nch):
                    k.tr(pt[:, c * 128:(c + 1) * 128], src_bf[:, c * 128:(c + 1) * 128], cst["ident"])
                ptv = pt[:, 0:nch * 128].rr("p (c f) -> p c f", f=128)
                if mul_tab is None:
                    k.cp("act", dst_tok, ptv)
                else:
                    k.tt("dve", dst_tok, ptv, mul_tab.rr("p (o f) -> p o f", o=1).bc([128, nch, 128]), ALU.mult)

            def rwkv_pass(hp, B):
                d = B.d
                hc = slice(hp * 128, (hp + 1) * 128)
                fb, hb = B.fb, B.hb
                ARb, Btok, Ktok, Vtok, gC, Am, NMb, Ppb = B.ARb, B.Btok, B.Ktok, B.Vtok, B.gC, B.Am, B.NMb, B.Ppb
                Xsb, Usb, Sst, Sbf, WAs = B.Xsb, B.Usb, B.Sst, B.Sbf, B.WAs

                def P(role, lo=0, hi=512):
                    return PS((role + B.off) % 8, lo, hi)

                def P2(role, lo, hi):
                    return PS2((role + B.off) % 8, lo, hi)
                order = list(range(9)) if d == 0 else [0] + list(range(8, 0, -1))
                k.memset("dve", Sst, 0.0)
                k.memset("dve", Sbf, 0.0)
                for ri in order:
                    t0, t1 = RNG[ri]
                    rF, kF, vF = fb[0], fb[1], fb[2]
                    k.dma("sp", rF, pbuf[0 + hp][:, t0:t1])
                    k.dma("sp", kF, pbuf[4 + hp][:, t0:t1])
                    k.dma("sp", vF, pbuf[8 + hp][:, t0:t1])
                    k.dma("pool", WAs, pbuf[12][:, t0:t1])
                    sw, Lr, Lin, Lex, icl, kk, t1b, t2b, Eb = fb[3], fb[4], fb[5], fb[6], fb[7], fb[8], fb[9], fb[10], fb[11]
                    k.mm(P(0, 0, W), WAup[0:64, l, d, hc], WAs[0:64, :])
                    k.mm(P(1, 0, W), WAup[64:128, l, d, hc], WAs[64:128, :])
                    yield
                    k.act(sw, P(0, 0, W), AF.Sigmoid, bias=SPc("w0", (l * 2 + d) * 4 + hp))
                    k.act(icl, P(1, 0, W), AF.Sigmoid, bias=SPc("a0", (l * 2 + d) * 4 + hp))
                    k.ts("dve", kk, kF, SPc("kk", l * 4 + hp), None, ALU.mult)
                    k.act(t1b, kk, AF.Square)
                    k.mm(P(0, 0, W), cst["bo"], t1b)
                    yield
                    k.scan(Lr, cst["rmask"][:, :W], sw, 0.0, ALU.mult, ALU.add)
                    totb = v3(Lr)[:, :, 127:128].bc([128, nch, 128])
                    if d == 0:
                        k.cp("dve", Lin, Lr)
                    else:
                        k.tt("dve", t2b, sw, Lr, ALU.subtract)
                        k.tt("dve", v3(Lin), v3(t2b), totb, ALU.add)
                    k.tt("dve", Lex, Lin, sw, ALU.subtract)
                    k.act(gC, v3(Lr)[:, :, 127], AF.Exp, scale=-CDEC)
                    k.ts("dve", t2b, P(0, 0, W), 1e-24, None, ALU.max)
                    k.act(t2b, t2b, AF.Sqrt)
                    yield
                    k.recip(t2b, t2b)
                    k.tt("dve", kk, kk, t2b, ALU.mult)
                    k.act(Eb, Lex, AF.Exp, scale=-CDEC)
                    yield
                    k.stt(ARb[:, :, 0, :], v3(kk), -1.0, v3(Eb), ALU.mult, ALU.mult)
                    k.act(Eb, Lin, AF.Exp, scale=-CDEC)
                    yield
                    k.tt("dve", ARb[:, :, 1, :], v3(rF), v3(Eb), ALU.mult)
                    ktil, bp = t1b, t2b
                    k.ts("dve", ktil, icl, SPc("ka", l * 4 + hp), omka[:, l * 4 + hp:l * 4 + hp + 1], ALU.mult, ALU.add)
                    k.tt("dve", ktil, ktil, kF, ALU.mult)
                    k.tt("dve", bp, kk, icl, ALU.mult)
                    BH, KH, bck, kck, vbf = hb[0], hb[1], hb[2], hb[3], hb[4]
                    k.act(Eb, Lin, AF.Exp, scale=CDEC)
                    yield
                    k.tt("dve", BH, bp, Eb, ALU.mult)
                    k.tt("dve", KH, ktil, Eb, ALU.mult)
                    k.tt("dve", v3(Lex), v3(Lin), totb, ALU.subtract)
                    k.act(Eb, Lex, AF.Exp, scale=CDEC)
                    yield
                    k.tt("dve", bck, bp, Eb, ALU.mult)
                    k.tt("dve", kck, ktil, Eb, ALU.mult)
                    k.cp("act", vbf, vF)
                    yield
                    transposes(bck, Btok, (4 + B.off) % 8)
                    yield
                    transposes(kck, Ktok, (5 + B.off) % 8)
                    yield
                    transposes(vbf, Vtok, (4 + B.off) % 8)
                    yield
                    corder = list(range(nch)) if d == 0 else list(range(nch - 1, -1, -1))
                    for c in corder:
                        cs = slice(c * 128, (c + 1) * 128)
                        for e in range(2):
                            Re = slice(64 * e, 64 * e + 64)
                            arv = ARb[Re, c, :, :].rr("p a t -> p (a t)")
                            k.mm(P(e, 0, 256), BH[Re, cs], arv)
                            k.mm(P(e, 256, 512), KH[Re, cs], arv)
                            k.mm(P(2 + e, 0, 128), ARb[Re, c, 0, :], BH[Re, cs])
                        yield
                        nm0 = NMb[0]
                        k.tt("dve", nm0[:, :, 0, :], P2(0, 0, 128), cst["mask2"][:, d, 0:128].rr("p (o t) -> p o t", o=1).bc([128, 2, 128]), ALU.mult)
                        k.tt("dve", nm0[:, :, 1, :], P2(2, 0, 128), cst["mmask"][:, d, :].rr("p (o t) -> p o t", o=1).bc([128, 2, 128]), ALU.mult)
                        k.tt("dve", Ppb[0], nm0[:, :, 0, :], cst["ident"].rr("p (o t) -> p o t", o=1).bc([128, 2, 128]), ALU.add)
                        for e in range(2):
                            k.tt("dve", Am[:, e, :, :], P(e).rr("p (a t) -> p a t", a=2),
                                 cst["mask2"][:, d, :].rr("p (o t) -> p o t", o=1).bc([128, 2, 256]), ALU.mult)
                        yield
                        cur = 0
                        pcur = 0
                        for itn in range(7):
                            nmc, nmn = NMb[cur], NMb[1 - cur]
                            pc, pn = Ppb[pcur], Ppb[1 - pcur]
                            def rc(x):
                                return x.bitcast(F32R) if DBL_R else x
                            for e in range(2):
                                if itn < 5:
                                    k.mm(P(4, e * 256, e * 256 + 128), rc(nmc[:, e, 1, :]), rc(nmc[:, e, 0, :]))
                                if itn < 6:
                                    k.mm(P(4, e * 256 + 128, e * 256 + 256), rc(nmc[:, e, 0, :]), rc(nmc[:, e, 1, :]))
                                if itn >= 1:
                                    k.mm(P(5, e * 128, e * 128 + 128), rc(nmc[:, e, 1, :]), rc(pc[:, e, :]))
                            yield
                            if itn < 5:
                                k.cp("act", nmn.rr("p e a t -> p (e a t)"), P(4))
                            elif itn == 5:
                                k.cp("act", nmn[:, :, 1, :], P(4).rr("p (e a t) -> p e a t", e=2, a=2)[:, :, 1, :])
                            if itn >= 1:
                                k.tt("dve", pn, P(5, 0, 256).rr("p (e t) -> p e t", e=2), pc, ALU.add)
                                pcur = 1 - pcur
                            yield
                            cur = 1 - cur
                        cur = pcur
                        TT = Ppb[cur]
                        for e in range(2):
                            Re = slice(64 * e, 64 * e + 64)
                            k.mm(P(2 + e, 128, 192), ARb[Re, c, 0, :], Sbf[Re, :], True, False)
                            k.mm(P(2 + e, 128, 192), Am[:, e, 1, 0:128], Vtok[:, c, Re], False, True)
                        yield
                        k.cp("act", Xsb, P2(2, 128, 192))
                        yield
                        for e in range(2):
                            k.mm(P(2 + e, 192, 256), TT[:, e, :], Xsb[:, e, :])
                        yield
                        k.cp("act", Usb, P2(2, 192, 256))
                        yield
                        for e in range(2):
                            Re = slice(64 * e, 64 * e + 64)
                            po = P(6 + e, 0, 128)[Re]
                            k.mm(po, Sbf[Re, :], ARb[Re, c, 1, :], True, False)
                            k.mm(po, Usb[:, e, :], Am[:, e, 0, 128:256], False, False)
                            k.mm(po, Vtok[:, c, Re], Am[:, e, 1, 128:256], False, True)
                        for e in range(2):
                            Ce = slice(64 * e, 64 * e + 64)
                            pd = P(5, 256, 320)[Ce]
                            k.mm(pd, Btok[:, c, Ce], Usb[:, e, :], True, False)
                            k.mm(pd, Ktok[:, c, Ce], Vtok[:, c, Ce], False, True)
                        yield
                        for e in range(2):
                            Re = slice(64 * e, 64 * e + 64)
                            po = P(6 + e, 0, 128)[Re]
                            yv = ybuf[Re, t0 + c * 128:t0 + (c + 1) * 128]
                            k.tt("dve", yv, yv, po, ALU.add)
                        k.stt(Sst, Sst, gC[:, c:c + 1], P(5, 256, 320), ALU.mult, ALU.add)
                        k.cp("act", Sbf, Sst)
                        yield

            def ret_pass(hp, B, d):
                fb, hb = B.fb, B.hb
                Ktok, Vtok, Sst, Sbf, scm, dTt, qdt, kdt = B.Ktok, B.Vtok, B.Sst, B.Sbf, B.scm, B.dTt, B.qdt, B.kdt

                order = list(range(9)) if d == 0 else [0] + list(range(8, 0, -1))
                k.dma("sp", dTt, CD["dT"][d * 4 + hp])
                k.dma("sp", qdt, CD["qd"][d * 4 + hp])
                k.dma("sp", kdt, CD["kd"][d * 4 + hp])
                k.memset("dve", Sst, 0.0)
                k.memset("dve", Sbf, 0.0)
                for ri in order:
                    t0, t1 = RNG[ri]
                    qF, kF, vF, gF, cosF, sinF = fb[0], fb[1], fb[2], fb[3], fb[4], fb[5]
                    k.dma("sp", qF, pbuf[15 + hp][:, t0:t1])
                    k.dma("sp", kF, pbuf[19 + hp][:, t0:t1])
                    k.dma("sp", vF, pbuf[23 + hp][:, t0:t1])
                    k.dma("sp", gF, pbuf[(27 if d == 0 else 31) + hp][:, t0:t1])
                    k.dma("sp", cosF, CD["cosT"][:, t0:t1])
                    k.dma("sp", sinF, CD["ssinT"][:, t0:t1])
                    qb, kb, qr, kr, qh, vbf = hb[0], hb[1], hb[2], hb[3], hb[4], hb[5]
                    t1b, t2b = fb[6], fb[7]
                    for (src, sb_, dstb, isq) in ((qF, qb, qr, True), (kF, kb, kr, False)):
                        k.cp("act", sb_, src)
                        yield
                        k.mm(PS(6, 256, 512), cst["pm"], sb_)
                        k.tt("dve", t1b, src, cosF, ALU.mult)
                        yield
                        k.tt("dve", t2b, PS(6, 256, 512), sinF, ALU.mult)
                        k.tt("dve", t1b, t1b, t2b, ALU.add)
                        if isq:
                            k.cp("act", dstb, t1b)
                            k.tt("dve", v3(qh), v3(t1b), qdt.rr("p (o t) -> p o t", o=1).bc([128, nch, 128]), ALU.mult)
                        else:
                            k.act(dstb, t1b, AF.Copy, scale=0.125)
                        yield
                    k.cp("act", vbf, vF)
                    yield
                    transposes(kr, Ktok, None, mul_tab=kdt, region=PS(7, 256, 384))
                    yield
                    transposes(vbf, Vtok, None, region=PS(7, 256, 384))
                    yield
                    orng = fb[8]
                    corder = list(range(nch)) if d == 0 else list(range(nch - 1, -1, -1))
                    for c in corder:
                        cs = slice(c * 128, (c + 1) * 128)
                        for e in range(2):
                            Re = slice(64 * e, 64 * e + 64)
                            k.mm(PS(2 + e, 256, 384), kr[Re, cs], qr[Re, cs])
                        yield
                        k.tt("dve", scm, PS2(2, 256, 384), dTt.rr("p (e t) -> p e t", e=2), ALU.mult)
                        yield
                        for e in range(2):
                            Re = slice(64 * e, 64 * e + 64)
                            po = PS(2 + e, 384, 512)[Re]
                            k.mm(po, Sbf[Re, :], qh[Re, cs], True, False)
                            k.mm(po, Vtok[:, c, Re], scm[:, e, :], False, True)
                        for e in range(2):
                            Ce = slice(64 * e, 64 * e + 64)
                            pd = PS(7, 384, 448)[Ce]
                            k.mm(pd, Ktok[:, c, Ce], Vtok[:, c, Ce], True, True)
                        yield
                        for e in range(2):
                            Re = slice(64 * e, 64 * e + 64)
                            k.cp("act", orng[Re, cs], PS(2 + e, 384, 512)[Re])
                        k.stt(Sst, Sst, cst["cd"][:, d * 4 + hp:d * 4 + hp + 1], PS(7, 384, 448), ALU.mult, ALU.add)
                        k.cp("act", Sbf, Sst)
                        yield
                    gnv = fb[9]
                    gn_block(orng, SPc("rgw", l * 4 + hp), SPc("rgb", l * 4 + hp), eps_t, gnv, None, None, (fb[10], fb[11], fb[6]), region=PS(6, 256, 512))
                    k.act(gF, gF, AF.Silu)
                    k.tt("dve", gnv, gnv, gF, ALU.mult)
                    k.tt("dve", ybuf2[:, t0:t1], ybuf2[:, t0:t1], gnv, ALU.add)
                    yield

            for hp in range(4 if "skip_rwkv" not in dbg else 0):
                hc = slice(hp * 128, (hp + 1) * 128)
                fb = SB[0].fb
                if l == 1:
                    for (t0, t1) in RNG:
                        vF, vfF, sg = fb[0], fb[1], fb[2]
                        k.dma("sp", vF, pbuf[8 + hp][:, t0:t1])
                        k.dma("sp", vfF, vfd[hp][:, t0:t1])
                        k.dma("pool", G1s, pbuf[14][0:64, t0:t1])
                        k.mm(PS(0, 0, W), GVup[32:64, 1, hc], G1s[32:64, :])
                        k.act(sg, PS(0, 0, W), AF.Sigmoid, bias=SPc("v0", hp))
                        k.tt("dve", vfF, vfF, vF, ALU.subtract)
                        k.tt("dve", vfF, vfF, sg, ALU.mult)
                        k.tt("dve", vF, vF, vfF, ALU.add)
                        k.dma("sp", pbuf[8 + hp][:, t0:t1], vF)
                k.memset("dve", ybuf, 0.0)
                k.memset("dve", ybuf2, 0.0)
                def ret_both(hp_):
                    yield from ret_pass(hp_, SBr[0], 0)
                    yield from ret_pass(hp_, SBr[0], 1)
                gl = []
                if "only_ret" not in dbg:
                    gl += [rwkv_pass(hp, SB[0]), rwkv_pass(hp, SB[1])]
                if "skip_ret" not in dbg:
                    gl += [ret_both(hp)]
                run_streams(gl)
                k.cp("act", oT[:, 4 + hp, :], ybuf2)
                for (t0, t1) in RNG:
                    rF, kF, vF = fb[0], fb[1], fb[2]
                    k.dma("sp", rF, pbuf[0 + hp][:, t0:t1])
                    k.dma("sp", kF, pbuf[4 + hp][:, t0:t1])
                    k.dma("sp", vF, pbuf[8 + hp][:, t0:t1])
                    k.dma("pool", G0s, pbuf[13][:, t0:t1])
                    k.dma("pool", G1s, pbuf[14][0:64, t0:t1])
                    k.stt(rF, rF, SPc("rk", l * 4 + hp), kF, ALU.mult, ALU.mult)
                    k.mm(PS(2, 0, W), cst["bo"], rF)
                    k.tt("dve", vF, vF, PS(2, 0, W), ALU.mult)
                    gnv = fb[3]
                    gn_block(ybuf[:, t0:t1], SPc("lnw", l * 4 + hp), SPc("lnb", l * 4 + hp), eps_r, gnv, 0, 1, (fb[9], fb[10], fb[11]))
                    k.tt("dve", gnv, gnv, vF, ALU.add)
                    k.mm(PS(3, 0, W), G0up[:, l, hc], G0s, True, False)
                    k.mm(PS(3, 0, W), GVup[0:32, l, hc], G1s[0:32, :], False, True)
                    k.tt("dve", oT[:, hp, t0:t1], gnv, PS(3, 0, W), ALU.mult)
            ph.close()
            k.dma("sp", xT, xsp_d)
            if ("oT_%d_%d" % (b, l)) in dbg:
                o_ = V(nc.dram_tensor("dbg_oT_%d_%d" % (b, l), [128, 8, T], BF16, kind="ExternalOutput").ap(), Reg())
                dbg_out["oT_%d_%d" % (b, l)] = o_
                k.dma("sp", o_, oT)
            if "stop_p3" in dbg:
                break
            ph = Phase()
            wob = ph.sb([128, 8, 1024], BF16)
            k.dma("pool", wob, WD["w_out"][l].rr("(c p) n -> p c n", p=128))
            n = 0
            for m in range(8):
                for (t0, t1) in BLK:
                    W = t1 - t0
                    if last and t0 < NCTX:
                        continue
                    pb = n % 8
                    n += 1
                    for kc in range(8):
                        k.mm(PS(pb, 0, W), wob[:, kc, m * 128:(m + 1) * 128], oT[:, kc, t0:t1], start=(kc == 0), stop=(kc == 7))
                    wi = who_idx(t0 < NCTX, b)
                    k.stt(xT[:, m, t0:t1], PS(pb, 0, W), ada[l][:, 16 + m, wi:wi + 1], xT[:, m, t0:t1], ALU.mult, ALU.add)
            ph.close()
            ph = Phase()
            norm(ph, lambda c, ic: A2[l][:, c, who_idx(ic, b):who_idx(ic, b) + 1],
                 lambda c, ic: ada[l][:, 24 + c, who_idx(ic, b):who_idx(ic, b) + 1],
                 lambda c, t0, t1: hT[:, c, t0:t1], BLK[1:] if last else BLK)
            ph.close()
            ph = Phase()
            wgf = [ph.sb([128, 8, 512], BF16) for i in range(2)]
            actb = ph.sb([128, NJ, 512], BF16)
            wfo = [ph.sb([128, NJ, 128], BF16) for i in range(2)]
            sgb = [ph.sb([128, 512]) for i in range(2)]
            wfi_v = WD["w_ffn_in"][l].rr("(c p) n -> p c n", p=128)
            wfo_v = WD["w_ffn_out"][l].rr("(j p) n -> p j n", p=128)
            wi_ = 0
            for (t0, t1) in BLK:
                W = t1 - t0
                if (last and t0 < NCTX) or "skip_ffn" in dbg:
                    continue
                for jg in range(6):
                    nj = 4 if jg < 5 else 2
                    wgate = wgf[0]
                    wup = wgf[1]
                    k.dma("pool", wgate[:, :, 0:nj * 128], wfi_v[:, :, jg * 512:jg * 512 + nj * 128])
                    k.dma("pool", wup[:, :, 0:nj * 128], wfi_v[:, :, HID + jg * 512:HID + jg * 512 + nj * 128])
                    for jj in range(nj):
                        j = jg * 4 + jj
                        pg = (2 * j) % 8
                        pu = (2 * j + 1) % 8
                        for c in range(8):
                            k.mm(PS(pg, 0, W), wgate[:, c, jj * 128:(jj + 1) * 128], hT[:, c, t0:t1], start=(c == 0), stop=(c == 7))
                        for c in range(8):
                            k.mm(PS(pu, 0, W), wup[:, c, jj * 128:(jj + 1) * 128], hT[:, c, t0:t1], start=(c == 0), stop=(c == 7))
                        sgt = sgb[j % 2]
                        k.act(sgt[:, :W], PS(pg, 0, W), AF.Silu)
                        k.tt("dve", actb[:, j, :W], sgt[:, :W], PS(pu, 0, W), ALU.mult)
                for m in range(8):
                    wf = wfo[wi_ % 2]
                    wi_ += 1
                    k.dma("pool", wf, wfo_v[:, :, m * 128:(m + 1) * 128])
                    pb = m % 8
                    for j in range(NJ):
                        k.mm(PS(pb, 0, W), wf[:, j, :], actb[:, j, :W], start=(j == 0), stop=(j == NJ - 1))
                    wi = who_idx(t0 < NCTX, b)
                    k.stt(xT[:, m, t0:t1], PS(pb, 0, W), ada[l][:, 40 + m, wi:wi + 1], xT[:, m, t0:t1], ALU.mult, ALU.add)
            ph.close()
            if ("xT_%d_%d" % (b, l)) in dbg:
                o_ = V(nc.dram_tensor("dbg_xT_%d_%d" % (b, l), [128, 8, T], F32, kind="ExternalOutput").ap(), Reg())
                dbg_out["xT_%d_%d" % (b, l)] = o_
                k.dma("sp", o_, xT)
        if "stop_p2" in dbg or "stop_p3" in dbg:
            continue
        ph = Phase()
        obuf = [ph.sb([128, T]) for i in range(2)]
        rst_all = ph.sb([128, T])
        sq = [ph.sb([128, 512]) for i in range(2)]
        n = 0
        for (t0, t1) in BLK[1:]:
            W = t1 - t0
            for c in range(8):
                s = sq[n % 2]
                n += 1
                k.act(s[:, :W], xT[:, c, t0:t1], AF.Square)
                k.mm(PS(0, 0, W), cst["onesr"], s[:, :W], start=(c == 0), stop=(c == 7))
            k.act(rst_all[:, t0:t1], PS(0, 0, W), AF.Sqrt, bias=eps_n)
            k.recip(rst_all[:, t0:t1], rst_all[:, t0:t1])
        for c in range(8):
            ob = obuf[c % 2]
            k.tt("dve", ob[:, NCTX:T], xT[:, c, NCTX:T], rst_all[:, NCTX:T], ALU.mult)
            k.act(ob[:, NCTX:T], ob[:, NCTX:T], AF.Copy, scale=SPc("normf", c))
            k.dma("sp", outT_d[b, c * 128:(c + 1) * 128, :], ob[:, NCTX:T])
        ph.close()
    k.wait_all("sp", [outT_d.g[0]] + [v.g[0] for v in dbg_out.values()])
    es.close()
    return nc, k, dbg_out


_CACHE = {}


def kernel(**inp):
    inp = {k_: np.asarray(v) for k_, v in inp.items()}
    ncores = 8
    B = inp["x"].shape[0]
    nb = B // ncores
    if "nc" not in _CACHE:
        _CACHE["nc"] = build(nb)[0]
    nc = _CACHE["nc"]
    xT = np.ascontiguousarray(np.transpose(inp["x"], (0, 2, 1)))
    ctxT = np.ascontiguousarray(np.transpose(inp["ctx"], (0, 2, 1)))
    sp = build_sp(inp)
    consts = build_consts()
    in_maps = []
    for i in range(ncores):
        m = {"xT": xT[i * nb:(i + 1) * nb], "ctxT": ctxT[i * nb:(i + 1) * nb]}
        cT = np.zeros((D, 5), np.float32)
        cT[:, :nb] = inp["c"][i * nb:(i + 1) * nb].T
        cT[:, 4] = inp["c_ctx"]
        m["cT"] = cT
        m["sp"] = sp
        for n_ in CONST_SHAPES:
            m["c_" + n_] = consts[n_]
        for n_ in WEIGHT_SHAPES:
            m[n_] = inp[n_]
        in_maps.append(m)
    res = run_bass_kernel_spmd(nc, in_maps, core_ids=list(range(ncores)))
    outT = np.concatenate([np.asarray(r["outT"]) for r in res.results], axis=0)
    return np.ascontiguousarray(np.transpose(outT, (0, 2, 1))).astype(np.float32)
```

```python
import numpy as np
import ml_dtypes
from contextlib import ExitStack
import concourse.bass as bass
import concourse.mybir as mybir
from concourse.bass_utils import run_bass_kernel_spmd

F32 = mybir.dt.float32
BF16 = mybir.dt.bfloat16
F32R = mybir.dt.float32r
DBL_R = False
ALU = mybir.AluOpType
AF = mybir.ActivationFunctionType

D = 1024
T = 2304
NCTX = 256
SEQ = 2048
DEPTH = 2
NIN = 4384
HID = 2816
NJ = HID // 128
CDEC = 0.6065306597126334
BLK = [(0, 256), (256, 768), (768, 1280), (1280, 1792), (1792, 2304)]
NPT = 35
SAME = True
NDS = 24

SPL = {}
_o = 0
for _n, _w in [("norm1", 16), ("norm2", 16), ("normf", 8), ("bada", 96), ("mu", 2 * 15 * 7), ("w0", 16), ("a0", 16),
               ("kk", 8), ("ka", 8), ("rk", 8), ("v0", 4), ("lnw", 8), ("lnb", 8), ("rgw", 8), ("rgb", 8)]:
    SPL[_n] = _o
    _o += _w
NSP = _o


def rwkv_tile_cols(l):
    tiles = []
    for i in range(12):
        tiles.append([i * 128 + p for p in range(128)])
    tiles.append([1536 + p for p in range(128)])
    tiles.append([1664 + p for p in range(128)])
    g1 = [1792 + p for p in range(32)]
    if l == 1:
        g1 += [("v", ch) for ch in range(32)]
    tiles.append(g1)
    return tiles


def build_sp(inp):
    sp = np.zeros((128, NSP), np.float32)
    p = np.arange(128)
    for l in range(2):
        for c in range(8):
            sp[:, SPL["norm1"] + l * 8 + c] = inp["norm1"][l, c * 128 + p]
            sp[:, SPL["norm2"] + l * 8 + c] = inp["norm2"][l, c * 128 + p]
        for q in range(48):
            sp[:, SPL["bada"] + l * 48 + q] = inp["b_ada"][l, q * 128 + p]
        tiles = rwkv_tile_cols(l)
        for ti, cols in enumerate(tiles):
            base = SPL["mu"] + (l * 15 + ti) * 7
            for pp, col in enumerate(cols):
                if isinstance(col, tuple):
                    ch = col[1]
                    m = inp["mu_vres"][0, ch]
                    ld, cd = ch // 8, ch // 16
                else:
                    m = inp["mu_rwkv"][l, col]
                    ld, cd = col // 456, col // 912
                sp[pp, base + 0] = m
                sp[pp, base + 1 + ld] = m
                sp[pp, base + 5 + cd] = m
        for d in range(2):
            for hp in range(4):
                sp[:, SPL["w0"] + (l * 2 + d) * 4 + hp] = inp["w0"][l, d, hp * 128 + p]
                sp[:, SPL["a0"] + (l * 2 + d) * 4 + hp] = inp["a0"][l, d, hp * 128 + p]
        for hp in range(4):
            sp[:, SPL["kk"] + l * 4 + hp] = inp["k_k"][l, hp * 128 + p]
            sp[:, SPL["ka"] + l * 4 + hp] = inp["k_a"][l, hp * 128 + p]
            sp[:, SPL["rk"] + l * 4 + hp] = inp["r_k"][l].reshape(512)[hp * 128 + p]
            sp[:, SPL["lnw"] + l * 4 + hp] = inp["ln_x_w"][l, hp * 128 + p]
            sp[:, SPL["lnb"] + l * 4 + hp] = inp["ln_x_b"][l, hp * 128 + p]
            sp[:, SPL["rgw"] + l * 4 + hp] = inp["ret_gn_w"][l, hp * 128 + p]
            sp[:, SPL["rgb"] + l * 4 + hp] = inp["ret_gn_b"][l, hp * 128 + p]
    for c in range(8):
        sp[:, SPL["normf"] + c] = inp["norm_f"][c * 128 + p]
    for hp in range(4):
        sp[:, SPL["v0"] + hp] = inp["v0"][0, hp * 128 + p]
    return sp


def build_consts():
    c = {}
    s = np.arange(128)[:, None]
    t = np.arange(128)[None, :]
    c["ident"] = np.eye(128, dtype=np.float32)
    c["onesr"] = np.full((128, 128), 1.0 / 1024, np.float32)
    bo = np.zeros((128, 128), np.float32)
    bo[:64, :64] = 1
    bo[64:, 64:] = 1
    c["bo"] = bo
    pm = np.zeros((128, 128), np.float32)
    for m in range(128):
        n = m % 64
        pm[(m - n) + ((n + 32) % 64), m] = 1
    c["pm"] = pm
    m2 = np.zeros((2, 128, 256), np.float32)
    m2[0, :, :128] = s < t
    m2[0, :, 128:] = s <= t
    m2[1, :, :128] = s > t
    m2[1, :, 128:] = s >= t
    c["mask2"] = m2
    mm_ = np.zeros((2, 128, 128), np.float32)
    mm_[0] = s > t
    mm_[1] = s < t
    c["mmask"] = mm_
    rm = np.ones((128, 512), np.float32)
    rm[:, ::128] = 0
    c["rmask"] = rm
    tok = np.arange(SEQ)
    row = (tok // 64).astype(np.float32)
    col = (tok % 64).astype(np.float32)
    nf = 16
    freqs = (np.float32(10000.0) ** (-np.arange(nf, dtype=np.float32) / nf)).astype(np.float32)
    ang = np.concatenate([row[:, None] * freqs, col[:, None] * freqs], -1).astype(np.float32)
    cosT = np.ones((128, T), np.float32)
    ssinT = np.zeros((128, T), np.float32)
    for p_ in range(128):
        n = p_ % 64
        cosT[p_, NCTX:] = np.cos(ang[:, n % 32])
        ssinT[p_, NCTX:] = np.sin(ang[:, n % 32]) * (-1.0 if n < 32 else 1.0)
    c["cosT"] = cosT
    c["ssinT"] = ssinT
    lg = np.log(1.0 - 2.0 ** (-5.0 - np.arange(8, dtype=np.float64)))
    dt_ = np.zeros((2, 4, 128, 2, 128), np.float32)
    qd = np.zeros((2, 4, 128, 128), np.float32)
    kd = np.zeros((2, 4, 128, 128), np.float32)
    cd = np.zeros((128, 8), np.float32)
    sv = np.arange(128)
    for d in range(2):
        for hp in range(4):
            for e in range(2):
                h = 2 * hp + e
                g = lg[h] if d == 0 else lg[7 - h]
                if d == 0:
                    dt_[d, hp, :, e, :] = np.where(t >= s, np.exp(g * np.maximum(t - s, 0)), 0)
                    qd[d, hp, 64 * e:64 * e + 64, :] = np.exp(g * (sv + 1.0))[None, :]
                    kd[d, hp, :, 64 * e:64 * e + 64] = np.exp(g * (127.0 - sv))[:, None]
                else:
                    dt_[d, hp, :, e, :] = np.where(s >= t, np.exp(g * np.maximum(s - t, 0)), 0)
                    qd[d, hp, 64 * e:64 * e + 64, :] = np.exp(g * (128.0 - sv))[None, :]
                    kd[d, hp, :, 64 * e:64 * e + 64] = np.exp(g * sv)[:, None]
                cd[64 * e:64 * e + 64, d * 4 + hp] = np.exp(g * 128.0)
    c["dT"] = dt_.reshape(8, 128, 256)
    c["qd"] = qd.reshape(8, 128, 128)
    c["kd"] = kd.reshape(8, 128, 128)
    c["cd"] = cd
    return c


CONST_SHAPES = {"ident": [128, 128], "onesr": [128, 128], "bo": [128, 128], "pm": [128, 128], "mask2": [2, 128, 256],
                "mmask": [2, 128, 128], "rmask": [128, 512], "cosT": [128, T], "ssinT": [128, T],
                "dT": [8, 128, 256], "qd": [8, 128, 128], "kd": [8, 128, 128], "cd": [128, 8]}
WEIGHT_SHAPES = {"w_ada": [2, 1024, 6144], "w_in": [2, 1024, NIN], "w_vres_down": [1, 1024, 32],
                 "w_up": [2, 2, 64, 512], "a_up": [2, 2, 64, 512], "g_up": [2, 160, 512], "v_up": [1, 32, 512],
                 "w_out": [2, 1024, 1024], "w_ffn_in": [2, 1024, 2 * HID], "w_ffn_out": [2, HID, 1024]}


class Reg:
    __slots__ = ("w", "r", "excl")

    def __init__(self, excl=False):
        self.w = None
        self.r = {}
        self.excl = excl


class V:
    def __init__(self, ap, g):
        self.ap = ap
        self.g = g if isinstance(g, list) else [g]

    def __getitem__(self, idx):
        return V(self.ap[idx], self.g)

    def rr(self, pat, **kw):
        return V(self.ap.rearrange(pat, **kw), self.g)

    def bc(self, shape):
        return V(self.ap.to_broadcast(shape), self.g)

    def bitcast(self, dt):
        return V(self.ap.bitcast(dt), self.g)


class KB:
    def __init__(self, nc, es):
        self.nc = nc
        self.es = es
        self.eng = {"pe": nc.tensor, "dve": nc.vector, "act": nc.scalar, "pool": nc.gpsimd, "sp": nc.sync}
        self.sem = {e: es.enter_context(nc.semaphore("s_" + e)) for e in self.eng}
        self.cnt = {e: 0 for e in self.eng}
        self.seen = {e: {} for e in self.eng}
        self.dsem = [es.enter_context(nc.semaphore("d%d" % i)) for i in range(NDS)]
        self.dcnt = [0] * NDS
        self.dnext = 0
        self.nins = 0

    def sb(self, name, shape, dt=F32):
        t = self.es.enter_context(self.nc.sbuf_tensor(name, list(shape), dt))
        return V(t[:], Reg())

    def _wait(self, e, ev):
        key, sem, val = ev
        if self.seen[e].get(key, 0) >= val:
            return
        if key == e and (e == "pe" or not SAME):
            return
        self.eng[e].wait_ge(sem, val)
        self.seen[e][key] = val
        self.nins += 1

    def _deps(self, e, reads, writes):
        for r in reads:
            if r.w is not None:
                self._wait(e, r.w)
        for w in writes:
            if w.w is not None:
                self._wait(e, w.w)
            for ev in list(w.r.values()):
                self._wait(e, ev)

    def _commit(self, ev, reads, writes):
        for r in reads:
            old = r.r.get(ev[0])
            if old is None or old[2] < ev[2]:
                r.r[ev[0]] = ev
        for w in writes:
            w.w = ev
            w.r = {}

    def op(self, e, fn, reads, writes):
        ex = [r for r in reads if r.excl]
        if ex:
            writes = list(writes) + ex
        self._deps(e, reads, writes)
        ins = fn(self.eng[e])
        self.cnt[e] += 1
        ins.then_inc(self.sem[e], 1)
        self.nins += 1
        self._commit((e, self.sem[e], self.cnt[e]), reads, writes)

    def dma(self, q, out, in_):
        reads, writes = in_.g, out.g
        i = self.dnext
        self.dnext = (i + 1) % NDS
        key = ("d", i)
        if self.dcnt[i] > 0:
            self._wait(q, (key, self.dsem[i], self.dcnt[i]))
        self._deps(q, reads, writes)
        self.dcnt[i] += 16
        self.eng[q].dma_start(out=out.ap, in_=in_.ap).then_inc(self.dsem[i], 16)
        self.nins += 1
        self._commit((key, self.dsem[i], self.dcnt[i]), reads, writes)

    def wait_all(self, e, regs):
        self._deps(e, regs, [])

    def tt(self, e, out, in0, in1, op):
        self.op(e, lambda E: E.tensor_tensor(out=out.ap, in0=in0.ap, in1=in1.ap, op=op), in0.g + in1.g, out.g)

    def ts(self, e, out, in0, s1, s2=None, op0=ALU.mult, op1=None):
        rd = list(in0.g)
        a1 = s1
        a2 = s2
        if isinstance(s1, V):
            rd += s1.g
            a1 = s1.ap
        if isinstance(s2, V):
            rd += s2.g
            a2 = s2.ap
        if op1 is None:
            self.op(e, lambda E: E.tensor_scalar(out=out.ap, in0=in0.ap, scalar1=a1, scalar2=None, op0=op0), rd, out.g)
        else:
            self.op(e, lambda E: E.tensor_scalar(out=out.ap, in0=in0.ap, scalar1=a1, scalar2=a2, op0=op0, op1=op1), rd, out.g)

    def stt(self, out, in0, sc, in1, op0, op1):
        rd = in0.g + in1.g
        a = sc
        if isinstance(sc, V):
            rd = rd + sc.g
            a = sc.ap
        self.op("dve", lambda E: E.scalar_tensor_tensor(out=out.ap, in0=in0.ap, scalar=a, in1=in1.ap, op0=op0, op1=op1), rd, out.g)

    def act(self, out, in_, func, bias=None, scale=1.0):
        rd = list(in_.g)
        kw = {}
        if isinstance(bias, V):
            rd += bias.g
            kw["bias"] = bias.ap
        elif bias is not None:
            kw["bias"] = bias
        if isinstance(scale, V):
            rd += scale.g
            kw["scale"] = scale.ap
        else:
            kw["scale"] = scale
        self.op("act", lambda E: E.activation(out=out.ap, in_=in_.ap, func=func, **kw), rd, out.g)

    def cp(self, e, out, in_):
        if e == "act":
            self.act(out, in_, AF.Copy)
        else:
            self.op(e, lambda E: E.tensor_copy(out=out.ap, in_=in_.ap), in_.g, out.g)

    def memset(self, e, out, val):
        self.op(e, lambda E: E.memset(out.ap, val), [], out.g)

    def recip(self, out, in_):
        self.op("dve", lambda E: E.reciprocal(out=out.ap, in_=in_.ap), in_.g, out.g)

    def mm(self, out, lhsT, rhs, start=True, stop=True):
        self.op("pe", lambda E: E.matmul(out.ap, lhsT=lhsT.ap, rhs=rhs.ap, start=start, stop=stop), lhsT.g + rhs.g, out.g)

    def tr(self, out, in_, ident):
        self.op("pe", lambda E: E.transpose(out.ap, in_.ap, ident.ap), in_.g + ident.g, out.g)

    def scan(self, out, d0, d1, init, op0, op1):
        self.op("dve", lambda E: E.tensor_tensor_scan(out=out.ap, data0=d0.ap, data1=d1.ap, initial=init, op0=op0, op1=op1),
                d0.g + d1.g, out.g)


def build(nb, nlayers=DEPTH, dbg=()):
    nc = bass.Bass("TRN2", target_bir_lowering=False)
    es = ExitStack()
    k = KB(nc, es)

    def din(name, shape):
        return V(nc.dram_tensor(name, list(shape), F32, kind="ExternalInput").ap(), Reg())

    xT_d = din("xT", [nb, D, SEQ])
    ctxT_d = din("ctxT", [nb, D, NCTX])
    cT_d = din("cT", [D, 5])
    sp_d = din("sp", [128, NSP])
    CD = {n: din("c_" + n, s) for n, s in CONST_SHAPES.items()}
    WD = {n: din(n, s) for n, s in WEIGHT_SHAPES.items()}
    outT_d = V(nc.dram_tensor("outT", [nb, D, SEQ], F32, kind="ExternalOutput").ap(), Reg())
    pbuf_ap = nc.dram_tensor("pbuf", [NPT, 128, T], F32, kind="Internal").ap()
    pbuf = [V(pbuf_ap[i], Reg()) for i in range(NPT)]
    vf_ap = nc.dram_tensor("vfirst", [4, 128, T], F32, kind="Internal").ap()
    vfd = [V(vf_ap[i], Reg()) for i in range(4)]
    dbg_out = {}

    def dump(name, v, shape, q="sp"):
        if name in dbg:
            o = V(nc.dram_tensor("dbg_" + name, list(shape), F32, kind="ExternalOutput").ap(), Reg())
            dbg_out[name] = o
            k.dma(q, o, v)

    xT_t = es.enter_context(nc.sbuf_tensor("xT_sb", [128, 8 * T], F32))
    xT = V(xT_t[:].rearrange("p (c t) -> p c t", c=8), Reg())
    xsp_d = V(nc.dram_tensor("xspill", [128, 8, T], F32, kind="Internal").ap(), Reg())
    hT = k.sb("hT_sb", [128, 8, T], BF16)
    psall = es.enter_context(nc.psum_tensor("ps", [128, 8, 512], F32))
    PSR = [Reg(excl=True) for _ in range(8)]

    def PS(b, lo=0, hi=512):
        return V(psall[:, b, lo:hi], PSR[b])

    def PS2(b0, lo, hi):
        return V(psall[:, b0:b0 + 2, lo:hi], [PSR[b0], PSR[b0 + 1]])

    spt = k.sb("spt", [128, NSP])
    k.dma("sp", spt, sp_d)

    def SPc(name, idx):
        return spt[:, SPL[name] + idx:SPL[name] + idx + 1]

    cst = {}
    for n in ["onesr", "bo", "rmask"]:
        cst[n] = k.sb("sc_" + n, CONST_SHAPES[n])
        k.dma("sp", cst[n], CD[n])
    for n in ["ident", "pm"]:
        cst[n] = k.sb("sc_" + n, CONST_SHAPES[n], BF16)
        k.dma("pool", cst[n], CD[n])
    cst["mask2"] = k.sb("sc_mask2", [128, 2, 256])
    cst["mmask"] = k.sb("sc_mmask", [128, 2, 128])
    for d in range(2):
        k.dma("sp", cst["mask2"][:, d, :], CD["mask2"][d])
        k.dma("sp", cst["mmask"][:, d, :], CD["mmask"][d])
    cst["cd"] = k.sb("sc_cd", [128, 8])
    k.dma("sp", cst["cd"], CD["cd"])
    WAup = k.sb("WAup", [128, 2, 2, 512], BF16)
    G0up = k.sb("G0up", [128, 2, 512], BF16)
    GVup = k.sb("GVup", [64, 2, 512], BF16)
    for l in range(2):
        for d in range(2):
            k.dma("pool", WAup[0:64, l, d, :], WD["w_up"][l, d])
            k.dma("pool", WAup[64:128, l, d, :], WD["a_up"][l, d])
        k.dma("pool", G0up[:, l, :], WD["g_up"][l, 0:128, :])
        k.dma("pool", GVup[0:32, l, :], WD["g_up"][l, 128:160, :])
    k.dma("pool", GVup[32:64, 1, :], WD["v_up"][0])
    omm = k.sb("omm", [128, 30])
    mu0 = spt[:, SPL["mu"]:SPL["mu"] + 210].rr("p (t s) -> p t s", s=7)[:, :, 0]
    k.ts("dve", omm, mu0, -1.0, 1.0, ALU.mult, ALU.add)
    omka = k.sb("omka", [128, 8])
    k.ts("dve", omka, spt[:, SPL["ka"]:SPL["ka"] + 8], -1.0, 1.0, ALU.mult, ALU.add)

    ARW = 15360
    arena = es.enter_context(nc.sbuf_tensor("arena", [128, ARW], F32))

    class Phase:
        def __init__(self, extra=None):
            self.off = 0
            self.regs = []
            self.backs = [(arena, ARW)] + ([extra] if extra is not None else [])
            self.bi = 0

        def sb(self, shape, dt=F32):
            n = 1
            for s_ in shape[1:]:
                n *= s_
            words = n if dt == F32 else (n + 1) // 2
            words = (words + 7) // 8 * 8
            if self.off + words > self.backs[self.bi][1]:
                self.bi += 1
                self.off = 0
                assert self.bi < len(self.backs), "arena overflow"
                assert words <= self.backs[self.bi][1]
            ap = self.backs[self.bi][0][0:shape[0], self.off:self.off + words]
            self.off += words
            if dt != F32:
                ap = ap.bitcast(dt)
            ap = ap[:, 0:n]
            if len(shape) > 2:
                names = " ".join("d%d" % i for i in range(len(shape) - 1))
                ap = ap.rearrange("p (%s) -> p %s" % (names, names), **{"d%d" % i: shape[i + 1] for i in range(len(shape) - 1)})
            r = Reg()
            self.regs.append(r)
            return V(ap, r)

        def close(self):
            for e in ("pe", "dve", "act", "pool", "sp"):
                k._deps(e, [], self.regs)

    scT = k.sb("scT", [128, 8, 5])
    k.dma("sp", scT, cT_d.rr("(c p) w -> p c w", p=128))
    k.act(scT, scT, AF.Silu)
    ada = [k.sb("ada%d" % l, [128, 48, 5]) for l in range(2)]
    ph0 = Phase()
    wadab = [ph0.sb([128, 8, 512]) for i in range(2)]
    it = 0
    for l in range(nlayers):
        for qg in range(12):
            wb = wadab[it % 2]
            it += 1
            k.dma("sp", wb, WD["w_ada"][l].rr("(c p) n -> p c n", p=128)[:, :, qg * 512:(qg + 1) * 512])
            for qq in range(4):
                q = qg * 4 + qq
                for c in range(8):
                    k.mm(PS(0, q * 5, q * 5 + 5), wb[:, c, qq * 128:(qq + 1) * 128], scT[:, c, :], start=(c == 0), stop=(c == 7))
        k.tt("dve", ada[l], PS(0, 0, 240).rr("p (q w) -> p q w", w=5),
             spt[:, SPL["bada"] + l * 48:SPL["bada"] + (l + 1) * 48].rr("p (q o) -> p q o", o=1).bc([128, 48, 5]), ALU.add)
    ph0.close()
    A1 = [k.sb("A1_%d" % l, [128, 8, 5]) for l in range(2)]
    A2 = [k.sb("A2_%d" % l, [128, 8, 5]) for l in range(2)]
    for l in range(nlayers):
        for (A, nm, q0) in ((A1, "norm1", 8), (A2, "norm2", 32)):
            k.ts("dve", A[l], ada[l][:, q0:q0 + 8, :], 1.0, None, ALU.add)
            k.tt("dve", A[l], A[l], spt[:, SPL[nm] + l * 8:SPL[nm] + l * 8 + 8].rr("p (c o) -> p c o", o=1).bc([128, 8, 5]), ALU.mult)
    zero_col = k.sb("zero_col", [128, 1])
    k.memset("dve", zero_col, 0.0)
    eps_n = k.sb("eps_n", [128, 1])
    k.memset("dve", eps_n, 1e-6)
    eps_r = k.sb("eps_r", [128, 1])
    k.memset("dve", eps_r, 64e-5)
    eps_t = k.sb("eps_t", [128, 1])
    k.memset("dve", eps_t, 1e-5)

    def norm(ph, scale_fn, bias_fn, out_fn, blocks):
        sq = [ph.sb([128, 512]) for i in range(2)]
        rstd = ph.sb([128, 512])
        ntmp = [ph.sb([128, 512]) for i in range(2)]
        n = 0
        for (t0, t1) in blocks:
            W = t1 - t0
            for c in range(8):
                s = sq[n % 2]
                n += 1
                k.act(s[:, :W], xT[:, c, t0:t1], AF.Square)
                k.mm(PS(0, 0, W), cst["onesr"], s[:, :W], start=(c == 0), stop=(c == 7))
            k.act(rstd[:, :W], PS(0, 0, W), AF.Sqrt, bias=eps_n)
            k.recip(rstd[:, :W], rstd[:, :W])
            for c in range(8):
                tmp = ntmp[c % 2]
                k.tt("dve", tmp[:, :W], xT[:, c, t0:t1], rstd[:, :W], ALU.mult)
                b_ = bias_fn(c, t0 < NCTX)
                k.act(out_fn(c, t0, t1), tmp[:, :W], AF.Identity, bias=(b_ if b_ is not None else zero_col), scale=scale_fn(c, t0 < NCTX))

    def who_idx(is_ctx, b):
        return 4 if is_ctx else b

    RNG = [(256 * i, 256 * (i + 1)) for i in range(9)]
    RW = 256

    for b in range(nb):
        for c in range(8):
            k.dma("sp", xT[:, c, NCTX:T], xT_d[b, c * 128:(c + 1) * 128, :])
            k.dma("sp", xT[:, c, 0:NCTX], ctxT_d[b, c * 128:(c + 1) * 128, :])
        for l in range(nlayers):
            last = (l == DEPTH - 1)
            ph = Phase()
            norm(ph, lambda c, ic: A1[l][:, c, who_idx(ic, b):who_idx(ic, b) + 1],
                 lambda c, ic: ada[l][:, 0 + c, who_idx(ic, b):who_idx(ic, b) + 1],
                 lambda c, t0, t1: hT[:, c, t0:t1], BLK)
            ph.close()
            if ("hT_%d_%d" % (b, l)) in dbg:
                o_ = V(nc.dram_tensor("dbg_hT_%d_%d" % (b, l), [128, 8, T], BF16, kind="ExternalOutput").ap(), Reg())
                dbg_out["hT_%d_%d" % (b, l)] = o_
                k.dma("sp", o_, hT)
            ph = Phase()
            ubuf = [ph.sb([128, T]) for i in range(2)]
            lbuf = [ph.sb([128, T]) for i in range(1)]
            wg = [ph.sb([128, 8, 512], BF16) for i in range(2)]
            groups = [(0, 512), (512, 512), (1024, 512), (1536, 288)] + [(1824 + 512 * i, 512) for i in range(5)]
            ti = 0
            gi = 0
            for (c0, ncol) in groups:
                wgb = wg[gi % 2]
                gi += 1
                k.dma("pool", wgb[:, :, 0:ncol], WD["w_in"][l].rr("(c p) n -> p c n", p=128)[:, :, c0:c0 + ncol])
                if c0 == 1536 and l == 1:
                    k.dma("pool", wgb[:, :, 288:320], WD["w_vres_down"][0].rr("(c p) n -> p c n", p=128))
                if c0 == 1536:
                    tl = [(0, 128), (128, 128), (256, 64 if l == 1 else 32)]
                else:
                    tl = [(i * 128, 128) for i in range(4)]
                for (off, M) in tl:
                    is_rwkv = ti < 15
                    u = ubuf[ti % 2]
                    o = lbuf[0]
                    dst = u if is_rwkv else o
                    for bi, (t0, t1) in enumerate(BLK):
                        W = t1 - t0
                        pb = (ti * 5 + bi) % 8
                        for c in range(8):
                            k.mm(PS(pb, 0, W)[0:M], wgb[:, c, off:off + M], hT[:, c, t0:t1], start=(c == 0), stop=(c == 7))
                        k.cp("act" if (bi % 2 == 0) else "dve", dst[0:M, t0:t1], PS(pb, 0, W)[0:M])
                    if is_rwkv:
                        mub = SPL["mu"] + (l * 15 + ti) * 7
                        k.act(o[0:M, :], u[0:M, :], AF.Copy, scale=omm[0:M, l * 15 + ti:l * 15 + ti + 1])
                        cols = rwkv_tile_cols(l)[ti]
                        ldirs = sorted(set((c_[1] // 8) if isinstance(c_, tuple) else (c_ // 456) for c_ in cols))
                        cdirs = sorted(set((c_[1] // 16) if isinstance(c_, tuple) else (c_ // 912) for c_ in cols))
                        uL = u[0:M, NCTX:T].rr("p (r c) -> p r c", c=64)
                        oL = o[0:M, NCTX:T].rr("p (r c) -> p r c", c=64)
                        for dr in ldirs:
                            m = spt[0:M, mub + 1 + dr:mub + 2 + dr]
                            if dr == 0:
                                k.stt(oL[:, :, 1:64], uL[:, :, 0:63], m, oL[:, :, 1:64], ALU.mult, ALU.add)
                            elif dr == 1:
                                k.stt(oL[:, :, 0:63], uL[:, :, 1:64], m, oL[:, :, 0:63], ALU.mult, ALU.add)
                            elif dr == 2:
                                k.stt(o[0:M, NCTX + 64:T], u[0:M, NCTX:T - 64], m, o[0:M, NCTX + 64:T], ALU.mult, ALU.add)
                            else:
                                k.stt(o[0:M, NCTX:T - 64], u[0:M, NCTX + 64:T], m, o[0:M, NCTX:T - 64], ALU.mult, ALU.add)
                        for dr in cdirs:
                            m = spt[0:M, mub + 5 + dr:mub + 6 + dr]
                            if dr == 0:
                                k.stt(o[0:M, 1:NCTX], u[0:M, 0:NCTX - 1], m, o[0:M, 1:NCTX], ALU.mult, ALU.add)
                            else:
                                k.stt(o[0:M, 0:NCTX - 1], u[0:M, 1:NCTX], m, o[0:M, 0:NCTX - 1], ALU.mult, ALU.add)
                        if ti == 12:
                            k.act(o[0:64, :], o[0:64, :], AF.Tanh)
                        elif ti == 13:
                            k.act(o[0:128, :], o[0:128, :], AF.Sigmoid)
                        elif ti == 14:
                            k.act(o[0:32, :], o[0:32, :], AF.Sigmoid)
                    k.dma("sp", pbuf[ti][0:M, :], o[0:M, :])
                    if l == 0 and 8 <= ti < 12:
                        k.dma("sp", vfd[ti - 8], o[0:M, :])
                    ti += 1
            assert ti == NPT
            ph.close()
            if ("pbuf_%d_%d" % (b, l)) in dbg:
                o_ = V(nc.dram_tensor("dbg_pbuf_%d_%d" % (b, l), [NPT, 128, T], F32, kind="ExternalOutput").ap(), Reg())
                dbg_out["pbuf_%d_%d" % (b, l)] = o_
                for i_ in range(NPT):
                    k.dma("sp", o_[i_], pbuf[i_])
            if "stop_p2" in dbg:
                break
            k.dma("sp", xsp_d, xT)
            for e_ in ("pe", "dve", "act", "pool", "sp"):
                k._deps(e_, [], xT.g)
            ph = Phase(extra=(xT_t, 8 * T))
            G0s = ph.sb([128, RW], BF16)
            G1s = ph.sb([64, RW], BF16)
            ybuf = ph.sb([128, T])
            oT = hT
            W = RW
            nch = 2

            class St:
                pass

            def alloc_stream(d, kind):
                B = St()
                B.d = d
                B.fb = [ph.sb([128, RW]) for i in range(12)]
                B.hb = [ph.sb([128, RW], BF16) for i in range(6)]
                B.Ktok = ph.sb([128, 2, 128], BF16)
                B.Vtok = ph.sb([128, 2, 128], BF16)
                B.Sst = ph.sb([128, 64])
                B.Sbf = ph.sb([128, 64], BF16)
                if kind == "rwkv":
                    B.WAs = ph.sb([128, RW], BF16)
                    B.ARb = ph.sb([128, 2, 2, 128], BF16)
                    B.Btok = ph.sb([128, 2, 128], BF16)
                    B.gC = ph.sb([128, 2])
                    B.Am = ph.sb([128, 2, 2, 256], BF16)
                    B.NMb = [ph.sb([128, 2, 2, 128]) for i in range(2)]
                    B.Ppb = [ph.sb([128, 2, 128]) for i in range(2)]
                    B.Xsb = ph.sb([128, 2, 64])
                    B.Usb = ph.sb([128, 2, 64], BF16)
                else:
                    B.scm = ph.sb([128, 2, 128], BF16)
                    B.dTt = ph.sb([128, 256])
                    B.qdt = ph.sb([128, 128])
                    B.kdt = ph.sb([128, 128])
                B.off = 4 * d
                return B
            SB = [alloc_stream(0, "rwkv"), alloc_stream(1, "rwkv")]
            SBr = [alloc_stream(0, "ret")]
            ybuf2 = ph.sb([128, T])
            ybs = [ybuf, ph.sb([128, T])]
            efb = [ph.sb([128, RW]) for i in range(7)]

            def v3(x):
                return x.rr("p (c t) -> p c t", t=128)

            def run_streams(gens):
                gens = list(gens)
                while gens:
                    for g in list(gens):
                        try:
                            next(g)
                        except StopIteration:
                            gens.remove(g)

            def gn_block(src, wcol, bcol, eps_col, out_f, pa, pb_, gnb, region=None):
                cen, sqv, rs = gnb
                ra = region if region is not None else PS(pa, 0, W)
                rb = region if region is not None else PS(pb_, 0, W)
                k.mm(ra, cst["bo"], src, True, True)
                k.stt(cen, ra, -1.0 / 64, src, ALU.mult, ALU.add)
                k.act(sqv, cen, AF.Square)
                k.mm(rb, cst["bo"], sqv, True, True)
                k.act(rs, rb, AF.Sqrt, bias=eps_col, scale=1.0 / 64)
                k.recip(rs, rs)
                k.tt("dve", cen, cen, rs, ALU.mult)
                k.act(out_f, cen, AF.Identity, bias=bcol, scale=wcol)

            def transposes(src_bf, dst_tok, bank, mul_tab=None, region=None):
                pt = (region if region is not None else PS(bank)).bitcast(BF16)
                for c in range(nch):
                    k.tr(pt[:, c * 128:(c + 1) * 128], src_bf[:, c * 128:(c + 1) * 128], cst["ident"])
                ptv = pt[:, 0:nch * 128].rr("p (c f) -> p c f", f=128)
                if mul_tab is None:
                    k.cp("act", dst_tok, ptv)
                else:
                    k.tt("dve", dst_tok, ptv, mul_tab.rr("p (o f) -> p o f", o=1).bc([128, nch, 128]), ALU.mult)

            def rwkv_pass(hp, B, ybuf):
                d = B.d
                hc = slice(hp * 128, (hp + 1) * 128)
                fb, hb = B.fb, B.hb
                ARb, Btok, Ktok, Vtok, gC, Am, NMb, Ppb = B.ARb, B.Btok, B.Ktok, B.Vtok, B.gC, B.Am, B.NMb, B.Ppb
                Xsb, Usb, Sst, Sbf, WAs = B.Xsb, B.Usb, B.Sst, B.Sbf, B.WAs

                def P(role, lo=0, hi=512):
                    return PS((role + B.off) % 8, lo, hi)

                def P2(role, lo, hi):
                    return PS2((role + B.off) % 8, lo, hi)
                order = list(range(9)) if d == 0 else [0] + list(range(8, 0, -1))
                k.memset("dve", Sst, 0.0)
                k.memset("dve", Sbf, 0.0)
                for ri in order:
                    t0, t1 = RNG[ri]
                    rF, kF, vF = fb[0], fb[1], fb[2]
                    k.dma("sp", rF, pbuf[0 + hp][:, t0:t1])
                    k.dma("sp", kF, pbuf[4 + hp][:, t0:t1])
                    k.dma("sp", vF, pbuf[8 + hp][:, t0:t1])
                    k.dma("pool", WAs, pbuf[12][:, t0:t1])
                    sw, Lr, Lin, Lex, icl, kk, t1b, t2b, Eb = fb[3], fb[4], fb[5], fb[6], fb[7], fb[8], fb[9], fb[10], fb[11]
                    k.mm(P(0, 0, W), WAup[0:64, l, d, hc], WAs[0:64, :])
                    k.mm(P(1, 0, W), WAup[64:128, l, d, hc], WAs[64:128, :])
                    yield
                    k.act(sw, P(0, 0, W), AF.Sigmoid, bias=SPc("w0", (l * 2 + d) * 4 + hp))
                    k.act(icl, P(1, 0, W), AF.Sigmoid, bias=SPc("a0", (l * 2 + d) * 4 + hp))
                    k.ts("dve", kk, kF, SPc("kk", l * 4 + hp), None, ALU.mult)
                    k.act(t1b, kk, AF.Square)
                    k.mm(P(0, 0, W), cst["bo"], t1b)
                    yield
                    k.scan(Lr, cst["rmask"][:, :W], sw, 0.0, ALU.mult, ALU.add)
                    totb = v3(Lr)[:, :, 127:128].bc([128, nch, 128])
                    if d == 0:
                        k.cp("dve", Lin, Lr)
                    else:
                        k.tt("dve", t2b, sw, Lr, ALU.subtract)
                        k.tt("dve", v3(Lin), v3(t2b), totb, ALU.add)
                    k.tt("dve", Lex, Lin, sw, ALU.subtract)
                    k.act(gC, v3(Lr)[:, :, 127], AF.Exp, scale=-CDEC)
                    k.ts("dve", t2b, P(0, 0, W), 1e-24, None, ALU.max)
                    k.act(t2b, t2b, AF.Sqrt)
                    yield
                    k.recip(t2b, t2b)
                    k.tt("dve", kk, kk, t2b, ALU.mult)
                    k.act(Eb, Lex, AF.Exp, scale=-CDEC)
                    yield
                    k.stt(ARb[:, :, 0, :], v3(kk), -1.0, v3(Eb), ALU.mult, ALU.mult)
                    k.act(Eb, Lin, AF.Exp, scale=-CDEC)
                    yield
                    k.tt("dve", ARb[:, :, 1, :], v3(rF), v3(Eb), ALU.mult)
                    ktil, bp = t1b, t2b
                    k.ts("dve", ktil, icl, SPc("ka", l * 4 + hp), omka[:, l * 4 + hp:l * 4 + hp + 1], ALU.mult, ALU.add)
                    k.tt("dve", ktil, ktil, kF, ALU.mult)
                    k.tt("dve", bp, kk, icl, ALU.mult)
                    BH, KH, bck, kck, vbf = hb[0], hb[1], hb[2], hb[3], hb[4]
                    k.act(Eb, Lin, AF.Exp, scale=CDEC)
                    yield
                    k.tt("dve", BH, bp, Eb, ALU.mult)
                    k.tt("dve", KH, ktil, Eb, ALU.mult)
                    k.tt("dve", v3(Lex), v3(Lin), totb, ALU.subtract)
                    k.act(Eb, Lex, AF.Exp, scale=CDEC)
                    yield
                    k.tt("dve", bck, bp, Eb, ALU.mult)
                    k.tt("dve", kck, ktil, Eb, ALU.mult)
                    k.cp("act", vbf, vF)
                    yield
                    transposes(bck, Btok, (4 + B.off) % 8)
                    yield
                    transposes(kck, Ktok, (5 + B.off) % 8)
                    yield
                    transposes(vbf, Vtok, (4 + B.off) % 8)
                    yield
                    corder = list(range(nch)) if d == 0 else list(range(nch - 1, -1, -1))
                    for c in corder:
                        cs = slice(c * 128, (c + 1) * 128)
                        for e in range(2):
                            Re = slice(64 * e, 64 * e + 64)
                            arv = ARb[Re, c, :, :].rr("p a t -> p (a t)")
                            k.mm(P(e, 0, 256), BH[Re, cs], arv)
                            k.mm(P(e, 256, 512), KH[Re, cs], arv)
                            k.mm(P(2 + e, 0, 128), ARb[Re, c, 0, :], BH[Re, cs])
                        yield
                        nm0 = NMb[0]
                        k.tt("dve", nm0[:, :, 0, :], P2(0, 0, 128), cst["mask2"][:, d, 0:128].rr("p (o t) -> p o t", o=1).bc([128, 2, 128]), ALU.mult)
                        k.tt("dve", nm0[:, :, 1, :], P2(2, 0, 128), cst["mmask"][:, d, :].rr("p (o t) -> p o t", o=1).bc([128, 2, 128]), ALU.mult)
                        k.tt("dve", Ppb[0], nm0[:, :, 0, :], cst["ident"].rr("p (o t) -> p o t", o=1).bc([128, 2, 128]), ALU.add)
                        for e in range(2):
                            k.tt("dve", Am[:, e, :, :], P(e).rr("p (a t) -> p a t", a=2),
                                 cst["mask2"][:, d, :].rr("p (o t) -> p o t", o=1).bc([128, 2, 256]), ALU.mult)
                        yield
                        cur = 0
                        pcur = 0
                        for itn in range(7):
                            nmc, nmn = NMb[cur], NMb[1 - cur]
                            pc, pn = Ppb[pcur], Ppb[1 - pcur]
                            def rc(x):
                                return x.bitcast(F32R) if DBL_R else x
                            for e in range(2):
                                if itn < 5:
                                    k.mm(P(4, e * 256, e * 256 + 128), rc(nmc[:, e, 1, :]), rc(nmc[:, e, 0, :]))
                                if itn < 6:
                                    k.mm(P(4, e * 256 + 128, e * 256 + 256), rc(nmc[:, e, 0, :]), rc(nmc[:, e, 1, :]))
                                if itn >= 1:
                                    k.mm(P(5, e * 128, e * 128 + 128), rc(nmc[:, e, 1, :]), rc(pc[:, e, :]))
                            yield
                            if itn < 5:
                                k.cp("act", nmn.rr("p e a t -> p (e a t)"), P(4))
                            elif itn == 5:
                                k.cp("act", nmn[:, :, 1, :], P(4).rr("p (e a t) -> p e a t", e=2, a=2)[:, :, 1, :])
                            if itn >= 1:
                                k.tt("dve", pn, P(5, 0, 256).rr("p (e t) -> p e t", e=2), pc, ALU.add)
                                pcur = 1 - pcur
                            yield
                            cur = 1 - cur
                        cur = pcur
                        TT = Ppb[cur]
                        for e in range(2):
                            Re = slice(64 * e, 64 * e + 64)
                            k.mm(P(2 + e, 128, 192), ARb[Re, c, 0, :], Sbf[Re, :], True, False)
                            k.mm(P(2 + e, 128, 192), Am[:, e, 1, 0:128], Vtok[:, c, Re], False, True)
                        yield
                        k.cp("act", Xsb, P2(2, 128, 192))
                        yield
                        for e in range(2):
                            k.mm(P(2 + e, 192, 256), TT[:, e, :], Xsb[:, e, :])
                        yield
                        k.cp("act", Usb, P2(2, 192, 256))
                        yield
                        for e in range(2):
                            Re = slice(64 * e, 64 * e + 64)
                            po = P(6 + e, 0, 128)[Re]
                            k.mm(po, Sbf[Re, :], ARb[Re, c, 1, :], True, False)
                            k.mm(po, Usb[:, e, :], Am[:, e, 0, 128:256], False, False)
                            k.mm(po, Vtok[:, c, Re], Am[:, e, 1, 128:256], False, True)
                        for e in range(2):
                            Ce = slice(64 * e, 64 * e + 64)
                            pd = P(5, 256, 320)[Ce]
                            k.mm(pd, Btok[:, c, Ce], Usb[:, e, :], True, False)
                            k.mm(pd, Ktok[:, c, Ce], Vtok[:, c, Ce], False, True)
                        yield
                        for e in range(2):
                            Re = slice(64 * e, 64 * e + 64)
                            po = P(6 + e, 0, 128)[Re]
                            yv = ybuf[Re, t0 + c * 128:t0 + (c + 1) * 128]
                            k.tt("dve", yv, yv, po, ALU.add)
                        k.stt(Sst, Sst, gC[:, c:c + 1], P(5, 256, 320), ALU.mult, ALU.add)
                        k.cp("act", Sbf, Sst)
                        yield

            def ret_pass(hp, B, d):
                fb, hb = B.fb, B.hb
                Ktok, Vtok, Sst, Sbf, scm, dTt, qdt, kdt = B.Ktok, B.Vtok, B.Sst, B.Sbf, B.scm, B.dTt, B.qdt, B.kdt

                order = list(range(9)) if d == 0 else [0] + list(range(8, 0, -1))
                k.dma("sp", dTt, CD["dT"][d * 4 + hp])
                k.dma("sp", qdt, CD["qd"][d * 4 + hp])
                k.dma("sp", kdt, CD["kd"][d * 4 + hp])
                k.memset("dve", Sst, 0.0)
                k.memset("dve", Sbf, 0.0)
                for ri in order:
                    t0, t1 = RNG[ri]
                    qF, kF, vF, gF, cosF, sinF = fb[0], fb[1], fb[2], fb[3], fb[4], fb[5]
                    k.dma("sp", qF, pbuf[15 + hp][:, t0:t1])
                    k.dma("sp", kF, pbuf[19 + hp][:, t0:t1])
                    k.dma("sp", vF, pbuf[23 + hp][:, t0:t1])
                    k.dma("sp", gF, pbuf[(27 if d == 0 else 31) + hp][:, t0:t1])
                    k.dma("sp", cosF, CD["cosT"][:, t0:t1])
                    k.dma("sp", sinF, CD["ssinT"][:, t0:t1])
                    qb, kb, qr, kr, qh, vbf = hb[0], hb[1], hb[2], hb[3], hb[4], hb[5]
                    t1b, t2b = fb[6], fb[7]
                    for (src, sb_, dstb, isq) in ((qF, qb, qr, True), (kF, kb, kr, False)):
                        k.cp("act", sb_, src)
                        yield
                        k.mm(PS(6, 256, 512), cst["pm"], sb_)
                        k.tt("dve", t1b, src, cosF, ALU.mult)
                        yield
                        k.tt("dve", t2b, PS(6, 256, 512), sinF, ALU.mult)
                        k.tt("dve", t1b, t1b, t2b, ALU.add)
                        if isq:
                            k.cp("act", dstb, t1b)
                            k.tt("dve", v3(qh), v3(t1b), qdt.rr("p (o t) -> p o t", o=1).bc([128, nch, 128]), ALU.mult)
                        else:
                            k.act(dstb, t1b, AF.Copy, scale=0.125)
                        yield
                    k.cp("act", vbf, vF)
                    yield
                    transposes(kr, Ktok, None, mul_tab=kdt, region=PS(7, 256, 384))
                    yield
                    transposes(vbf, Vtok, None, region=PS(7, 256, 384))
                    yield
                    orng = fb[8]
                    corder = list(range(nch)) if d == 0 else list(range(nch - 1, -1, -1))
                    for c in corder:
                        cs = slice(c * 128, (c + 1) * 128)
                        for e in range(2):
                            Re = slice(64 * e, 64 * e + 64)
                            k.mm(PS(2 + e, 256, 384), kr[Re, cs], qr[Re, cs])
                        yield
                        k.tt("dve", scm, PS2(2, 256, 384), dTt.rr("p (e t) -> p e t", e=2), ALU.mult)
                        yield
                        for e in range(2):
                            Re = slice(64 * e, 64 * e + 64)
                            po = PS(2 + e, 384, 512)[Re]
                            k.mm(po, Sbf[Re, :], qh[Re, cs], True, False)
                            k.mm(po, Vtok[:, c, Re], scm[:, e, :], False, True)
                        for e in range(2):
                            Ce = slice(64 * e, 64 * e + 64)
                            pd = PS(7, 384, 448)[Ce]
                            k.mm(pd, Ktok[:, c, Ce], Vtok[:, c, Ce], True, True)
                        yield
                        for e in range(2):
                            Re = slice(64 * e, 64 * e + 64)
                            k.cp("act", orng[Re, cs], PS(2 + e, 384, 512)[Re])
                        k.stt(Sst, Sst, cst["cd"][:, d * 4 + hp:d * 4 + hp + 1], PS(7, 384, 448), ALU.mult, ALU.add)
                        k.cp("act", Sbf, Sst)
                        yield
                    gnv = fb[9]
                    gn_block(orng, SPc("rgw", l * 4 + hp), SPc("rgb", l * 4 + hp), eps_t, gnv, None, None, (fb[10], fb[11], fb[6]), region=PS(6, 256, 512))
                    k.act(gF, gF, AF.Silu)
                    k.tt("dve", gnv, gnv, gF, ALU.mult)
                    k.tt("dve", ybuf2[:, t0:t1], ybuf2[:, t0:t1], gnv, ALU.add)
                    yield

            EREG = PS(6, 256, 512)

            def vres_gen(hp):
                hc = slice(hp * 128, (hp + 1) * 128)
                for (t0, t1) in RNG:
                    vF, vfF, sg = efb[0], efb[1], efb[2]
                    k.dma("sp", vF, pbuf[8 + hp][:, t0:t1])
                    k.dma("sp", vfF, vfd[hp][:, t0:t1])
                    k.dma("pool", G1s, pbuf[14][0:64, t0:t1])
                    k.mm(EREG, GVup[32:64, 1, hc], G1s[32:64, :])
                    yield
                    k.act(sg, EREG, AF.Sigmoid, bias=SPc("v0", hp))
                    k.tt("dve", vfF, vfF, vF, ALU.subtract)
                    yield
                    k.tt("dve", vfF, vfF, sg, ALU.mult)
                    k.tt("dve", vF, vF, vfF, ALU.add)
                    k.dma("sp", pbuf[8 + hp][:, t0:t1], vF)
                    yield

            def epi_gen(hp, yb):
                hc = slice(hp * 128, (hp + 1) * 128)
                for (t0, t1) in RNG:
                    rF, kF, vF, gnv = efb[0], efb[1], efb[2], efb[3]
                    k.dma("sp", rF, pbuf[0 + hp][:, t0:t1])
                    k.dma("sp", kF, pbuf[4 + hp][:, t0:t1])
                    k.dma("sp", vF, pbuf[8 + hp][:, t0:t1])
                    k.dma("pool", G0s, pbuf[13][:, t0:t1])
                    k.dma("pool", G1s, pbuf[14][0:64, t0:t1])
                    k.stt(rF, rF, SPc("rk", l * 4 + hp), kF, ALU.mult, ALU.mult)
                    k.mm(EREG, cst["bo"], rF)
                    yield
                    k.tt("dve", vF, vF, EREG, ALU.mult)
                    yield
                    gn_block(yb[:, t0:t1], SPc("lnw", l * 4 + hp), SPc("lnb", l * 4 + hp), eps_r, gnv, None, None,
                             (efb[4], efb[5], efb[6]), region=EREG)
                    k.tt("dve", gnv, gnv, vF, ALU.add)
                    yield
                    k.mm(EREG, G0up[:, l, hc], G0s, True, False)
                    k.mm(EREG, GVup[0:32, l, hc], G1s[0:32, :], False, True)
                    yield
                    k.tt("dve", oT[:, hp, t0:t1], gnv, EREG, ALU.mult)
                    yield

            def ret_both(hp_):
                yield from ret_pass(hp_, SBr[0], 0)
                yield from ret_pass(hp_, SBr[0], 1)

            def third(hp_):
                yield from ret_both(hp_)
                if hp_ > 0:
                    yield from epi_gen(hp_ - 1, ybs[(hp_ - 1) % 2])
                if l == 1 and hp_ < 3:
                    yield from vres_gen(hp_ + 1)

            if l == 1:
                run_streams([vres_gen(0)])
            for hp in range(4):
                yb = ybs[hp % 2]
                k.memset("dve", yb, 0.0)
                k.memset("dve", ybuf2, 0.0)
                run_streams([rwkv_pass(hp, SB[0], yb), rwkv_pass(hp, SB[1], yb), third(hp)])
                k.cp("act", oT[:, 4 + hp, :], ybuf2)
            run_streams([epi_gen(3, ybs[1])])
            ph.close()
            k.dma("sp", xT, xsp_d)
            if ("oT_%d_%d" % (b, l)) in dbg:
                o_ = V(nc.dram_tensor("dbg_oT_%d_%d" % (b, l), [128, 8, T], BF16, kind="ExternalOutput").ap(), Reg())
                dbg_out["oT_%d_%d" % (b, l)] = o_
                k.dma("sp", o_, oT)
            if "stop_p3" in dbg:
                break
            ph = Phase()
            wob = ph.sb([128, 8, 1024], BF16)
            k.dma("pool", wob, WD["w_out"][l].rr("(c p) n -> p c n", p=128))
            n = 0
            for m in range(8):
                for (t0, t1) in BLK:
                    W = t1 - t0
                    if last and t0 < NCTX:
                        continue
                    pb = n % 8
                    n += 1
                    for kc in range(8):
                        k.mm(PS(pb, 0, W), wob[:, kc, m * 128:(m + 1) * 128], oT[:, kc, t0:t1], start=(kc == 0), stop=(kc == 7))
                    wi = who_idx(t0 < NCTX, b)
                    k.stt(xT[:, m, t0:t1], PS(pb, 0, W), ada[l][:, 16 + m, wi:wi + 1], xT[:, m, t0:t1], ALU.mult, ALU.add)
            ph.close()
            ph = Phase()
            norm(ph, lambda c, ic: A2[l][:, c, who_idx(ic, b):who_idx(ic, b) + 1],
                 lambda c, ic: ada[l][:, 24 + c, who_idx(ic, b):who_idx(ic, b) + 1],
                 lambda c, t0, t1: hT[:, c, t0:t1], BLK[1:] if last else BLK)
            ph.close()
            ph = Phase()
            wgf = [ph.sb([128, 8, 512], BF16) for i in range(2)]
            actb = ph.sb([128, NJ, 512], BF16)
            wfo = [ph.sb([128, NJ, 128], BF16) for i in range(2)]
            sgb = [ph.sb([128, 512]) for i in range(2)]
            wfi_v = WD["w_ffn_in"][l].rr("(c p) n -> p c n", p=128)
            wfo_v = WD["w_ffn_out"][l].rr("(j p) n -> p j n", p=128)
            wi_ = 0
            for (t0, t1) in BLK:
                W = t1 - t0
                if (last and t0 < NCTX) or "skip_ffn" in dbg:
                    continue
                for jg in range(6):
                    nj = 4 if jg < 5 else 2
                    wgate = wgf[0]
                    wup = wgf[1]
                    k.dma("pool", wgate[:, :, 0:nj * 128], wfi_v[:, :, jg * 512:jg * 512 + nj * 128])
                    k.dma("pool", wup[:, :, 0:nj * 128], wfi_v[:, :, HID + jg * 512:HID + jg * 512 + nj * 128])
                    for jj in range(nj):
                        j = jg * 4 + jj
                        pg = (2 * j) % 8
                        pu = (2 * j + 1) % 8
                        for c in range(8):
                            k.mm(PS(pg, 0, W), wgate[:, c, jj * 128:(jj + 1) * 128], hT[:, c, t0:t1], start=(c == 0), stop=(c == 7))
                        for c in range(8):
                            k.mm(PS(pu, 0, W), wup[:, c, jj * 128:(jj + 1) * 128], hT[:, c, t0:t1], start=(c == 0), stop=(c == 7))
                        sgt = sgb[j % 2]
                        k.act(sgt[:, :W], PS(pg, 0, W), AF.Silu)
                        k.tt("dve", actb[:, j, :W], sgt[:, :W], PS(pu, 0, W), ALU.mult)
                for m in range(8):
                    wf = wfo[wi_ % 2]
                    wi_ += 1
                    k.dma("pool", wf, wfo_v[:, :, m * 128:(m + 1) * 128])
                    pb = m % 8
                    for j in range(NJ):
                        k.mm(PS(pb, 0, W), wf[:, j, :], actb[:, j, :W], start=(j == 0), stop=(j == NJ - 1))
                    wi = who_idx(t0 < NCTX, b)
                    k.stt(xT[:, m, t0:t1], PS(pb, 0, W), ada[l][:, 40 + m, wi:wi + 1], xT[:, m, t0:t1], ALU.mult, ALU.add)
            ph.close()
            if ("xT_%d_%d" % (b, l)) in dbg:
                o_ = V(nc.dram_tensor("dbg_xT_%d_%d" % (b, l), [128, 8, T], F32, kind="ExternalOutput").ap(), Reg())
                dbg_out["xT_%d_%d" % (b, l)] = o_
                k.dma("sp", o_, xT)
        if "stop_p2" in dbg or "stop_p3" in dbg:
            continue
        ph = Phase()
        obuf = [ph.sb([128, T]) for i in range(2)]
        rst_all = ph.sb([128, T])
        sq = [ph.sb([128, 512]) for i in range(2)]
        n = 0
        for (t0, t1) in BLK[1:]:
            W = t1 - t0
            for c in range(8):
                s = sq[n % 2]
                n += 1
                k.act(s[:, :W], xT[:, c, t0:t1], AF.Square)
                k.mm(PS(0, 0, W), cst["onesr"], s[:, :W], start=(c == 0), stop=(c == 7))
            k.act(rst_all[:, t0:t1], PS(0, 0, W), AF.Sqrt, bias=eps_n)
            k.recip(rst_all[:, t0:t1], rst_all[:, t0:t1])
        for c in range(8):
            ob = obuf[c % 2]
            k.tt("dve", ob[:, NCTX:T], xT[:, c, NCTX:T], rst_all[:, NCTX:T], ALU.mult)
            k.act(ob[:, NCTX:T], ob[:, NCTX:T], AF.Copy, scale=SPc("normf", c))
            k.dma("sp", outT_d[b, c * 128:(c + 1) * 128, :], ob[:, NCTX:T])
        ph.close()
    k.wait_all("sp", [outT_d.g[0]] + [v.g[0] for v in dbg_out.values()])
    es.close()
    return nc, k, dbg_out


_CACHE = {}


def kernel(**inp):
    inp = {k_: np.asarray(v) for k_, v in inp.items()}
    ncores = 8
    B = inp["x"].shape[0]
    nb = B // ncores
    if "nc" not in _CACHE:
        _CACHE["nc"] = build(nb)[0]
    nc = _CACHE["nc"]
    xT = np.ascontiguousarray(np.transpose(inp["x"], (0, 2, 1)))
    ctxT = np.ascontiguousarray(np.transpose(inp["ctx"], (0, 2, 1)))
    sp = build_sp(inp)
    consts = build_consts()
    in_maps = []
    for i in range(ncores):
        m = {"xT": xT[i * nb:(i + 1) * nb], "ctxT": ctxT[i * nb:(i + 1) * nb]}
        cT = np.zeros((D, 5), np.float32)
        cT[:, :nb] = inp["c"][i * nb:(i + 1) * nb].T
        cT[:, 4] = inp["c_ctx"]
        m["cT"] = cT
        m["sp"] = sp
        for n_ in CONST_SHAPES:
            m["c_" + n_] = consts[n_]
        for n_ in WEIGHT_SHAPES:
            m[n_] = inp[n_]
        in_maps.append(m)
    res = run_bass_kernel_spmd(nc, in_maps, core_ids=list(range(ncores)))
    outT = np.concatenate([np.asarray(r["outT"]) for r in res.results], axis=0)
    return np.ascontiguousarray(np.transpose(outT, (0, 2, 1))).astype(np.float32)
```

```python
import numpy as np
import ml_dtypes
from contextlib import ExitStack
import concourse.bass as bass
import concourse.mybir as mybir
from concourse.bass_utils import run_bass_kernel_spmd

F32 = mybir.dt.float32
BF16 = mybir.dt.bfloat16
F32R = mybir.dt.float32r
DBL_R = False
ALU = mybir.AluOpType
AF = mybir.ActivationFunctionType

D = 1024
T = 2304
NCTX = 256
SEQ = 2048
DEPTH = 2
NIN = 4384
HID = 2816
NJ = HID // 128
CDEC = 0.6065306597126334
BLK = [(0, 256), (256, 768), (768, 1280), (1280, 1792), (1792, 2304)]
NPT = 35
SAME = True
NDS = 24

SPL = {}
_o = 0
for _n, _w in [("norm1", 16), ("norm2", 16), ("normf", 8), ("bada", 96), ("mu", 2 * 15 * 7), ("w0", 16), ("a0", 16),
               ("kk", 8), ("ka", 8), ("rk", 8), ("v0", 4), ("lnw", 8), ("lnb", 8), ("rgw", 8), ("rgb", 8)]:
    SPL[_n] = _o
    _o += _w
NSP = _o


def rwkv_tile_cols(l):
    tiles = []
    for i in range(12):
        tiles.append([i * 128 + p for p in range(128)])
    tiles.append([1536 + p for p in range(128)])
    tiles.append([1664 + p for p in range(128)])
    g1 = [1792 + p for p in range(32)]
    if l == 1:
        g1 += [("v", ch) for ch in range(32)]
    tiles.append(g1)
    return tiles


def build_sp(inp):
    sp = np.zeros((128, NSP), np.float32)
    p = np.arange(128)
    for l in range(2):
        for c in range(8):
            sp[:, SPL["norm1"] + l * 8 + c] = inp["norm1"][l, c * 128 + p]
            sp[:, SPL["norm2"] + l * 8 + c] = inp["norm2"][l, c * 128 + p]
        for q in range(48):
            sp[:, SPL["bada"] + l * 48 + q] = inp["b_ada"][l, q * 128 + p]
        tiles = rwkv_tile_cols(l)
        for ti, cols in enumerate(tiles):
            base = SPL["mu"] + (l * 15 + ti) * 7
            for pp, col in enumerate(cols):
                if isinstance(col, tuple):
                    ch = col[1]
                    m = inp["mu_vres"][0, ch]
                    ld, cd = ch // 8, ch // 16
                else:
                    m = inp["mu_rwkv"][l, col]
                    ld, cd = col // 456, col // 912
                sp[pp, base + 0] = m
                sp[pp, base + 1 + ld] = m
                sp[pp, base + 5 + cd] = m
        for d in range(2):
            for hp in range(4):
                sp[:, SPL["w0"] + (l * 2 + d) * 4 + hp] = inp["w0"][l, d, hp * 128 + p]
                sp[:, SPL["a0"] + (l * 2 + d) * 4 + hp] = inp["a0"][l, d, hp * 128 + p]
        for hp in range(4):
            sp[:, SPL["kk"] + l * 4 + hp] = inp["k_k"][l, hp * 128 + p]
            sp[:, SPL["ka"] + l * 4 + hp] = inp["k_a"][l, hp * 128 + p]
            sp[:, SPL["rk"] + l * 4 + hp] = inp["r_k"][l].reshape(512)[hp * 128 + p]
            sp[:, SPL["lnw"] + l * 4 + hp] = inp["ln_x_w"][l, hp * 128 + p]
            sp[:, SPL["lnb"] + l * 4 + hp] = inp["ln_x_b"][l, hp * 128 + p]
            sp[:, SPL["rgw"] + l * 4 + hp] = inp["ret_gn_w"][l, hp * 128 + p]
            sp[:, SPL["rgb"] + l * 4 + hp] = inp["ret_gn_b"][l, hp * 128 + p]
    for c in range(8):
        sp[:, SPL["normf"] + c] = inp["norm_f"][c * 128 + p]
    for hp in range(4):
        sp[:, SPL["v0"] + hp] = inp["v0"][0, hp * 128 + p]
    return sp


def build_consts():
    c = {}
    s = np.arange(128)[:, None]
    t = np.arange(128)[None, :]
    c["ident"] = np.eye(128, dtype=np.float32)
    c["onesr"] = np.full((128, 128), 1.0 / 1024, np.float32)
    bo = np.zeros((128, 128), np.float32)
    bo[:64, :64] = 1
    bo[64:, 64:] = 1
    c["bo"] = bo
    pm = np.zeros((128, 128), np.float32)
    for m in range(128):
        n = m % 64
        pm[(m - n) + ((n + 32) % 64), m] = 1
    c["pm"] = pm
    m2 = np.zeros((2, 128, 256), np.float32)
    m2[0, :, :128] = s < t
    m2[0, :, 128:] = s <= t
    m2[1, :, :128] = s > t
    m2[1, :, 128:] = s >= t
    c["mask2"] = m2
    mm_ = np.zeros((2, 128, 128), np.float32)
    mm_[0] = s > t
    mm_[1] = s < t
    c["mmask"] = mm_
    rm = np.ones((128, 512), np.float32)
    rm[:, ::128] = 0
    c["rmask"] = rm
    tok = np.arange(SEQ)
    row = (tok // 64).astype(np.float32)
    col = (tok % 64).astype(np.float32)
    nf = 16
    freqs = (np.float32(10000.0) ** (-np.arange(nf, dtype=np.float32) / nf)).astype(np.float32)
    ang = np.concatenate([row[:, None] * freqs, col[:, None] * freqs], -1).astype(np.float32)
    cosT = np.ones((128, T), np.float32)
    ssinT = np.zeros((128, T), np.float32)
    for p_ in range(128):
        n = p_ % 64
        cosT[p_, NCTX:] = np.cos(ang[:, n % 32])
        ssinT[p_, NCTX:] = np.sin(ang[:, n % 32]) * (-1.0 if n < 32 else 1.0)
    c["cosT"] = cosT
    c["ssinT"] = ssinT
    lg = np.log(1.0 - 2.0 ** (-5.0 - np.arange(8, dtype=np.float64)))
    dt_ = np.zeros((2, 4, 128, 2, 128), np.float32)
    qd = np.zeros((2, 4, 128, 128), np.float32)
    kd = np.zeros((2, 4, 128, 128), np.float32)
    cd = np.zeros((128, 8), np.float32)
    sv = np.arange(128)
    for d in range(2):
        for hp in range(4):
            for e in range(2):
                h = 2 * hp + e
                g = lg[h] if d == 0 else lg[7 - h]
                if d == 0:
                    dt_[d, hp, :, e, :] = np.where(t >= s, np.exp(g * np.maximum(t - s, 0)), 0)
                    qd[d, hp, 64 * e:64 * e + 64, :] = np.exp(g * (sv + 1.0))[None, :]
                    kd[d, hp, :, 64 * e:64 * e + 64] = np.exp(g * (127.0 - sv))[:, None]
                else:
                    dt_[d, hp, :, e, :] = np.where(s >= t, np.exp(g * np.maximum(s - t, 0)), 0)
                    qd[d, hp, 64 * e:64 * e + 64, :] = np.exp(g * (128.0 - sv))[None, :]
                    kd[d, hp, :, 64 * e:64 * e + 64] = np.exp(g * sv)[:, None]
                cd[64 * e:64 * e + 64, d * 4 + hp] = np.exp(g * 128.0)
    c["dT"] = dt_.reshape(8, 128, 256)
    c["qd"] = qd.reshape(8, 128, 128)
    c["kd"] = kd.reshape(8, 128, 128)
    c["cd"] = cd
    return c


CONST_SHAPES = {"ident": [128, 128], "onesr": [128, 128], "bo": [128, 128], "pm": [128, 128], "mask2": [2, 128, 256],
                "mmask": [2, 128, 128], "rmask": [128, 512], "cosT": [128, T], "ssinT": [128, T],
                "dT": [8, 128, 256], "qd": [8, 128, 128], "kd": [8, 128, 128], "cd": [128, 8]}
WEIGHT_SHAPES = {"w_ada": [2, 1024, 6144], "w_in": [2, 1024, NIN], "w_vres_down": [1, 1024, 32],
                 "w_up": [2, 2, 64, 512], "a_up": [2, 2, 64, 512], "g_up": [2, 160, 512], "v_up": [1, 32, 512],
                 "w_out": [2, 1024, 1024], "w_ffn_in": [2, 1024, 2 * HID], "w_ffn_out": [2, HID, 1024]}


class Reg:
    __slots__ = ("w", "r", "excl")

    def __init__(self, excl=False):
        self.w = None
        self.r = {}
        self.excl = excl


class V:
    def __init__(self, ap, g):
        self.ap = ap
        self.g = g if isinstance(g, list) else [g]

    def __getitem__(self, idx):
        return V(self.ap[idx], self.g)

    def rr(self, pat, **kw):
        return V(self.ap.rearrange(pat, **kw), self.g)

    def bc(self, shape):
        return V(self.ap.to_broadcast(shape), self.g)

    def bitcast(self, dt):
        return V(self.ap.bitcast(dt), self.g)


class KB:
    def __init__(self, nc, es):
        self.nc = nc
        self.es = es
        self.eng = {"pe": nc.tensor, "dve": nc.vector, "act": nc.scalar, "pool": nc.gpsimd, "sp": nc.sync}
        self.sem = {e: es.enter_context(nc.semaphore("s_" + e)) for e in self.eng}
        self.cnt = {e: 0 for e in self.eng}
        self.seen = {e: {} for e in self.eng}
        self.dsem = [es.enter_context(nc.semaphore("d%d" % i)) for i in range(NDS)]
        self.dcnt = [0] * NDS
        self.dnext = 0
        self.nins = 0

    def sb(self, name, shape, dt=F32):
        t = self.es.enter_context(self.nc.sbuf_tensor(name, list(shape), dt))
        return V(t[:], Reg())

    def _wait(self, e, ev):
        key, sem, val = ev
        if self.seen[e].get(key, 0) >= val:
            return
        if key == e and (e == "pe" or not SAME):
            return
        self.eng[e].wait_ge(sem, val)
        self.seen[e][key] = val
        self.nins += 1

    def _deps(self, e, reads, writes):
        for r in reads:
            if r.w is not None:
                self._wait(e, r.w)
        for w in writes:
            if w.w is not None:
                self._wait(e, w.w)
            for ev in list(w.r.values()):
                self._wait(e, ev)

    def _commit(self, ev, reads, writes):
        for r in reads:
            old = r.r.get(ev[0])
            if old is None or old[2] < ev[2]:
                r.r[ev[0]] = ev
        for w in writes:
            w.w = ev
            w.r = {}

    def op(self, e, fn, reads, writes):
        ex = [r for r in reads if r.excl]
        if ex:
            writes = list(writes) + ex
        self._deps(e, reads, writes)
        ins = fn(self.eng[e])
        self.cnt[e] += 1
        ins.then_inc(self.sem[e], 1)
        self.nins += 1
        self._commit((e, self.sem[e], self.cnt[e]), reads, writes)

    def dma(self, q, out, in_):
        reads, writes = in_.g, out.g
        i = self.dnext
        self.dnext = (i + 1) % NDS
        key = ("d", i)
        if self.dcnt[i] > 0:
            self._wait(q, (key, self.dsem[i], self.dcnt[i]))
        self._deps(q, reads, writes)
        self.dcnt[i] += 16
        self.eng[q].dma_start(out=out.ap, in_=in_.ap).then_inc(self.dsem[i], 16)
        self.nins += 1
        self._commit((key, self.dsem[i], self.dcnt[i]), reads, writes)

    def wait_all(self, e, regs):
        self._deps(e, regs, [])

    def tt(self, e, out, in0, in1, op):
        self.op(e, lambda E: E.tensor_tensor(out=out.ap, in0=in0.ap, in1=in1.ap, op=op), in0.g + in1.g, out.g)

    def ts(self, e, out, in0, s1, s2=None, op0=ALU.mult, op1=None):
        rd = list(in0.g)
        a1 = s1
        a2 = s2
        if isinstance(s1, V):
            rd += s1.g
            a1 = s1.ap
        if isinstance(s2, V):
            rd += s2.g
            a2 = s2.ap
        if op1 is None:
            self.op(e, lambda E: E.tensor_scalar(out=out.ap, in0=in0.ap, scalar1=a1, scalar2=None, op0=op0), rd, out.g)
        else:
            self.op(e, lambda E: E.tensor_scalar(out=out.ap, in0=in0.ap, scalar1=a1, scalar2=a2, op0=op0, op1=op1), rd, out.g)

    def stt(self, out, in0, sc, in1, op0, op1):
        rd = in0.g + in1.g
        a = sc
        if isinstance(sc, V):
            rd = rd + sc.g
            a = sc.ap
        self.op("dve", lambda E: E.scalar_tensor_tensor(out=out.ap, in0=in0.ap, scalar=a, in1=in1.ap, op0=op0, op1=op1), rd, out.g)

    def act(self, out, in_, func, bias=None, scale=1.0):
        rd = list(in_.g)
        kw = {}
        if isinstance(bias, V):
            rd += bias.g
            kw["bias"] = bias.ap
        elif bias is not None:
            kw["bias"] = bias
        if isinstance(scale, V):
            rd += scale.g
            kw["scale"] = scale.ap
        else:
            kw["scale"] = scale
        self.op("act", lambda E: E.activation(out=out.ap, in_=in_.ap, func=func, **kw), rd, out.g)

    def cp(self, e, out, in_):
        if e == "act":
            self.act(out, in_, AF.Copy)
        else:
            self.op(e, lambda E: E.tensor_copy(out=out.ap, in_=in_.ap), in_.g, out.g)

    def memset(self, e, out, val):
        self.op(e, lambda E: E.memset(out.ap, val), [], out.g)

    def recip(self, out, in_):
        self.op("dve", lambda E: E.reciprocal(out=out.ap, in_=in_.ap), in_.g, out.g)

    def mm(self, out, lhsT, rhs, start=True, stop=True):
        self.op("pe", lambda E: E.matmul(out.ap, lhsT=lhsT.ap, rhs=rhs.ap, start=start, stop=stop), lhsT.g + rhs.g, out.g)

    def tr(self, out, in_, ident):
        self.op("pe", lambda E: E.transpose(out.ap, in_.ap, ident.ap), in_.g + ident.g, out.g)

    def scan(self, out, d0, d1, init, op0, op1):
        self.op("dve", lambda E: E.tensor_tensor_scan(out=out.ap, data0=d0.ap, data1=d1.ap, initial=init, op0=op0, op1=op1),
                d0.g + d1.g, out.g)


def build(nb, nlayers=DEPTH, dbg=()):
    nc = bass.Bass("TRN2", target_bir_lowering=False)
    es = ExitStack()
    k = KB(nc, es)

    def din(name, shape):
        return V(nc.dram_tensor(name, list(shape), F32, kind="ExternalInput").ap(), Reg())

    xT_d = din("xT", [nb, D, SEQ])
    ctxT_d = din("ctxT", [nb, D, NCTX])
    cT_d = din("cT", [D, 5])
    sp_d = din("sp", [128, NSP])
    CD = {n: din("c_" + n, s) for n, s in CONST_SHAPES.items()}
    WD = {n: din(n, s) for n, s in WEIGHT_SHAPES.items()}
    outT_d = V(nc.dram_tensor("outT", [nb, D, SEQ], F32, kind="ExternalOutput").ap(), Reg())
    pbuf_ap = nc.dram_tensor("pbuf", [NPT, 128, T], F32, kind="Internal").ap()
    pbuf = [V(pbuf_ap[i], Reg()) for i in range(NPT)]
    vf_ap = nc.dram_tensor("vfirst", [4, 128, T], F32, kind="Internal").ap()
    vfd = [V(vf_ap[i], Reg()) for i in range(4)]
    dbg_out = {}

    def dump(name, v, shape, q="sp"):
        if name in dbg:
            o = V(nc.dram_tensor("dbg_" + name, list(shape), F32, kind="ExternalOutput").ap(), Reg())
            dbg_out[name] = o
            k.dma(q, o, v)

    xT_t = es.enter_context(nc.sbuf_tensor("xT_sb", [128, 8 * T], F32))
    xT = V(xT_t[:].rearrange("p (c t) -> p c t", c=8), Reg())
    xsp_d = V(nc.dram_tensor("xspill", [128, 8, T], F32, kind="Internal").ap(), Reg())
    hT = k.sb("hT_sb", [128, 8, T], BF16)
    psall = es.enter_context(nc.psum_tensor("ps", [128, 8, 512], F32))
    PSR = [Reg(excl=True) for _ in range(8)]

    def PS(b, lo=0, hi=512):
        return V(psall[:, b, lo:hi], PSR[b])

    def PS2(b0, lo, hi):
        return V(psall[:, b0:b0 + 2, lo:hi], [PSR[b0], PSR[b0 + 1]])

    spt = k.sb("spt", [128, NSP])
    k.dma("sp", spt, sp_d)

    def SPc(name, idx):
        return spt[:, SPL[name] + idx:SPL[name] + idx + 1]

    cst = {}
    for n in ["onesr", "bo", "rmask"]:
        cst[n] = k.sb("sc_" + n, CONST_SHAPES[n])
        k.dma("sp", cst[n], CD[n])
    for n in ["ident", "pm"]:
        cst[n] = k.sb("sc_" + n, CONST_SHAPES[n], BF16)
        k.dma("pool", cst[n], CD[n])
    cst["mask2"] = k.sb("sc_mask2", [128, 2, 256])
    cst["mmask"] = k.sb("sc_mmask", [128, 2, 128])
    for d in range(2):
        k.dma("sp", cst["mask2"][:, d, :], CD["mask2"][d])
        k.dma("sp", cst["mmask"][:, d, :], CD["mmask"][d])
    cst["cd"] = k.sb("sc_cd", [128, 8])
    k.dma("sp", cst["cd"], CD["cd"])
    WAup = k.sb("WAup", [128, 2, 2, 512], BF16)
    G0up = k.sb("G0up", [128, 2, 512], BF16)
    GVup = k.sb("GVup", [64, 2, 512], BF16)
    for l in range(2):
        for d in range(2):
            k.dma("pool", WAup[0:64, l, d, :], WD["w_up"][l, d])
            k.dma("pool", WAup[64:128, l, d, :], WD["a_up"][l, d])
        k.dma("pool", G0up[:, l, :], WD["g_up"][l, 0:128, :])
        k.dma("pool", GVup[0:32, l, :], WD["g_up"][l, 128:160, :])
    k.dma("pool", GVup[32:64, 1, :], WD["v_up"][0])
    omm = k.sb("omm", [128, 30])
    mu0 = spt[:, SPL["mu"]:SPL["mu"] + 210].rr("p (t s) -> p t s", s=7)[:, :, 0]
    k.ts("dve", omm, mu0, -1.0, 1.0, ALU.mult, ALU.add)
    omka = k.sb("omka", [128, 8])
    k.ts("dve", omka, spt[:, SPL["ka"]:SPL["ka"] + 8], -1.0, 1.0, ALU.mult, ALU.add)

    ARW = 15360
    arena = es.enter_context(nc.sbuf_tensor("arena", [128, ARW], F32))

    class Phase:
        def __init__(self, extra=None):
            self.off = 0
            self.regs = []
            self.backs = [(arena, ARW)] + ([extra] if extra is not None else [])
            self.bi = 0

        def sb(self, shape, dt=F32):
            n = 1
            for s_ in shape[1:]:
                n *= s_
            words = n if dt == F32 else (n + 1) // 2
            words = (words + 7) // 8 * 8
            if self.off + words > self.backs[self.bi][1]:
                self.bi += 1
                self.off = 0
                assert self.bi < len(self.backs), "arena overflow"
                assert words <= self.backs[self.bi][1]
            ap = self.backs[self.bi][0][0:shape[0], self.off:self.off + words]
            self.off += words
            if dt != F32:
                ap = ap.bitcast(dt)
            ap = ap[:, 0:n]
            if len(shape) > 2:
                names = " ".join("d%d" % i for i in range(len(shape) - 1))
                ap = ap.rearrange("p (%s) -> p %s" % (names, names), **{"d%d" % i: shape[i + 1] for i in range(len(shape) - 1)})
            r = Reg()
            self.regs.append(r)
            return V(ap, r)

        def close(self):
            for e in ("pe", "dve", "act", "pool", "sp"):
                k._deps(e, [], self.regs)

    scT = k.sb("scT", [128, 8, 5])
    k.dma("sp", scT, cT_d.rr("(c p) w -> p c w", p=128))
    k.act(scT, scT, AF.Silu)
    ada = [k.sb("ada%d" % l, [128, 48, 5]) for l in range(2)]
    ph0 = Phase()
    wadab = [ph0.sb([128, 8, 512]) for i in range(2)]
    it = 0
    for l in range(nlayers):
        for qg in range(12):
            wb = wadab[it % 2]
            it += 1
            k.dma("sp", wb, WD["w_ada"][l].rr("(c p) n -> p c n", p=128)[:, :, qg * 512:(qg + 1) * 512])
            for qq in range(4):
                q = qg * 4 + qq
                for c in range(8):
                    k.mm(PS(0, q * 5, q * 5 + 5), wb[:, c, qq * 128:(qq + 1) * 128], scT[:, c, :], start=(c == 0), stop=(c == 7))
        k.tt("dve", ada[l], PS(0, 0, 240).rr("p (q w) -> p q w", w=5),
             spt[:, SPL["bada"] + l * 48:SPL["bada"] + (l + 1) * 48].rr("p (q o) -> p q o", o=1).bc([128, 48, 5]), ALU.add)
    ph0.close()
    A1 = [k.sb("A1_%d" % l, [128, 8, 5]) for l in range(2)]
    A2 = [k.sb("A2_%d" % l, [128, 8, 5]) for l in range(2)]
    for l in range(nlayers):
        for (A, nm, q0) in ((A1, "norm1", 8), (A2, "norm2", 32)):
            k.ts("dve", A[l], ada[l][:, q0:q0 + 8, :], 1.0, None, ALU.add)
            k.tt("dve", A[l], A[l], spt[:, SPL[nm] + l * 8:SPL[nm] + l * 8 + 8].rr("p (c o) -> p c o", o=1).bc([128, 8, 5]), ALU.mult)
    zero_col = k.sb("zero_col", [128, 1])
    k.memset("dve", zero_col, 0.0)
    eps_n = k.sb("eps_n", [128, 1])
    k.memset("dve", eps_n, 1e-6)
    eps_r = k.sb("eps_r", [128, 1])
    k.memset("dve", eps_r, 64e-5)
    eps_t = k.sb("eps_t", [128, 1])
    k.memset("dve", eps_t, 1e-5)

    def norm(ph, scale_fn, bias_fn, out_fn, blocks):
        sq = [ph.sb([128, 512]) for i in range(2)]
        rstd = ph.sb([128, 512])
        ntmp = [ph.sb([128, 512]) for i in range(2)]
        n = 0
        for (t0, t1) in blocks:
            W = t1 - t0
            for c in range(8):
                s = sq[n % 2]
                n += 1
                k.act(s[:, :W], xT[:, c, t0:t1], AF.Square)
                k.mm(PS(0, 0, W), cst["onesr"], s[:, :W], start=(c == 0), stop=(c == 7))
            k.act(rstd[:, :W], PS(0, 0, W), AF.Sqrt, bias=eps_n)
            k.recip(rstd[:, :W], rstd[:, :W])
            for c in range(8):
                tmp = ntmp[c % 2]
                k.tt("dve", tmp[:, :W], xT[:, c, t0:t1], rstd[:, :W], ALU.mult)
                b_ = bias_fn(c, t0 < NCTX)
                k.act(out_fn(c, t0, t1), tmp[:, :W], AF.Identity, bias=(b_ if b_ is not None else zero_col), scale=scale_fn(c, t0 < NCTX))

    def who_idx(is_ctx, b):
        return 4 if is_ctx else b

    RNG = [(256 * i, 256 * (i + 1)) for i in range(9)]
    RW = 256

    for b in range(nb):
        for c in range(8):
            k.dma("sp", xT[:, c, NCTX:T], xT_d[b, c * 128:(c + 1) * 128, :])
            k.dma("sp", xT[:, c, 0:NCTX], ctxT_d[b, c * 128:(c + 1) * 128, :])
        for l in range(nlayers):
            last = (l == DEPTH - 1)
            ph = Phase()
            norm(ph, lambda c, ic: A1[l][:, c, who_idx(ic, b):who_idx(ic, b) + 1],
                 lambda c, ic: ada[l][:, 0 + c, who_idx(ic, b):who_idx(ic, b) + 1],
                 lambda c, t0, t1: hT[:, c, t0:t1], BLK)
            ph.close()
            if ("hT_%d_%d" % (b, l)) in dbg:
                o_ = V(nc.dram_tensor("dbg_hT_%d_%d" % (b, l), [128, 8, T], BF16, kind="ExternalOutput").ap(), Reg())
                dbg_out["hT_%d_%d" % (b, l)] = o_
                k.dma("sp", o_, hT)
            ph = Phase()
            ubuf = [ph.sb([128, T]) for i in range(2)]
            lbuf = [ph.sb([128, T]) for i in range(1)]
            wg = [ph.sb([128, 8, 512], BF16) for i in range(2)]
            groups = [(0, 512), (512, 512), (1024, 512), (1536, 288)] + [(1824 + 512 * i, 512) for i in range(5)]
            ti = 0
            gi = 0
            for (c0, ncol) in groups:
                wgb = wg[gi % 2]
                gi += 1
                k.dma("pool", wgb[:, :, 0:ncol], WD["w_in"][l].rr("(c p) n -> p c n", p=128)[:, :, c0:c0 + ncol])
                if c0 == 1536 and l == 1:
                    k.dma("pool", wgb[:, :, 288:320], WD["w_vres_down"][0].rr("(c p) n -> p c n", p=128))
                if c0 == 1536:
                    tl = [(0, 128), (128, 128), (256, 64 if l == 1 else 32)]
                else:
                    tl = [(i * 128, 128) for i in range(4)]
                for (off, M) in tl:
                    is_rwkv = ti < 15
                    u = ubuf[ti % 2]
                    o = lbuf[0]
                    dst = u if is_rwkv else o
                    for bi, (t0, t1) in enumerate(BLK):
                        W = t1 - t0
                        pb = (ti * 5 + bi) % 8
                        for c in range(8):
                            k.mm(PS(pb, 0, W)[0:M], wgb[:, c, off:off + M], hT[:, c, t0:t1], start=(c == 0), stop=(c == 7))
                        k.cp("act" if (bi % 2 == 0) else "dve", dst[0:M, t0:t1], PS(pb, 0, W)[0:M])
                    if is_rwkv:
                        mub = SPL["mu"] + (l * 15 + ti) * 7
                        k.act(o[0:M, :], u[0:M, :], AF.Copy, scale=omm[0:M, l * 15 + ti:l * 15 + ti + 1])
                        cols = rwkv_tile_cols(l)[ti]
                        ldirs = sorted(set((c_[1] // 8) if isinstance(c_, tuple) else (c_ // 456) for c_ in cols))
                        cdirs = sorted(set((c_[1] // 16) if isinstance(c_, tuple) else (c_ // 912) for c_ in cols))
                        uL = u[0:M, NCTX:T].rr("p (r c) -> p r c", c=64)
                        oL = o[0:M, NCTX:T].rr("p (r c) -> p r c", c=64)
                        for dr in ldirs:
                            m = spt[0:M, mub + 1 + dr:mub + 2 + dr]
                            if dr == 0:
                                k.stt(oL[:, :, 1:64], uL[:, :, 0:63], m, oL[:, :, 1:64], ALU.mult, ALU.add)
                            elif dr == 1:
                                k.stt(oL[:, :, 0:63], uL[:, :, 1:64], m, oL[:, :, 0:63], ALU.mult, ALU.add)
                            elif dr == 2:
                                k.stt(o[0:M, NCTX + 64:T], u[0:M, NCTX:T - 64], m, o[0:M, NCTX + 64:T], ALU.mult, ALU.add)
                            else:
                                k.stt(o[0:M, NCTX:T - 64], u[0:M, NCTX + 64:T], m, o[0:M, NCTX:T - 64], ALU.mult, ALU.add)
                        for dr in cdirs:
                            m = spt[0:M, mub + 5 + dr:mub + 6 + dr]
                            if dr == 0:
                                k.stt(o[0:M, 1:NCTX], u[0:M, 0:NCTX - 1], m, o[0:M, 1:NCTX], ALU.mult, ALU.add)
                            else:
                                k.stt(o[0:M, 0:NCTX - 1], u[0:M, 1:NCTX], m, o[0:M, 0:NCTX - 1], ALU.mult, ALU.add)
                        if ti == 12:
                            k.act(o[0:64, :], o[0:64, :], AF.Tanh)
                        elif ti == 13:
                            k.act(o[0:128, :], o[0:128, :], AF.Sigmoid)
                        elif ti == 14:
                            k.act(o[0:32, :], o[0:32, :], AF.Sigmoid)
                    k.dma("sp", pbuf[ti][0:M, :], o[0:M, :])
                    if l == 0 and 8 <= ti < 12:
                        k.dma("sp", vfd[ti - 8], o[0:M, :])
                    ti += 1
            assert ti == NPT
            ph.close()
            if ("pbuf_%d_%d" % (b, l)) in dbg:
                o_ = V(nc.dram_tensor("dbg_pbuf_%d_%d" % (b, l), [NPT, 128, T], F32, kind="ExternalOutput").ap(), Reg())
                dbg_out["pbuf_%d_%d" % (b, l)] = o_
                for i_ in range(NPT):
                    k.dma("sp", o_[i_], pbuf[i_])
            if "stop_p2" in dbg:
                break
            k.dma("sp", xsp_d, xT)
            for e_ in ("pe", "dve", "act", "pool", "sp"):
                k._deps(e_, [], xT.g)
            ph = Phase(extra=(xT_t, 8 * T))
            G0s = ph.sb([128, RW], BF16)
            G1s = ph.sb([64, RW], BF16)
            ybuf = ph.sb([128, T])
            oT = hT
            W = RW
            nch = 2

            class St:
                pass

            def alloc_stream(d, kind):
                B = St()
                B.d = d
                B.fb = [ph.sb([128, RW]) for i in range(12)]
                B.hb = [ph.sb([128, RW], BF16) for i in range(6)]
                B.Ktok = ph.sb([128, 2, 128], BF16)
                B.Vtok = ph.sb([128, 2, 128], BF16)
                B.Sst = ph.sb([128, 64])
                B.Sbf = ph.sb([128, 64], BF16)
                if kind == "rwkv":
                    B.WAs = ph.sb([128, RW], BF16)
                    B.ARb = ph.sb([128, 2, 2, 128], BF16)
                    B.Btok = ph.sb([128, 2, 128], BF16)
                    B.gC = ph.sb([128, 2])
                    B.Am = ph.sb([128, 2, 2, 256], BF16)
                    B.NMb = [ph.sb([128, 2, 2, 128]) for i in range(2)]
                    B.Ppb = [ph.sb([128, 2, 128]) for i in range(2)]
                    B.Xsb = ph.sb([128, 2, 64])
                    B.Usb = ph.sb([128, 2, 64], BF16)
                else:
                    B.scm = ph.sb([128, 2, 128], BF16)
                    B.dTt = ph.sb([128, 256])
                    B.qdt = ph.sb([128, 128])
                    B.kdt = ph.sb([128, 128])
                B.off = 4 * d
                return B
            SB = [alloc_stream(0, "rwkv"), alloc_stream(1, "rwkv")]
            SBr = [alloc_stream(0, "ret")]
            ybuf2 = ph.sb([128, T])
            ybs = [ybuf, ph.sb([128, T])]
            efb = [ph.sb([128, RW]) for i in range(7)]

            def v3(x):
                return x.rr("p (c t) -> p c t", t=128)

            def run_streams(gens):
                gens = list(gens)
                while gens:
                    for g in list(gens):
                        try:
                            next(g)
                        except StopIteration:
                            gens.remove(g)

            def gn_block(src, wcol, bcol, eps_col, out_f, pa, pb_, gnb, region=None):
                cen, sqv, rs = gnb
                ra = region if region is not None else PS(pa, 0, W)
                rb = region if region is not None else PS(pb_, 0, W)
                k.mm(ra, cst["bo"], src, True, True)
                k.stt(cen, ra, -1.0 / 64, src, ALU.mult, ALU.add)
                k.act(sqv, cen, AF.Square)
                k.mm(rb, cst["bo"], sqv, True, True)
                k.act(rs, rb, AF.Sqrt, bias=eps_col, scale=1.0 / 64)
                k.recip(rs, rs)
                k.tt("dve", cen, cen, rs, ALU.mult)
                k.act(out_f, cen, AF.Identity, bias=bcol, scale=wcol)

            def transposes(src_bf, dst_tok, bank, mul_tab=None, region=None):
                pt = (region if region is not None else PS(bank)).bitcast(BF16)
                for c in range(nch):
                    k.tr(pt[:, c * 128:(c + 1) * 128], src_bf[:, c * 128:(c + 1) * 128], cst["ident"])
                ptv = pt[:, 0:nch * 128].rr("p (c f) -> p c f", f=128)
                if mul_tab is None:
                    k.cp("act", dst_tok, ptv)
                else:
                    k.tt("dve", dst_tok, ptv, mul_tab.rr("p (o f) -> p o f", o=1).bc([128, nch, 128]), ALU.mult)

            def rwkv_pass(hp, B, ybuf):
                d = B.d
                hc = slice(hp * 128, (hp + 1) * 128)
                fb, hb = B.fb, B.hb
                ARb, Btok, Ktok, Vtok, gC, Am, NMb, Ppb = B.ARb, B.Btok, B.Ktok, B.Vtok, B.gC, B.Am, B.NMb, B.Ppb
                Xsb, Usb, Sst, Sbf, WAs = B.Xsb, B.Usb, B.Sst, B.Sbf, B.WAs

                def P(role, lo=0, hi=512):
                    return PS((role + B.off) % 8, lo, hi)

                def P2(role, lo, hi):
                    return PS2((role + B.off) % 8, lo, hi)
                order = list(range(9)) if d == 0 else [0] + list(range(8, 0, -1))
                k.memset("dve", Sst, 0.0)
                k.memset("dve", Sbf, 0.0)
                for ri in order:
                    t0, t1 = RNG[ri]
                    rF, kF, vF = fb[0], fb[1], fb[2]
                    k.dma("sp", rF, pbuf[0 + hp][:, t0:t1])
                    k.dma("sp", kF, pbuf[4 + hp][:, t0:t1])
                    k.dma("sp", vF, pbuf[8 + hp][:, t0:t1])
                    k.dma("pool", WAs, pbuf[12][:, t0:t1])
                    sw, Lr, Lin, Lex, icl, kk, t1b, t2b, Eb = fb[3], fb[4], fb[5], fb[6], fb[7], fb[8], fb[9], fb[10], fb[11]
                    k.mm(P(0, 0, W), WAup[0:64, l, d, hc], WAs[0:64, :])
                    k.mm(P(1, 0, W), WAup[64:128, l, d, hc], WAs[64:128, :])
                    yield
                    k.act(sw, P(0, 0, W), AF.Sigmoid, bias=SPc("w0", (l * 2 + d) * 4 + hp))
                    k.act(icl, P(1, 0, W), AF.Sigmoid, bias=SPc("a0", (l * 2 + d) * 4 + hp))
                    k.ts("dve", kk, kF, SPc("kk", l * 4 + hp), None, ALU.mult)
                    k.act(t1b, kk, AF.Square)
                    k.mm(P(0, 0, W), cst["bo"], t1b)
                    yield
                    k.scan(Lr, cst["rmask"][:, :W], sw, 0.0, ALU.mult, ALU.add)
                    totb = v3(Lr)[:, :, 127:128].bc([128, nch, 128])
                    if d == 0:
                        k.cp("dve", Lin, Lr)
                    else:
                        k.tt("dve", t2b, sw, Lr, ALU.subtract)
                        k.tt("dve", v3(Lin), v3(t2b), totb, ALU.add)
                    k.tt("dve", Lex, Lin, sw, ALU.subtract)
                    k.act(gC, v3(Lr)[:, :, 127], AF.Exp, scale=-CDEC)
                    k.ts("dve", t2b, P(0, 0, W), 1e-24, None, ALU.max)
                    k.act(t2b, t2b, AF.Sqrt)
                    yield
                    k.recip(t2b, t2b)
                    k.tt("dve", kk, kk, t2b, ALU.mult)
                    k.act(Eb, Lex, AF.Exp, scale=-CDEC)
                    yield
                    k.stt(ARb[:, :, 0, :], v3(kk), -1.0, v3(Eb), ALU.mult, ALU.mult)
                    k.act(Eb, Lin, AF.Exp, scale=-CDEC)
                    yield
                    k.tt("dve", ARb[:, :, 1, :], v3(rF), v3(Eb), ALU.mult)
                    ktil, bp = t1b, t2b
                    k.ts("dve", ktil, icl, SPc("ka", l * 4 + hp), omka[:, l * 4 + hp:l * 4 + hp + 1], ALU.mult, ALU.add)
                    k.tt("dve", ktil, ktil, kF, ALU.mult)
                    k.tt("pool", bp, kk, icl, ALU.mult)
                    BH, KH, bck, kck, vbf = hb[0], hb[1], hb[2], hb[3], hb[4]
                    k.act(Eb, Lin, AF.Exp, scale=CDEC)
                    yield
                    k.tt("dve", BH, bp, Eb, ALU.mult)
                    k.tt("pool", KH, ktil, Eb, ALU.mult)
                    k.tt("dve", v3(Lex), v3(Lin), totb, ALU.subtract)
                    k.act(Eb, Lex, AF.Exp, scale=CDEC)
                    yield
                    k.tt("dve", bck, bp, Eb, ALU.mult)
                    k.tt("pool", kck, ktil, Eb, ALU.mult)
                    k.cp("act", vbf, vF)
                    yield
                    transposes(bck, Btok, (4 + B.off) % 8)
                    yield
                    transposes(kck, Ktok, (5 + B.off) % 8)
                    yield
                    transposes(vbf, Vtok, (4 + B.off) % 8)
                    yield
                    corder = list(range(nch)) if d == 0 else list(range(nch - 1, -1, -1))
                    for c in corder:
                        cs = slice(c * 128, (c + 1) * 128)
                        for e in range(2):
                            Re = slice(64 * e, 64 * e + 64)
                            arv = ARb[Re, c, :, :].rr("p a t -> p (a t)")
                            k.mm(P(e, 0, 256), BH[Re, cs], arv)
                            k.mm(P(e, 256, 512), KH[Re, cs], arv)
                            k.mm(P(2 + e, 0, 128), ARb[Re, c, 0, :], BH[Re, cs])
                        yield
                        nm0 = NMb[0]
                        k.tt("dve", nm0[:, :, 0, :], P2(0, 0, 128), cst["mask2"][:, d, 0:128].rr("p (o t) -> p o t", o=1).bc([128, 2, 128]), ALU.mult)
                        k.tt("dve", nm0[:, :, 1, :], P2(2, 0, 128), cst["mmask"][:, d, :].rr("p (o t) -> p o t", o=1).bc([128, 2, 128]), ALU.mult)
                        k.tt("dve", Ppb[0], nm0[:, :, 0, :], cst["ident"].rr("p (o t) -> p o t", o=1).bc([128, 2, 128]), ALU.add)
                        for e in range(2):
                            k.tt("dve", Am[:, e, :, :], P(e).rr("p (a t) -> p a t", a=2),
                                 cst["mask2"][:, d, :].rr("p (o t) -> p o t", o=1).bc([128, 2, 256]), ALU.mult)
                        yield
                        cur = 0
                        pcur = 0
                        for itn in range(7):
                            nmc, nmn = NMb[cur], NMb[1 - cur]
                            pc, pn = Ppb[pcur], Ppb[1 - pcur]
                            def rc(x):
                                return x.bitcast(F32R) if DBL_R else x
                            for e in range(2):
                                if itn < 5:
                                    k.mm(P(4, e * 256, e * 256 + 128), rc(nmc[:, e, 1, :]), rc(nmc[:, e, 0, :]))
                                if itn < 6:
                                    k.mm(P(4, e * 256 + 128, e * 256 + 256), rc(nmc[:, e, 0, :]), rc(nmc[:, e, 1, :]))
                                if itn >= 1:
                                    k.mm(P(5, e * 128, e * 128 + 128), rc(nmc[:, e, 1, :]), rc(pc[:, e, :]))
                            yield
                            if itn < 5:
                                k.cp("act", nmn.rr("p e a t -> p (e a t)"), P(4))
                            elif itn == 5:
                                k.cp("act", nmn[:, :, 1, :], P(4).rr("p (e a t) -> p e a t", e=2, a=2)[:, :, 1, :])
                            if itn >= 1:
                                k.tt("dve", pn, P(5, 0, 256).rr("p (e t) -> p e t", e=2), pc, ALU.add)
                                pcur = 1 - pcur
                            yield
                            cur = 1 - cur
                        cur = pcur
                        TT = Ppb[cur]
                        for e in range(2):
                            Re = slice(64 * e, 64 * e + 64)
                            k.mm(P(2 + e, 128, 192), ARb[Re, c, 0, :], Sbf[Re, :], True, False)
                            k.mm(P(2 + e, 128, 192), Am[:, e, 1, 0:128], Vtok[:, c, Re], False, True)
                        yield
                        k.cp("act", Xsb, P2(2, 128, 192))
                        yield
                        for e in range(2):
                            k.mm(P(2 + e, 192, 256), TT[:, e, :], Xsb[:, e, :])
                        yield
                        k.cp("act", Usb, P2(2, 192, 256))
                        yield
                        for e in range(2):
                            Re = slice(64 * e, 64 * e + 64)
                            po = P(6 + e, 0, 128)[Re]
                            k.mm(po, Sbf[Re, :], ARb[Re, c, 1, :], True, False)
                            k.mm(po, Usb[:, e, :], Am[:, e, 0, 128:256], False, False)
                            k.mm(po, Vtok[:, c, Re], Am[:, e, 1, 128:256], False, True)
                        for e in range(2):
                            Ce = slice(64 * e, 64 * e + 64)
                            pd = P(5, 256, 320)[Ce]
                            k.mm(pd, Btok[:, c, Ce], Usb[:, e, :], True, False)
                            k.mm(pd, Ktok[:, c, Ce], Vtok[:, c, Ce], False, True)
                        yield
                        for e in range(2):
                            Re = slice(64 * e, 64 * e + 64)
                            po = P(6 + e, 0, 128)[Re]
                            yv = ybuf[Re, t0 + c * 128:t0 + (c + 1) * 128]
                            k.tt("dve", yv, yv, po, ALU.add)
                        k.stt(Sst, Sst, gC[:, c:c + 1], P(5, 256, 320), ALU.mult, ALU.add)
                        k.cp("act", Sbf, Sst)
                        yield

            def ret_pass(hp, B, d):
                fb, hb = B.fb, B.hb
                Ktok, Vtok, Sst, Sbf, scm, dTt, qdt, kdt = B.Ktok, B.Vtok, B.Sst, B.Sbf, B.scm, B.dTt, B.qdt, B.kdt

                order = list(range(9)) if d == 0 else [0] + list(range(8, 0, -1))
                k.dma("sp", dTt, CD["dT"][d * 4 + hp])
                k.dma("sp", qdt, CD["qd"][d * 4 + hp])
                k.dma("sp", kdt, CD["kd"][d * 4 + hp])
                k.memset("dve", Sst, 0.0)
                k.memset("dve", Sbf, 0.0)
                for ri in order:
                    t0, t1 = RNG[ri]
                    qF, kF, vF, gF, cosF, sinF = fb[0], fb[1], fb[2], fb[3], fb[4], fb[5]
                    k.dma("sp", qF, pbuf[15 + hp][:, t0:t1])
                    k.dma("sp", kF, pbuf[19 + hp][:, t0:t1])
                    k.dma("sp", vF, pbuf[23 + hp][:, t0:t1])
                    k.dma("sp", gF, pbuf[(27 if d == 0 else 31) + hp][:, t0:t1])
                    k.dma("sp", cosF, CD["cosT"][:, t0:t1])
                    k.dma("sp", sinF, CD["ssinT"][:, t0:t1])
                    qb, kb, qr, kr, qh, vbf = hb[0], hb[1], hb[2], hb[3], hb[4], hb[5]
                    t1b, t2b = fb[6], fb[7]
                    for (src, sb_, dstb, isq) in ((qF, qb, qr, True), (kF, kb, kr, False)):
                        k.cp("act", sb_, src)
                        yield
                        k.mm(PS(6, 256, 512), cst["pm"], sb_)
                        k.tt("dve", t1b, src, cosF, ALU.mult)
                        yield
                        k.tt("dve", t2b, PS(6, 256, 512), sinF, ALU.mult)
                        k.tt("dve", t1b, t1b, t2b, ALU.add)
                        if isq:
                            k.cp("act", dstb, t1b)
                            k.tt("dve", v3(qh), v3(t1b), qdt.rr("p (o t) -> p o t", o=1).bc([128, nch, 128]), ALU.mult)
                        else:
                            k.act(dstb, t1b, AF.Copy, scale=0.125)
                        yield
                    k.cp("act", vbf, vF)
                    yield
                    transposes(kr, Ktok, None, mul_tab=kdt, region=PS(7, 256, 384))
                    yield
                    transposes(vbf, Vtok, None, region=PS(7, 256, 384))
                    yield
                    orng = fb[8]
                    corder = list(range(nch)) if d == 0 else list(range(nch - 1, -1, -1))
                    for c in corder:
                        cs = slice(c * 128, (c + 1) * 128)
                        for e in range(2):
                            Re = slice(64 * e, 64 * e + 64)
                            k.mm(PS(2 + e, 256, 384), kr[Re, cs], qr[Re, cs])
                        yield
                        k.tt("dve", scm, PS2(2, 256, 384), dTt.rr("p (e t) -> p e t", e=2), ALU.mult)
                        yield
                        for e in range(2):
                            Re = slice(64 * e, 64 * e + 64)
                            po = PS(2 + e, 384, 512)[Re]
                            k.mm(po, Sbf[Re, :], qh[Re, cs], True, False)
                            k.mm(po, Vtok[:, c, Re], scm[:, e, :], False, True)
                        for e in range(2):
                            Ce = slice(64 * e, 64 * e + 64)
                            pd = PS(7, 384, 448)[Ce]
                            k.mm(pd, Ktok[:, c, Ce], Vtok[:, c, Ce], True, True)
                        yield
                        for e in range(2):
                            Re = slice(64 * e, 64 * e + 64)
                            k.cp("act", orng[Re, cs], PS(2 + e, 384, 512)[Re])
                        k.stt(Sst, Sst, cst["cd"][:, d * 4 + hp:d * 4 + hp + 1], PS(7, 384, 448), ALU.mult, ALU.add)
                        k.cp("act", Sbf, Sst)
                        yield
                    gnv = fb[9]
                    gn_block(orng, SPc("rgw", l * 4 + hp), SPc("rgb", l * 4 + hp), eps_t, gnv, None, None, (fb[10], fb[11], fb[6]), region=PS(6, 256, 512))
                    k.act(gF, gF, AF.Silu)
                    k.tt("dve", gnv, gnv, gF, ALU.mult)
                    k.tt("dve", ybuf2[:, t0:t1], ybuf2[:, t0:t1], gnv, ALU.add)
                    yield

            EREG = PS(6, 256, 512)

            def vres_gen(hp):
                hc = slice(hp * 128, (hp + 1) * 128)
                for (t0, t1) in RNG:
                    vF, vfF, sg = efb[0], efb[1], efb[2]
                    k.dma("sp", vF, pbuf[8 + hp][:, t0:t1])
                    k.dma("sp", vfF, vfd[hp][:, t0:t1])
                    k.dma("pool", G1s, pbuf[14][0:64, t0:t1])
                    k.mm(EREG, GVup[32:64, 1, hc], G1s[32:64, :])
                    yield
                    k.act(sg, EREG, AF.Sigmoid, bias=SPc("v0", hp))
                    k.tt("dve", vfF, vfF, vF, ALU.subtract)
                    yield
                    k.tt("dve", vfF, vfF, sg, ALU.mult)
                    k.tt("dve", vF, vF, vfF, ALU.add)
                    k.dma("sp", pbuf[8 + hp][:, t0:t1], vF)
                    yield

            def epi_gen(hp, yb):
                hc = slice(hp * 128, (hp + 1) * 128)
                for (t0, t1) in RNG:
                    rF, kF, vF, gnv = efb[0], efb[1], efb[2], efb[3]
                    k.dma("sp", rF, pbuf[0 + hp][:, t0:t1])
                    k.dma("sp", kF, pbuf[4 + hp][:, t0:t1])
                    k.dma("sp", vF, pbuf[8 + hp][:, t0:t1])
                    k.dma("pool", G0s, pbuf[13][:, t0:t1])
                    k.dma("pool", G1s, pbuf[14][0:64, t0:t1])
                    k.stt(rF, rF, SPc("rk", l * 4 + hp), kF, ALU.mult, ALU.mult)
                    k.mm(EREG, cst["bo"], rF)
                    yield
                    k.tt("dve", vF, vF, EREG, ALU.mult)
                    yield
                    gn_block(yb[:, t0:t1], SPc("lnw", l * 4 + hp), SPc("lnb", l * 4 + hp), eps_r, gnv, None, None,
                             (efb[4], efb[5], efb[6]), region=EREG)
                    k.tt("dve", gnv, gnv, vF, ALU.add)
                    yield
                    k.mm(EREG, G0up[:, l, hc], G0s, True, False)
                    k.mm(EREG, GVup[0:32, l, hc], G1s[0:32, :], False, True)
                    yield
                    k.tt("dve", oT[:, hp, t0:t1], gnv, EREG, ALU.mult)
                    yield

            def ret_both(hp_):
                yield from ret_pass(hp_, SBr[0], 0)
                yield from ret_pass(hp_, SBr[0], 1)

            def third(hp_):
                yield from ret_both(hp_)
                if hp_ > 0:
                    yield from epi_gen(hp_ - 1, ybs[(hp_ - 1) % 2])
                if l == 1 and hp_ < 3:
                    yield from vres_gen(hp_ + 1)

            if l == 1:
                run_streams([vres_gen(0)])
            for hp in range(4):
                yb = ybs[hp % 2]
                k.memset("dve", yb, 0.0)
                k.memset("dve", ybuf2, 0.0)
                run_streams([rwkv_pass(hp, SB[0], yb), rwkv_pass(hp, SB[1], yb), third(hp)])
                k.cp("act", oT[:, 4 + hp, :], ybuf2)
            run_streams([epi_gen(3, ybs[1])])
            ph.close()
            k.dma("sp", xT, xsp_d)
            if ("oT_%d_%d" % (b, l)) in dbg:
                o_ = V(nc.dram_tensor("dbg_oT_%d_%d" % (b, l), [128, 8, T], BF16, kind="ExternalOutput").ap(), Reg())
                dbg_out["oT_%d_%d" % (b, l)] = o_
                k.dma("sp", o_, oT)
            if "stop_p3" in dbg:
                break
            ph = Phase()
            wob = ph.sb([128, 8, 1024], BF16)
            k.dma("pool", wob, WD["w_out"][l].rr("(c p) n -> p c n", p=128))
            n = 0
            for m in range(8):
                for (t0, t1) in BLK:
                    W = t1 - t0
                    if last and t0 < NCTX:
                        continue
                    pb = n % 8
                    n += 1
                    for kc in range(8):
                        k.mm(PS(pb, 0, W), wob[:, kc, m * 128:(m + 1) * 128], oT[:, kc, t0:t1], start=(kc == 0), stop=(kc == 7))
                    wi = who_idx(t0 < NCTX, b)
                    k.stt(xT[:, m, t0:t1], PS(pb, 0, W), ada[l][:, 16 + m, wi:wi + 1], xT[:, m, t0:t1], ALU.mult, ALU.add)
            ph.close()
            ph = Phase()
            norm(ph, lambda c, ic: A2[l][:, c, who_idx(ic, b):who_idx(ic, b) + 1],
                 lambda c, ic: ada[l][:, 24 + c, who_idx(ic, b):who_idx(ic, b) + 1],
                 lambda c, t0, t1: hT[:, c, t0:t1], BLK[1:] if last else BLK)
            ph.close()
            ph = Phase()
            wgf = [ph.sb([128, 8, 512], BF16) for i in range(2)]
            actb = ph.sb([128, NJ, 512], BF16)
            wfo = [ph.sb([128, NJ, 128], BF16) for i in range(2)]
            sgb = [ph.sb([128, 512]) for i in range(2)]
            wfi_v = WD["w_ffn_in"][l].rr("(c p) n -> p c n", p=128)
            wfo_v = WD["w_ffn_out"][l].rr("(j p) n -> p j n", p=128)
            wi_ = 0
            for (t0, t1) in BLK:
                W = t1 - t0
                if (last and t0 < NCTX) or "skip_ffn" in dbg:
                    continue
                for jg in range(6):
                    nj = 4 if jg < 5 else 2
                    wgate = wgf[0]
                    wup = wgf[1]
                    k.dma("pool", wgate[:, :, 0:nj * 128], wfi_v[:, :, jg * 512:jg * 512 + nj * 128])
                    k.dma("pool", wup[:, :, 0:nj * 128], wfi_v[:, :, HID + jg * 512:HID + jg * 512 + nj * 128])
                    for jj in range(nj):
                        j = jg * 4 + jj
                        pg = (2 * j) % 8
                        pu = (2 * j + 1) % 8
                        for c in range(8):
                            k.mm(PS(pg, 0, W), wgate[:, c, jj * 128:(jj + 1) * 128], hT[:, c, t0:t1], start=(c == 0), stop=(c == 7))
                        for c in range(8):
                            k.mm(PS(pu, 0, W), wup[:, c, jj * 128:(jj + 1) * 128], hT[:, c, t0:t1], start=(c == 0), stop=(c == 7))
                        sgt = sgb[j % 2]
                        k.act(sgt[:, :W], PS(pg, 0, W), AF.Silu)
                        k.tt("dve", actb[:, j, :W], sgt[:, :W], PS(pu, 0, W), ALU.mult)
                for m in range(8):
                    wf = wfo[wi_ % 2]
                    wi_ += 1
                    k.dma("pool", wf, wfo_v[:, :, m * 128:(m + 1) * 128])
                    pb = m % 8
                    for j in range(NJ):
                        k.mm(PS(pb, 0, W), wf[:, j, :], actb[:, j, :W], start=(j == 0), stop=(j == NJ - 1))
                    wi = who_idx(t0 < NCTX, b)
                    k.stt(xT[:, m, t0:t1], PS(pb, 0, W), ada[l][:, 40 + m, wi:wi + 1], xT[:, m, t0:t1], ALU.mult, ALU.add)
            ph.close()
            if ("xT_%d_%d" % (b, l)) in dbg:
                o_ = V(nc.dram_tensor("dbg_xT_%d_%d" % (b, l), [128, 8, T], F32, kind="ExternalOutput").ap(), Reg())
                dbg_out["xT_%d_%d" % (b, l)] = o_
                k.dma("sp", o_, xT)
        if "stop_p2" in dbg or "stop_p3" in dbg:
            continue
        ph = Phase()
        obuf = [ph.sb([128, T]) for i in range(2)]
        rst_all = ph.sb([128, T])
        sq = [ph.sb([128, 512]) for i in range(2)]
        n = 0
        for (t0, t1) in BLK[1:]:
            W = t1 - t0
            for c in range(8):
                s = sq[n % 2]
                n += 1
                k.act(s[:, :W], xT[:, c, t0:t1], AF.Square)
                k.mm(PS(0, 0, W), cst["onesr"], s[:, :W], start=(c == 0), stop=(c == 7))
            k.act(rst_all[:, t0:t1], PS(0, 0, W), AF.Sqrt, bias=eps_n)
            k.recip(rst_all[:, t0:t1], rst_all[:, t0:t1])
        for c in range(8):
            ob = obuf[c % 2]
            k.tt("dve", ob[:, NCTX:T], xT[:, c, NCTX:T], rst_all[:, NCTX:T], ALU.mult)
            k.act(ob[:, NCTX:T], ob[:, NCTX:T], AF.Copy, scale=SPc("normf", c))
            k.dma("sp", outT_d[b, c * 128:(c + 1) * 128, :], ob[:, NCTX:T])
        ph.close()
    k.wait_all("sp", [outT_d.g[0]] + [v.g[0] for v in dbg_out.values()])
    es.close()
    return nc, k, dbg_out


_CACHE = {}


def kernel(**inp):
    inp = {k_: np.asarray(v) for k_, v in inp.items()}
    ncores = 8
    B = inp["x"].shape[0]
    nb = B // ncores
    if "nc" not in _CACHE:
        _CACHE["nc"] = build(nb)[0]
    nc = _CACHE["nc"]
    xT = np.ascontiguousarray(np.transpose(inp["x"], (0, 2, 1)))
    ctxT = np.ascontiguousarray(np.transpose(inp["ctx"], (0, 2, 1)))
    sp = build_sp(inp)
    consts = build_consts()
    in_maps = []
    for i in range(ncores):
        m = {"xT": xT[i * nb:(i + 1) * nb], "ctxT": ctxT[i * nb:(i + 1) * nb]}
        cT = np.zeros((D, 5), np.float32)
        cT[:, :nb] = inp["c"][i * nb:(i + 1) * nb].T
        cT[:, 4] = inp["c_ctx"]
        m["cT"] = cT
        m["sp"] = sp
        for n_ in CONST_SHAPES:
            m["c_" + n_] = consts[n_]
        for n_ in WEIGHT_SHAPES:
            m[n_] = inp[n_]
        in_maps.append(m)
    res = run_bass_kernel_spmd(nc, in_maps, core_ids=list(range(ncores)))
    outT = np.concatenate([np.asarray(r["outT"]) for r in res.results], axis=0)
    return np.ascontiguousarray(np.transpose(outT, (0, 2, 1))).astype(np.float32)
```
